# Optimizing a Trainium2 kernel written in Bass

```python
import jax, jax.numpy as jnp
from jax import lax
import numpy as np

D_MODEL = 1024
BATCH = 8
SEQ = 2048
DEPTH = 4
DEC_BATCH = 8
DEC_SEQ = 16
PAST_LEN = 4096

CHUNK = 64
GMLP_WIDTH = 1024
GMLP_HEADS = 8
GMLP_HEAD_DIM = GMLP_WIDTH // GMLP_HEADS
GMLP_CHUNK = 128
SSD_WIDTH = 1024
SSD_HEAD_DIM = 64
SSD_HEADS = SSD_WIDTH // SSD_HEAD_DIM
SSD_GROUPS = 2
SSD_HEADS_PER_GROUP = SSD_HEADS // SSD_GROUPS
SSD_STATE = 128
SSD_CONV = 4
SSD_CHUNK = 128
SSD_CONV_DIM = SSD_WIDTH + 2 * SSD_GROUPS * SSD_STATE
MIX_WIDTH = GMLP_WIDTH + SSD_WIDTH
D_IN_PROJ = 2 * GMLP_WIDTH + SSD_WIDTH + SSD_CONV_DIM + SSD_HEADS
D_FF = 4 * D_MODEL
ALPHA = (2 * DEPTH) ** 0.25
BETA = (8 * DEPTH) ** -0.25
LN_EPS = 1e-5
RMS_EPS = 1e-5

kernel_name = "hymba_gmlp_ssd_deepnorm_stream"


def layer_norm(x, g, b):
    xf = x.astype(jnp.float32)
    mu = jnp.mean(xf, axis=-1, keepdims=True)
    var = jnp.mean(jnp.square(xf - mu), axis=-1, keepdims=True)
    y = (xf - mu) * lax.rsqrt(var + LN_EPS) * g.astype(jnp.float32) + b.astype(jnp.float32)
    return y.astype(x.dtype)


def gated_rms_norm(y, z, g):
    bsz, L, _ = y.shape
    h = y.astype(jnp.float32) * jax.nn.silu(z.astype(jnp.float32))
    h = h.reshape(bsz, L, SSD_GROUPS, SSD_WIDTH // SSD_GROUPS)
    h = h * lax.rsqrt(jnp.mean(jnp.square(h), axis=-1, keepdims=True) + RMS_EPS)
    return (h.reshape(bsz, L, SSD_WIDTH) * g.astype(jnp.float32)).astype(z.dtype)


def gmlp_mix(u, v, ln_g, ln_b, w_s, b_s, offset):
    bsz, L, _ = v.shape
    u = jax.nn.gelu(u)
    v = layer_norm(jax.nn.gelu(v), ln_g, ln_b)
    q = min(L, GMLP_CHUNK)
    causal = jnp.tril(jnp.ones((GMLP_CHUNK, GMLP_CHUNK), dtype=bool))
    w = jnp.where(causal, w_s, jnp.zeros_like(w_s))[:, offset:offset + q, offset:offset + q]
    bq = b_s[:, offset:offset + q]
    vc = v.reshape(bsz, L // q, q, GMLP_HEADS, GMLP_HEAD_DIM)
    s = jnp.einsum('hts,bcshd->bcthd', w, vc) + bq.T[None, None, :, :, None]
    return u * s.reshape(bsz, L, GMLP_WIDTH), v


def causal_conv(xbc, conv_state, w, b):
    L = xbc.shape[1]
    xp = jnp.concatenate([conv_state.astype(xbc.dtype), xbc], axis=1)
    out = b + w[0] * xp[:, 0:L]
    for k in range(1, SSD_CONV):
        out = out + w[k] * xp[:, k:k + L]
    return jax.nn.silu(out), xp[:, -(SSD_CONV - 1):]


def ssd_scan(x, dt, a, bm, cm, h0):
    f32 = jnp.float32
    bsz, L = x.shape[:2]
    q = min(L, SSD_CHUNK)
    nc = L // q
    G, R, P, N = SSD_GROUPS, SSD_HEADS_PER_GROUP, SSD_HEAD_DIM, SSD_STATE
    xf = x.astype(f32).reshape(bsz, nc, q, G, R, P)
    dtc = dt.reshape(bsz, nc, q, G, R)
    bc = bm.astype(f32).reshape(bsz, nc, q, G, N)
    cc = cm.astype(f32).reshape(bsz, nc, q, G, N)
    acum = jnp.cumsum(dtc * a.reshape(G, R), axis=2)
    seg = acum[:, :, :, None] - acum[:, :, None, :]
    causal = jnp.tril(jnp.ones((q, q), dtype=bool))[:, :, None, None]
    decay = jnp.exp(jnp.where(causal, seg, -jnp.inf))
    cb = jnp.einsum('bcign,bcjgn->bcijg', cc, bc)
    y_diag = jnp.einsum('bcijg,bcijgr,bcjgr,bcjgrp->bcigrp', cb, decay, dtc, xf)
    dec_end = jnp.exp(acum[:, :, -1:] - acum)
    st = jnp.einsum('bcjgn,bcjgr,bcjgrp->bcgrpn', bc, dec_end * dtc, xf)
    block_decay = jnp.exp(acum[:, :, -1])

    def step(h, inp):
        s_c, d_c = inp
        return d_c[..., None, None] * h + s_c, h

    h_init = h0.astype(f32).reshape(bsz, G, R, P, N)
    h_final, h_starts = lax.scan(step, h_init, (jnp.moveaxis(st, 1, 0), jnp.moveaxis(block_decay, 1, 0)))
    h_starts = jnp.moveaxis(h_starts, 0, 1)
    y_off = jnp.einsum('bcign,bcgrpn,bcigr->bcigrp', cc, h_starts, jnp.exp(acum))
    y = (y_diag + y_off).reshape(bsz, L, SSD_HEADS, P)
    return y, h_final.reshape(bsz, SSD_HEADS, P, N)


def mixer(x, ssm_state, conv_state, offset, w_in, gmlp_ln_g, gmlp_ln_b, gmlp_ws, gmlp_bs,
          conv_w, conv_b, dt_bias, a_log, d_skip, ssd_norm_g, w_out):
    bsz, L, _ = x.shape
    proj = jnp.einsum('bld,de->ble', x, w_in)
    s1 = GMLP_WIDTH
    s2 = 2 * GMLP_WIDTH
    s3 = s2 + SSD_WIDTH
    s4 = s3 + SSD_CONV_DIM
    u, v, z, xbc, dt = proj[..., :s1], proj[..., s1:s2], proj[..., s2:s3], proj[..., s3:s4], proj[..., s4:]
    y_g, v_rows = gmlp_mix(u, v, gmlp_ln_g, gmlp_ln_b, gmlp_ws, gmlp_bs, offset)
    xbc, new_conv = causal_conv(xbc, conv_state, conv_w, conv_b)
    xs = xbc[..., :SSD_WIDTH].reshape(bsz, L, SSD_HEADS, SSD_HEAD_DIM)
    bm = xbc[..., SSD_WIDTH:SSD_WIDTH + SSD_GROUPS * SSD_STATE].reshape(bsz, L, SSD_GROUPS, SSD_STATE)
    cm = xbc[..., SSD_WIDTH + SSD_GROUPS * SSD_STATE:].reshape(bsz, L, SSD_GROUPS, SSD_STATE)
    dtp = jax.nn.softplus(dt.astype(jnp.float32) + dt_bias.astype(jnp.float32))
    a = -jnp.exp(a_log.astype(jnp.float32))
    y, new_ssm = ssd_scan(xs, dtp, a, bm, cm, ssm_state)
    y = y + d_skip.astype(jnp.float32)[:, None] * xs.astype(jnp.float32)
    y_s = gated_rms_norm(y.reshape(bsz, L, SSD_WIDTH), z, ssd_norm_g)
    out = jnp.einsum('ble,ed->bld', jnp.concatenate([y_g, y_s], axis=-1), w_out)
    return out, new_ssm.astype(ssm_state.dtype), new_conv, v_rows


def squared_relu_ffn(x, w1, w2):
    h = jnp.square(jax.nn.relu(jnp.einsum('bld,df->blf', x, w1)))
    return jnp.einsum('blf,fd->bld', h, w2)


def setup_inputs(seed: int = 0) -> dict:
    key = jax.random.key(seed)
    ks = jax.random.split(key, 24)
    f32 = jnp.float32
    nrm = lambda k, shp: jax.random.normal(k, shp, dtype=f32)
    dt0 = jnp.exp(jax.random.uniform(ks[9], (DEPTH, SSD_HEADS), minval=np.log(1e-3), maxval=np.log(1e-1)))
    return {
        'x_prompt': nrm(ks[0], (BATCH, SEQ, D_MODEL)),
        'x_sample': nrm(ks[1], (DEC_BATCH, DEC_SEQ, D_MODEL)),
        'state_ssm': 0.3 * nrm(ks[2], (DEPTH, DEC_BATCH, SSD_HEADS, SSD_HEAD_DIM, SSD_STATE)),
        'state_conv': nrm(ks[3], (DEPTH, DEC_BATCH, SSD_CONV - 1, SSD_CONV_DIM)),
        'w_in': nrm(ks[4], (DEPTH, D_MODEL, D_IN_PROJ)) * D_MODEL ** -0.5,
        'gmlp_ln_g': 1.0 + 0.02 * nrm(ks[5], (DEPTH, GMLP_WIDTH)),
        'gmlp_ln_b': 0.02 * nrm(ks[6], (DEPTH, GMLP_WIDTH)),
        'gmlp_ws': nrm(ks[7], (DEPTH, GMLP_HEADS, GMLP_CHUNK, GMLP_CHUNK)) * GMLP_CHUNK ** -0.5,
        'gmlp_bs': 1.0 + 0.02 * nrm(ks[8], (DEPTH, GMLP_HEADS, GMLP_CHUNK)),
        'conv_w': nrm(ks[10], (DEPTH, SSD_CONV, SSD_CONV_DIM)) * SSD_CONV ** -0.5,
        'conv_b': 0.02 * nrm(ks[11], (DEPTH, SSD_CONV_DIM)),
        'dt_bias': dt0 + jnp.log(-jnp.expm1(-dt0)),
        'a_log': jnp.log(jax.random.uniform(ks[12], (DEPTH, SSD_HEADS), minval=1.0, maxval=16.0)),
        'd_skip': 1.0 + 0.02 * nrm(ks[13], (DEPTH, SSD_HEADS)),
        'ssd_norm_g': 1.0 + 0.02 * nrm(ks[14], (DEPTH, SSD_WIDTH)),
        'w_out': nrm(ks[15], (DEPTH, MIX_WIDTH, D_MODEL)) * (MIX_WIDTH ** -0.5 * BETA),
        'ln1_g': 1.0 + 0.02 * nrm(ks[16], (DEPTH, D_MODEL)),
        'ln1_b': 0.02 * nrm(ks[17], (DEPTH, D_MODEL)),
        'w_ff1': nrm(ks[18], (DEPTH, D_MODEL, D_FF)) * (D_MODEL ** -0.5 * BETA),
        'w_ff2': nrm(ks[19], (DEPTH, D_FF, D_MODEL)) * (D_FF ** -0.5 * BETA),
        'ln2_g': 1.0 + 0.02 * nrm(ks[20], (DEPTH, D_MODEL)),
        'ln2_b': 0.02 * nrm(ks[21], (DEPTH, D_MODEL)),
    }


def reference(x_prompt, x_sample, state_ssm, state_conv, w_in, gmlp_ln_g, gmlp_ln_b, gmlp_ws, gmlp_bs,
              conv_w, conv_b, dt_bias, a_log, d_skip, ssd_norm_g, w_out, ln1_g, ln1_b, w_ff1, w_ff2,
              ln2_g, ln2_b):
    bp = x_prompt.shape[0]
    zero_ssm = jnp.zeros((bp, SSD_HEADS, SSD_HEAD_DIM, SSD_STATE), x_prompt.dtype)
    zero_conv = jnp.zeros((bp, SSD_CONV - 1, SSD_CONV_DIM), x_prompt.dtype)
    sample_offset = PAST_LEN % GMLP_CHUNK
    xp, xs = x_prompt, x_sample
    ssm_p, conv_p, ssm_s, conv_s, v_s = [], [], [], [], []
    for l in range(DEPTH):
        lw = (w_in[l], gmlp_ln_g[l], gmlp_ln_b[l], gmlp_ws[l], gmlp_bs[l], conv_w[l], conv_b[l],
              dt_bias[l], a_log[l], d_skip[l], ssd_norm_g[l], w_out[l])
        hp, sp, cp, _ = mixer(xp, zero_ssm, zero_conv, 0, *lw)
        hs, ss, cs, vs = mixer(xs, state_ssm[l], state_conv[l], sample_offset, *lw)
        xp = layer_norm(ALPHA * xp + hp, ln1_g[l], ln1_b[l])
        xs = layer_norm(ALPHA * xs + hs, ln1_g[l], ln1_b[l])
        xp = layer_norm(ALPHA * xp + squared_relu_ffn(xp, w_ff1[l], w_ff2[l]), ln2_g[l], ln2_b[l])
        xs = layer_norm(ALPHA * xs + squared_relu_ffn(xs, w_ff1[l], w_ff2[l]), ln2_g[l], ln2_b[l])
        ssm_p.append(sp)
        conv_p.append(cp)
        ssm_s.append(ss)
        conv_s.append(cs)
        v_s.append(vs)
    return (xp, xs, jnp.stack(ssm_p), jnp.stack(conv_p), jnp.stack(ssm_s), jnp.stack(conv_s), jnp.stack(v_s))
```

```python
import contextlib
import numpy as np
import concourse.bass as bass
import concourse.mybir as mybir
from concourse.bass_utils import run_bass_kernel_spmd

F32 = mybir.dt.float32
BF16 = mybir.dt.bfloat16
AF = mybir.ActivationFunctionType
ALU = mybir.AluOpType

DEPTH = 4
NTILES = 4
ALPHA = (2 * DEPTH) ** 0.25
NS = 7


class Eng:
    def __init__(self, name, h, sem, step=1, is_pe=False):
        self.name = name; self.h = h; self.sem = sem; self.step = step
        self.count = 0; self.is_pe = is_pe; self.waited = {}


class Buf:
    def __init__(self, t=None, name=""):
        self.t = t; self.name = name; self.w = None; self.r = {}

    def __getitem__(self, k):
        return self.t[k]


class AliasBuf(Buf):
    def __init__(self, base, view, name=""):
        self.base = base; self.view = view; self.name = name

    def __getitem__(self, k):
        return self.view[k]

    @property
    def w(self):
        return self.base.w

    @w.setter
    def w(self, v):
        self.base.w = v

    @property
    def r(self):
        return self.base.r

    @r.setter
    def r(self, v):
        self.base.r = v


class K:
    def __init__(self, nc, es):
        self.nc = nc; self.es = es
        self.pe = Eng("pe", nc.tensor, self.sem("s_pe"), is_pe=True)
        self.act = Eng("act", nc.scalar, self.sem("s_act"))
        self.dve = Eng("dve", nc.vector, self.sem("s_dve"))
        self.pool = Eng("pool", nc.gpsimd, self.sem("s_pool"))
        self.sp = Eng("sp", nc.sync, self.sem("s_sp"))
        self.dq = [Eng(f"dq{i}", None, self.sem(f"s_dq{i}"), step=16) for i in range(16)]
        self.dq_i = 0
        self.nbuf = 0

    def sem(self, n):
        return self.es.enter_context(self.nc.semaphore(n))

    def sb(self, shape, dt, name=None):
        self.nbuf += 1
        name = name or f"b{self.nbuf}"
        return Buf(self.es.enter_context(self.nc.sbuf_tensor(name, list(shape), dt)), name)

    def ps(self, shape, dt, name=None):
        self.nbuf += 1
        name = name or f"p{self.nbuf}"
        return Buf(self.es.enter_context(self.nc.psum_tensor(name, list(shape), dt)), name)

    def _wait(self, eng, e2, ts):
        if e2 is eng and eng.is_pe:
            return
        if eng.waited.get(e2.name, 0) >= ts:
            return
        eng.h.wait_ge(e2.sem, ts)
        eng.waited[e2.name] = ts

    def _deps(self, eng, reads, writes):
        deps = {}

        def add(e2, ts):
            if deps.get(e2.name, (None, 0))[1] < ts:
                deps[e2.name] = (e2, ts)
        for b in reads:
            if b.w: add(*b.w)
        for b in writes:
            if b.w: add(*b.w)
            for e2, ts in b.r.values(): add(e2, ts)
        for e2, ts in deps.values():
            self._wait(eng, e2, ts)

    def op(self, eng, fn, reads=(), writes=()):
        self._deps(eng, reads, writes)
        ins = fn(eng.h)
        eng.count += 1
        ins.then_inc(eng.sem, 1)
        ts = eng.count
        for b in reads: b.r[eng.name] = (eng, ts)
        for b in writes:
            b.w = (eng, ts); b.r = {}
        return ins

    def dma(self, issuer, out, in_, reads=(), writes=(), q=None, **kw):
        if q is None:
            q = self.dq[self.dq_i]; self.dq_i = (self.dq_i + 1) % len(self.dq)
        if q.count > 0:
            self._wait(issuer, q, q.count * 16)
        self._deps(issuer, reads, writes)
        ins = issuer.h.dma_start(out=out, in_=in_, **kw)
        q.count += 1
        ins.then_inc(q.sem, 16)
        ts = q.count * 16
        for b in reads: b.r[q.name] = (q, ts)
        for b in writes:
            b.w = (q, ts); b.r = {}

    def finish(self, extra=()):
        for q in list(self.dq) + list(extra):
            if q.count: self._wait(self.sp, q, q.count * 16)


class Chunk:
    def __init__(self, L, col, rrow, seq, first, last, tok0):
        self.L = L; self.col = col; self.rrow = rrow; self.seq = seq
        self.first = first; self.last = last; self.tok0 = tok0


class _Stop(Exception):
    pass


def build_program(ntiles=NTILES, depth=DEPTH, stop=None):
    nc = bass.Bass("TRN2", target_bir_lowering=False)

    def din(name, shape):
        return nc.dram_tensor(name, list(shape), F32, kind="ExternalInput").ap()

    def dout(name, shape):
        return nc.dram_tensor(name, list(shape), F32, kind="ExternalOutput").ap()

    xp = din("xp", [2048, 1024]); xs = din("xs", [16, 1024])
    sssm = din("sssm", [4, 1024, 128]); sconv = din("sconv", [4, 3, 1536])
    w_in = din("w_in", [4, 1024, 4624])
    gln_g = din("gmlp_ln_g", [4, 1024]); gln_b = din("gmlp_ln_b", [4, 1024])
    gws = din("gmlp_ws", [4, 8, 128, 128]); gbs = din("gmlp_bs", [4, 1024])
    conv_w = din("conv_w", [4, 4, 1536]); conv_b = din("conv_b", [4, 1536])
    dt_bias = din("dt_bias", [1, 64]); a_log = din("a_log", [1, 64]); d_skip = din("d_skip", [1, 64])
    ssd_g = din("ssd_norm_g", [4, 1024])
    w_out = din("w_out", [4, 2048, 1024])
    ln1_g = din("ln1_g", [4, 1024]); ln1_b = din("ln1_b", [4, 1024])
    w_ff1 = din("w_ff1", [4, 1024, 4096]); w_ff2 = din("w_ff2", [4, 4096, 1024])
    ln2_g = din("ln2_g", [4, 1024]); ln2_b = din("ln2_b", [4, 1024])

    yp = dout("yp", [2048, 1024]); ys = dout("ys", [16, 1024])
    ssm_p = dout("ssm_p", [4, 1024, 128]); conv_p = dout("conv_p", [4, 3, 1536])
    ssm_s = dout("ssm_s", [4, 1024, 128]); conv_s = dout("conv_s", [4, 3, 1536])
    v_s = dout("v_s", [4, 16, 1024])

    with contextlib.ExitStack() as es:
        k = K(nc, es)
        pe, act, dve, pool, sp = k.pe, k.act, k.dve, k.pool, k.sp
        wq = [Eng(f"wq{i}", None, k.sem(f"s_wq{i}"), step=16) for i in range(NS)]
        pdq = Eng("pdq", None, k.sem("s_pdq"), step=16)

        resid = [k.sb([128, 1024], F32, f"resid{i}") for i in range(5)]
        xT = k.sb([128, 8, 528], BF16, "xT")
        mixT = k.sb([128, 16, 528], BF16, "mixT")
        mixT_c = [Buf(mixT.t, f"mixT_c{i}") for i in range(5)]
        xT_c = [Buf(xT.t, f"xT_c{i}") for i in range(5)]
        hT = k.sb([128, 8, 528], BF16, "hT")
        slots = [k.sb([128, 4096], BF16, f"slot{i}") for i in range(NS)]
        stP = [k.sb([128, 1024], F32, f"stP{i}") for i in range(4)]
        stS = k.sb([128, 1024], F32, "stS")
        statebs = [k.sb([128, 1024], BF16, "stateb0"), k.sb([128, 1024], BF16, "stateb1")]
        tailsP = [k.sb([128, 12, 3], F32, f"tailsP{i}") for i in range(4)]
        tailsS = k.sb([128, 12, 3], F32, "tailsS")
        rowc = k.sb([128, 2, 1024], F32, "rowc")
        bsb = k.sb([128, 8, 128], F32, "bsb")
        WsT = k.sb([128, 8, 128], BF16, "WsT")
        identb = k.sb([128, 128], BF16, "identb"); identf = k.sb([128, 128], F32, "identf")
        Um = k.sb([128, 128], F32, "Um"); Ls = k.sb([128, 128], F32, "Ls"); ones = k.sb([128, 128], F32, "ones")
        convw = k.sb([128, 4, 12, 4], F32, "convw"); convb = k.sb([128, 4, 12], F32, "convb")
        gn = k.sb([128, 4, 8], F32, "gn")
        dtb = k.sb([128, 64], F32, "dtb"); aall = k.sb([128, 64], F32, "aall"); Dall = k.sb([128, 64], F32, "Dall")
        Wdt = k.sb([128, 4, 8, 16], BF16, "Wdt")
        F1s = [k.sb([128, 1024], F32, "F1a"), k.sb([128, 1024], F32, "F1b")]; F1 = F1s[0]
        F2 = k.sb([128, 1024], F32, "F2"); F3 = k.sb([128, 1024], F32, "F3")
        H1 = k.sb([128, 1024], BF16, "H1")
        st_t = k.sb([128, 12, 132], BF16, "st")
        diagw = k.sb([128, 48, 128], BF16, "diagw")
        xcTs = [k.sb([128, 12, 128], BF16, "xcT0"), k.sb([128, 12, 128], BF16, "xcT1")]
        xt = F2; wcol = k.sb([128, 16], F32, "wcol"); xdt = k.sb([128, 1024], BF16, "xdt"); xw = k.sb([128, 1024], BF16, "xw")
        Bt = k.sb([128, 256], BF16, "Bt")
        rhsU = [k.sb([128, 4, 128], F32, f"rhsU{i}") for i in range(2)]
        dec = k.sb([128, 16, 128], BF16, "dec")
        cbm = k.sb([128, 2, 128], F32, "cbm")
        tmpS = AliasBuf(F3, F3.t[:, 0:512].rearrange("p (a b) -> p a b", a=4), "tmpS")
        bst = k.sb([128, 2, 6], F32, "bst"); mv = k.sb([128, 2], F32, "mv"); rs = k.sb([128, 2], F32, "rs")
        dtvs = [k.sb([128, 16], F32, "dtv0"), k.sb([128, 16], F32, "dtv1")]; das = [k.sb([128, 16], F32, "da0"), k.sb([128, 16], F32, "da1")]
        expas = [k.sb([128, 16], F32, "expa0"), k.sb([128, 16], F32, "expa1")]; bdecs = [k.sb([128, 16], F32, "bdec0"), k.sb([128, 16], F32, "bdec1")]
        ssq = k.sb([128, 2], F32, "ssq")

        pf = [k.ps([128, 512], F32, f"pf{i}") for i in range(6)]
        pb = [k.ps([128, 1024], BF16, f"pb{i}") for i in range(2)]
        bank_i = [0]; pb_i = [0]
        relu_tmp = [tmpS, rhsU[0], rhsU[1]]; relu_i = [0]

        def bank():
            b = pf[bank_i[0]]; bank_i[0] = (bank_i[0] + 1) % len(pf); return b

        def pbank():
            b = pb[pb_i[0]]; pb_i[0] = (pb_i[0] + 1) % len(pb); return b

        def mm(out, lhsT, rhs, start, stop, reads, writes):
            k.op(pe, lambda e: e.matmul(out, lhsT=lhsT, rhs=rhs, start=start, stop=stop), reads=reads, writes=writes)

        def tr(out, in_, ident, reads, writes):
            k.op(pe, lambda e: e.transpose(out=out, in_=in_, identity=ident), reads=reads, writes=writes)

        def actf(out, in_, func, reads, writes, **kw):
            k.op(act, lambda e: e.activation(out=out, in_=in_, func=func, **kw), reads=reads, writes=writes)

        def tt(out, in0, in1, op, reads, writes, eng=None):
            k.op(eng or dve, lambda e: e.tensor_tensor(out=out, in0=in0, in1=in1, op=op), reads=reads, writes=writes)

        def v3(ap, a):
            return ap.rearrange("p (a b) -> p a b", a=a)

        k.op(dve, lambda e: e.memset(identf[:], 1.0), writes=[identf])
        k.op(dve, lambda e: e.memset(Um[:], 1.0), writes=[Um])
        k.op(dve, lambda e: e.memset(Ls[:], 1.0), writes=[Ls])
        k.op(dve, lambda e: e.memset(ones[:], 1.0), writes=[ones])
        k.op(pool, lambda e: e.affine_select(out=identf[:], in_=identf[:], pattern=[[-1, 128]], compare_op=ALU.is_equal,
                                             fill=0.0, base=0, channel_multiplier=1), reads=[identf], writes=[identf])
        k.op(pool, lambda e: e.affine_select(out=Um[:], in_=Um[:], pattern=[[1, 128]], compare_op=ALU.is_ge,
                                             fill=0.0, base=0, channel_multiplier=-1), reads=[Um], writes=[Um])
        k.op(pool, lambda e: e.affine_select(out=Ls[:], in_=Ls[:], pattern=[[-1, 128]], compare_op=ALU.is_ge,
                                             fill=0.0, base=-1, channel_multiplier=1), reads=[Ls], writes=[Ls])
        k.op(dve, lambda e: e.tensor_copy(out=identb[:], in_=identf[:]), reads=[identf], writes=[identb])
        for c in range(4):
            k.dma(sp, resid[c][0:128, :], xp[c * 128:(c + 1) * 128, :], writes=[resid[c]])
        k.dma(sp, resid[4][0:16, :], xs[:, :], writes=[resid[4]])
        for l in range(4):
            for kk in range(4):
                k.dma(sp, convw[:, l, :, kk], conv_w[l, kk].rearrange("(cb p) -> p cb", p=128), writes=[convw], allow_slow_non_contiguous=True)
            k.dma(sp, convb[:, l], conv_b[l].rearrange("(cb p) -> p cb", p=128), writes=[convb], allow_slow_non_contiguous=True)
            k.dma(sp, gn[:, l], ssd_g[l].rearrange("(eb p) -> p eb", p=128), writes=[gn], allow_slow_non_contiguous=True)
        k.dma(sp, dtb[:], dt_bias.partition_broadcast(128), writes=[dtb])
        k.dma(sp, aall[:], a_log.partition_broadcast(128), writes=[aall])
        k.dma(sp, Dall[:], d_skip.partition_broadcast(128), writes=[Dall])
        for l in range(4):
            k.op(dve, lambda e: e.memset(stP[l][:], 0.0), writes=[stP[l]])
            k.op(dve, lambda e: e.memset(tailsP[l][:], 0.0), writes=[tailsP[l]])

        pieces = []
        sub_of = {}
        sub_counter = [0]
        plan = []

        def add_piece(src, a):
            pieces.append((src, a)); return len(pieces) - 1

        def kp(ap):
            return ap.rearrange("(kk p) n -> p kk n", p=128)

        for t in range(ntiles):
            for l in range(depth):
                d = {}
                d["u"] = [add_piece(kp(w_in[l][:, i * 512:(i + 1) * 512]), 8) for i in range(2)]
                d["v"] = [add_piece(kp(w_in[l][:, 1024 + i * 512:1024 + (i + 1) * 512]), 8) for i in range(2)]
                d["z"] = [add_piece(kp(w_in[l][:, 2048 + i * 512:2048 + (i + 1) * 512]), 8) for i in range(2)]
                d["x"] = [add_piece(kp(w_in[l][:, 3072 + i * 512:3072 + (i + 1) * 512]), 8) for i in range(3)]
                d["o"] = [add_piece(kp(w_out[l][i * 512:(i + 1) * 512, :]), 4) for i in range(4)]
                for q in range(4):
                    d[f"f1_{q}"] = [add_piece(kp(w_ff1[l][:, q * 1024 + i * 512:q * 1024 + (i + 1) * 512]), 8) for i in range(2)]
                    d[f"f2_{q}"] = [add_piece(kp(w_ff2[l][q * 1024 + i * 512:q * 1024 + (i + 1) * 512, :]), 4) for i in range(2)]
                plan.append(d)
        issued = [0]
        done_upto = [-1]

        def pump():
            while issued[0] < len(pieces) and (issued[0] < NS or issued[0] - NS <= done_upto[0]):
                j = issued[0]
                src, a = pieces[j]
                s = slots[j % NS]
                k.dma(pool, s[:].rearrange("p (a b) -> p a b", a=a), src, writes=[s], q=wq[j % NS])
                issued[0] += 1

        def W(j, a):
            assert j < issued[0], "weight piece not issued before use"
            s = slots[j % NS]
            return s, s[:].rearrange("p (a b) -> p a b", a=a)

        def release(upto):
            done_upto[0] = max(done_upto[0], upto)
            pump()

        pump()
        for l in range(4):
            k.dma(pool, Wdt[:, l], w_in[l][:, 4608:4624].rearrange("(kk p) n -> p kk n", p=128), writes=[Wdt], q=pdq)

        def make_xT(ch):
            L, col = ch.L, ch.col
            r = resid[ch.rrow]
            actf(H1[0:L, :], r[0:L, :], AF.Copy, [r], [H1])
            p = pbank()
            pv = v3(p[:], 8)
            for kk in range(8):
                tr(pv[:, kk, 0:L], H1[0:L, kk * 128:(kk + 1) * 128], identb[0:L, 0:L], [H1, identb], [p])
            k.op(dve, lambda e: e.tensor_copy(out=xT[:, :, col:col + L], in_=pv[:, :, 0:L]), reads=[p], writes=[xT_c[ch.rrow]])

        def make_xT_b(ch):
            L, col = ch.L, ch.col
            p = pbank()
            pv = v3(p[:], 8)
            for kk in range(8):
                tr(pv[:, kk, 0:L], xw[0:L, kk * 128:(kk + 1) * 128], identb[0:L, 0:L], [xw, identb], [p])
            k.op(dve, lambda e: e.tensor_copy(out=xT[:, :, col:col + L], in_=pv[:, :, 0:L]), reads=[p], writes=[xT_c[ch.rrow]])

        def layer_norm(ch, grow, brow):
            L = ch.L
            r = resid[ch.rrow]
            for i in range(2):
                k.op(dve, lambda e: e.bn_stats(out=bst[0:L, i, :], in_=r[0:L, i * 512:(i + 1) * 512]), reads=[r], writes=[bst])
            k.op(dve, lambda e: e.bn_aggr(out=mv[0:L, :], in_=bst[0:L, :, :]), reads=[bst], writes=[mv])
            actf(rs[0:L, 0:1], mv[0:L, 1:2], AF.Ln, [mv], [rs], bias=1e-5)
            actf(rs[0:L, 0:1], rs[0:L, 0:1], AF.Exp, [rs], [rs], scale=-0.5)
            k.op(dve, lambda e: e.tensor_scalar(out=r[0:L, :], in0=r[0:L, :], scalar1=mv[0:L, 0:1], scalar2=rs[0:L, 0:1],
                                                op0=ALU.subtract, op1=ALU.mult), reads=[r, mv, rs], writes=[r])
            tt(r[0:L, :], r[0:L, :], grow[0:L, :], ALU.mult, [r, rowc], [r])
            tt(r[0:L, :], r[0:L, :], brow[0:L, :], ALU.add, [r, rowc], [r])

        def load_rows(g_ap, b_ap):
            k.dma(sp, rowc[:, 0, :], g_ap.partition_broadcast(128), writes=[rowc])
            k.dma(sp, rowc[:, 1, :], b_ap.partition_broadcast(128), writes=[rowc])

        def layer_consts_a(l):
            k.dma(sp, v3(F1[:], 8), gws[l].rearrange("h t s -> t h s"), writes=[F1])
            actf(H1[:, :], F1[:, :], AF.Copy, [F1], [H1])
            k.dma(sp, bsb[:].rearrange("p a b -> p (a b)"), gbs[l:l + 1, :].partition_broadcast(128), writes=[bsb])
            for cb in range(12):
                for kk in range(4):
                    k.op(dve, lambda e: e.tensor_scalar(out=diagw[:, cb * 4 + kk, :], in0=identb[:, :], scalar1=convw[:, l, cb, kk:kk + 1],
                                                         scalar2=None, op0=ALU.mult), reads=[identb, convw], writes=[diagw])

        def layer_consts_b(l):
            p = pbank(); pv = v3(p[:], 8)
            for h in range(8):
                tr(pv[:, h, :], H1[:, h * 128:(h + 1) * 128], identb[:, :], [H1, identb], [p])
            tt(WsT[:], pv, Um[:].unsqueeze(1).to_broadcast([128, 8, 128]), ALU.mult, [p, Um], [WsT])

        def run_pipeline(gens):
            pending = [tuple(g) + (False,) * (3 - len(g)) for g in gens]; active = []
            while pending or active:
                if pending and (not active or active[-1][1] >= active[-1][2]):
                    g, lag, of = pending.pop(0)
                    active.append([g, 0, lag, of])
                order = [a for a in reversed(active) if not a[3]] + [a for a in active if a[3]]
                for a in order:
                    try:
                        next(a[0]); a[1] += 1
                    except StopIteration:
                        active.remove(a)

        par_ctr = [0]

        def A1_group(t, l, d):
            load_rows(gln_g[l:l + 1, :], gln_b[l:l + 1, :])
            colr = [(0, 512, [0, 1, 2, 3]), (512, 16, [4])] if t == 0 else [(0, 384, [0, 1, 2]), (384, 128, [3])]
            for ub in range(8):
                s_, sv = W(d["u"][ub // 4], 8)
                for (c0, n, cl) in colr:
                    b = bank()
                    for kk in range(8):
                        mm(b[:, 0:n], sv[:, kk, (ub % 4) * 128:(ub % 4 + 1) * 128], xT[:, kk, c0:c0 + n], kk == 0, kk == 7,
                           [s_] + [xT_c[c] for c in cl], [b])
                    actf(hT[:, ub, c0:c0 + n], b[:, 0:n], AF.Gelu_apprx_tanh, [b], [hT])
            release(d["u"][1])

        def A1_gen(t, l, ch, d, is_last):
            L, col = ch.L, ch.col
            par = par_ctr[0] % 2; par_ctr[0] += 1
            F1 = F1s[par]
            mx = mixT_c[ch.rrow]
            for nb in range(2):
                s_, sv = W(d["v"][nb], 8)
                b = bank()
                for kk in range(8):
                    mm(b[0:L, :], xT[:, kk, col:col + L], sv[:, kk, :], kk == 0, kk == 7, [s_, xT_c[ch.rrow]], [b])
                actf(F1[0:L, nb * 512:(nb + 1) * 512], b[0:L, :], AF.Gelu_apprx_tanh, [b], [F1])
            if is_last:
                release(d["v"][1])
            yield
            for i in range(2):
                k.op(dve, lambda e: e.bn_stats(out=bst[0:L, i, :], in_=F1[0:L, i * 512:(i + 1) * 512]), reads=[F1], writes=[bst])
            k.op(dve, lambda e: e.bn_aggr(out=mv[0:L, :], in_=bst[0:L, :, :]), reads=[bst], writes=[mv])
            actf(rs[0:L, 0:1], mv[0:L, 1:2], AF.Ln, [mv], [rs], bias=1e-5)
            actf(rs[0:L, 0:1], rs[0:L, 0:1], AF.Exp, [rs], [rs], scale=-0.5)
            k.op(dve, lambda e: e.tensor_scalar(out=F1[0:L, :], in0=F1[0:L, :], scalar1=mv[0:L, 0:1], scalar2=rs[0:L, 0:1],
                                                op0=ALU.subtract, op1=ALU.mult), reads=[F1, mv, rs], writes=[F1])
            tt(F1[0:L, :], F1[0:L, :], rowc[0:L, 0, :], ALU.mult, [F1, rowc], [F1])
            if ch.seq == "s":
                tt(F1[0:L, :], F1[0:L, :], rowc[0:L, 1, :], ALU.add, [F1, rowc], [F1])
                k.dma(sp, v_s[l], F1[0:L, :], reads=[F1])
                k.op(dve, lambda e: e.tensor_copy(out=H1[0:L, :], in_=F1[0:L, :]), reads=[F1], writes=[H1])
            else:
                tt(H1[0:L, :], F1[0:L, :], rowc[0:L, 1, :], ALU.add, [F1, rowc], [H1])
            yield
            bs2 = [bank(), bank()]
            for h in range(8):
                b = bs2[h // 4]
                mm(v3(b[:], 4)[:, h % 4, 0:L], H1[0:L, h * 128:(h + 1) * 128], WsT[0:L, h, 0:L], True, True, [H1, WsT], [b])
            for hb in range(2):
                b = bs2[hb]
                tt(tmpS[:, :, 0:L], v3(b[:], 4)[:, :, 0:L], bsb[:, 4 * hb:4 * hb + 4, 0:L], ALU.add, [b, bsb], [tmpS])
                tt(mixT[:, 4 * hb:4 * hb + 4, col:col + L], tmpS[:, :, 0:L], hT[:, 4 * hb:4 * hb + 4, col:col + L], ALU.mult,
                   [tmpS, hT], [mx])
            yield

        def A2F_gen(t, l, ch, d, is_last, par):
            L, col = ch.L, ch.col
            F1 = F1s[par]; expa = expas[par]; xcT = xcTs[par]; dtv = dtvs[par]; da = das[par]; bdec = bdecs[par]
            mx = mixT_c[ch.rrow]
            samp = ch.seq == "s"
            state = stS if samp else stP[l]
            tails = tailsS if samp else tailsP[l]
            has_state = samp or not ch.first
            if samp:
                stf = F1[:]
                k.dma(sp, v3(stf[:, 0:1024], 8), sssm[l].rearrange("(blk p) n -> p blk n", p=128), writes=[F1])
                for half in range(2):
                    b = bank()
                    for j in range(4):
                        blk = half * 4 + j
                        tr(b[:, j * 128:(j + 1) * 128], stf[:, blk * 128:(blk + 1) * 128], identf[:, :], [F1, identf], [b])
                    actf(stS[:, half * 512:(half + 1) * 512], b[:, :], AF.Copy, [b], [stS])
                for kk in range(3):
                    k.dma(sp, tailsS[:, :, kk], sconv[l, kk].rearrange("(cb p) -> p cb", p=128), writes=[tailsS], allow_slow_non_contiguous=True)
            bd = bank()
            for kk in range(8):
                mm(bd[0:L, 0:16], xT[:, kk, col:col + L], Wdt[:, l, kk, :], kk == 0, kk == 7, [xT_c[ch.rrow], Wdt], [bd])
            tt(dtv[0:L, :], bd[0:L, 0:16], dtb[0:L, l * 16:(l + 1) * 16], ALU.add, [bd, dtb], [dtv])
            actf(dtv[0:L, :], dtv[0:L, :], AF.Exp, [dtv], [dtv])
            actf(dtv[0:L, :], dtv[0:L, :], AF.Ln, [dtv], [dtv], bias=1.0)
            tt(da[0:L, :], dtv[0:L, :], aall[0:L, l * 16:(l + 1) * 16], ALU.mult, [dtv, aall], [da])
            mm(bd[0:L, 32:48], Um[0:L, 0:L], da[0:L, :], True, True, [Um, da], [bd])
            mm(bd[:, 64:80], ones[0:L, :], da[0:L, :], True, True, [ones, da], [bd])
            actf(expa[0:L, :], bd[0:L, 32:48], AF.Exp, [bd], [expa])
            actf(bdec[:, :], bd[:, 64:80], AF.Exp, [bd], [bdec])
            yield
            for nb in range(2):
                s_, sv = W(d["z"][nb], 8)
                b = bank()
                for kk in range(8):
                    mm(b[0:L, :], xT[:, kk, col:col + L], sv[:, kk, :], kk == 0, kk == 7, [s_, xT_c[ch.rrow]], [b])
                actf(F1[0:L, nb * 512:(nb + 1) * 512], b[0:L, :], AF.Silu, [b], [F1])
            yield
            xb_banks = [bank(), bank(), bank()]
            for cb in range(12):
                s_, sv = W(d["x"][cb // 4], 8)
                b = xb_banks[cb // 4]
                for kk in range(8):
                    mm(v3(b[:], 4)[:, cb % 4, 0:L], sv[:, kk, (cb % 4) * 128:(cb % 4 + 1) * 128], xT[:, kk, col:col + L],
                       kk == 0, kk == 7, [s_, xT_c[ch.rrow]], [b])
            if is_last:
                release(d["x"][2])
            k.op(dve, lambda e: e.tensor_copy(out=st_t[:, :, 0:3], in_=tails[:, :, :]), reads=[tails], writes=[st_t])
            for j in range(3):
                b = xb_banks[j]
                k.op(dve, lambda e: e.tensor_copy(out=st_t[:, 4 * j:4 * j + 4, 3:3 + L], in_=v3(b[:], 4)[:, :, 0:L]), reads=[b], writes=[st_t])
                k.op(dve, lambda e: e.tensor_copy(out=tails[:, 4 * j:4 * j + 4, :], in_=v3(b[:], 4)[:, :, L - 3:L]), reads=[b, st_t], writes=[tails])
            yield
            cv_banks = [bank(), bank(), bank()]
            for cb in range(12):
                b = cv_banks[cb // 4]
                for kk in range(4):
                    mm(v3(b[:], 4)[:, cb % 4, 0:L], diagw[:, cb * 4 + kk, :], st_t[:, cb, kk:kk + L], kk == 0, kk == 3, [diagw, st_t], [b])
            for cb in range(12):
                b = cv_banks[cb // 4]
                actf(xcT[:, cb, 0:L], v3(b[:], 4)[:, cb % 4, 0:L], AF.Silu, [b, convb], [xcT], bias=convb[:, l, cb:cb + 1])
            if ch.last:
                for j in range(3):
                    b = bank()
                    for i in range(4):
                        tr(b[0:3, i * 128:(i + 1) * 128], tails[:, 4 * j + i, :], identf[:, :], [tails, identf], [b])
                    rt = relu_tmp[j]
                    rtv = rt[:].rearrange("p a b -> p (a b)")
                    actf(rtv[0:3, :], b[0:3, :], AF.Copy, [b], [rt])
                    k.dma(sp, (conv_s if samp else conv_p)[l][:, j * 512:(j + 1) * 512], rtv[0:3, :], reads=[rt])
            yield

        def A2B_gen(t, l, ch, d, is_last, par):
            L, col = ch.L, ch.col
            F1 = F1s[par]; expa = expas[par]; xcT = xcTs[par]; dtv = dtvs[par]; da = das[par]; bdec = bdecs[par]
            mx = mixT_c[ch.rrow]
            samp = ch.seq == "s"
            state = stS if samp else stP[l]
            tails = tailsS if samp else tailsP[l]
            has_state = samp or not ch.first
            stateb = statebs[par]
            if samp or (has_state and ch.rrow == 0):
                actf(stateb[:, :], state[:, :], AF.Copy, [state], [stateb])
            if has_state:
                tt(v3(state[:, :], 16), v3(state[:, :], 16), bdec[:, :].unsqueeze(2).to_broadcast([128, 16, 64]), ALU.mult,
                   [state, bdec], [state], eng=pool)
            pA = pbank(); pAv = v3(pA[:], 8)
            for hb in range(8):
                tr(pAv[0:L, hb, :], xcT[:, hb, 0:L], identb[:, :], [xcT, identb], [pA])
            actf(xt[0:L, :], pA[0:L, :], AF.Copy, [pA], [xt])
            tt(v3(xdt[0:L, :], 16), v3(xt[0:L, :], 16), dtv[0:L, :].unsqueeze(2).to_broadcast([L, 16, 64]), ALU.mult, [xt, dtv], [xdt])
            tt(v3(F3[0:L, :], 16), v3(xt[0:L, :], 16), Dall[0:L, l * 16:(l + 1) * 16].unsqueeze(2).to_broadcast([L, 16, 64]),
               ALU.mult, [xt, Dall], [F3], eng=pool)
            pB = pbank()
            for g in range(2):
                tr(pB[0:L, g * 128:(g + 1) * 128], xcT[:, 8 + g, 0:L], identb[:, :], [xcT, identb], [pB])
            actf(Bt[0:L, :], pB[0:L, 0:256], AF.Copy, [pB], [Bt])
            bc = bank()
            for g in range(2):
                mm(v3(bc[:], 4)[0:L, g, 0:L], xcT[:, 8 + g, 0:L], xcT[:, 10 + g, 0:L], True, True, [xcT], [bc])
            tt(cbm[0:L, :, 0:L], v3(bc[:], 4)[0:L, 0:2, 0:L], Um[0:L, 0:L].unsqueeze(1).to_broadcast([L, 2, L]), ALU.mult, [bc, Um], [cbm])
            yield
            for hq in range(4):
                ru = rhsU[hq % 2]
                tt(ru[0:L, :, 0:L], Um[0:L, 0:L].unsqueeze(1).to_broadcast([L, 4, L]),
                   da[0:L, hq * 4:hq * 4 + 4].unsqueeze(2).to_broadcast([L, 4, L]), ALU.mult, [Um, da], [ru])
                b = bank()
                if L == 128:
                    mm(v3(b[:], 4)[0:L, :, 0:L], Ls[0:L, 0:L], ru[0:L, :, 0:L], True, True, [Ls, ru], [b])
                else:
                    for hh in range(4):
                        mm(v3(b[:], 4)[0:L, hh, 0:L], Ls[0:L, 0:L], ru[0:L, hh, 0:L], True, True, [Ls, ru], [b])
                actf(dec[0:L, hq * 4:hq * 4 + 4, 0:L], v3(b[:], 4)[0:L, :, 0:L], AF.Exp, [b], [dec])
            tt(wcol[0:L, :].unsqueeze(2), dtv[0:L, :].unsqueeze(2), dec[0:L, :, L - 1:L], ALU.mult, [dtv, dec], [wcol])
            tt(v3(xw[0:L, :], 16), v3(xt[0:L, :], 16), wcol[0:L, :].unsqueeze(2).to_broadcast([L, 16, 64]), ALU.mult, [xt, wcol], [xw])
            yield
            for g in range(2):
                tt(dec[0:L, g * 8:(g + 1) * 8, 0:L], dec[0:L, g * 8:(g + 1) * 8, 0:L],
                   cbm[0:L, g:g + 1, 0:L].to_broadcast([L, 8, L]), ALU.mult, [dec, cbm], [dec])
            bS = [bank(), bank()]
            for g in range(2):
                mm(bS[g][:, :], Bt[0:L, g * 128:(g + 1) * 128], xw[0:L, g * 512:(g + 1) * 512], True, True, [Bt, xw], [bS[g]])
            if has_state:
                for g in range(2):
                    tt(state[:, g * 512:(g + 1) * 512], state[:, g * 512:(g + 1) * 512], bS[g][:, :], ALU.add, [state, bS[g]], [state])
            else:
                for g in range(2):
                    actf(state[:, g * 512:(g + 1) * 512], bS[g][:, :], AF.Copy, [bS[g]], [state])
            if not samp and ch.rrow < 3:
                actf(statebs[1 - par][:, :], state[:, :], AF.Copy, [state], [statebs[1 - par]])
            yield
            bY = [bank(), bank()]
            for h in range(16):
                b = bY[h // 8]
                mm(b[0:L, (h % 8) * 64:(h % 8 + 1) * 64], dec[0:L, h, 0:L], xdt[0:L, h * 64:(h + 1) * 64], True, True, [dec, xdt], [b])
            if has_state:
                bO = [bank(), bank()]
                for g in range(2):
                    mm(bO[g][0:L, :], xcT[:, 10 + g, 0:L], stateb[:, g * 512:(g + 1) * 512], True, True, [xcT, stateb], [bO[g]])
                for g in range(2):
                    tt(v3(F2[0:L, g * 512:(g + 1) * 512], 8), v3(bO[g][0:L, :], 8),
                       expa[0:L, g * 8:(g + 1) * 8].unsqueeze(2).to_broadcast([L, 8, 64]), ALU.mult, [bO[g], expa], [F2])
                    tt(F2[0:L, g * 512:(g + 1) * 512], F2[0:L, g * 512:(g + 1) * 512], bY[g][0:L, :], ALU.add, [F2, bY[g]], [F2])
            else:
                for g in range(2):
                    actf(F2[0:L, g * 512:(g + 1) * 512], bY[g][0:L, :], AF.Copy, [bY[g]], [F2])
            tt(F2[0:L, :], F2[0:L, :], F3[0:L, :], ALU.add, [F2, F3], [F2])
            tt(F2[0:L, :], F2[0:L, :], F1[0:L, :], ALU.mult, [F2, F1], [F2])
            yield
            k.op(dve, lambda e: e.memset(ssq[:], 0.0), writes=[ssq])
            actf(rs[0:1, 0:1], ones[0:1, 0:1], AF.Exp, [ones], [rs])
            for g in range(2):
                actf(H1[0:L, g * 512:(g + 1) * 512], F2[0:L, g * 512:(g + 1) * 512], AF.Square, [F2, ssq], [H1, ssq],
                     accum_out=ssq[0:L, g:g + 1])
            actf(rs[0:L, :], ssq[0:L, :], AF.Ln, [ssq], [rs], scale=1.0 / 512.0, bias=1e-5)
            actf(rs[0:L, :], rs[0:L, :], AF.Exp, [rs], [rs], scale=-0.5)
            for g in range(2):
                k.op(dve, lambda e: e.tensor_scalar(out=H1[0:L, g * 512:(g + 1) * 512], in0=F2[0:L, g * 512:(g + 1) * 512],
                                                    scalar1=rs[0:L, g:g + 1], scalar2=None, op0=ALU.mult), reads=[F2, rs], writes=[H1])
            pC = pbank(); pCv = v3(pC[:], 8)
            for eb in range(8):
                tr(pCv[:, eb, 0:L], H1[0:L, eb * 128:(eb + 1) * 128], identb[0:L, 0:L], [H1, identb], [pC])
            k.op(dve, lambda e: e.tensor_copy(out=mixT[:, 8:16, col:col + L], in_=pCv[:, :, 0:L]), reads=[pC], writes=[mx])
            if ch.last:
                for half in range(2):
                    b = bank()
                    for j in range(4):
                        blk = half * 4 + j
                        tr(b[:, j * 128:(j + 1) * 128], state[:, blk * 128:(blk + 1) * 128], identf[:, :], [state, identf], [b])
                    actf(F3[:, half * 512:(half + 1) * 512], b[:, :], AF.Copy, [b], [F3])
                k.dma(sp, (ssm_s if samp else ssm_p)[l].rearrange("(blk p) n -> p blk n", p=128), v3(F3[:], 8), reads=[F3])
            yield

        def A3_gen(t, l, ch, d, is_first, is_last):
            L, col = ch.L, ch.col
            r = resid[ch.rrow]
            mx = mixT_c[ch.rrow]
            if is_first:
                load_rows(ln1_g[l:l + 1, :], ln1_b[l:l + 1, :])
                for pi in (2, 3):
                    s_, sv = W(d["o"][pi], 4)
                    for j in range(4):
                        eb = (pi - 2) * 4 + j
                        k.op(dve, lambda e: e.tensor_scalar(out=sv[:, j, :], in0=sv[:, j, :], scalar1=gn[:, l, eb:eb + 1], scalar2=None,
                                                            op0=ALU.mult), reads=[s_, gn], writes=[s_])
            for nb in range(2):
                b = bank()
                for kk in range(16):
                    s_, sv = W(d["o"][kk // 4], 4)
                    mm(b[0:L, :], mixT[:, kk, col:col + L], sv[:, kk % 4, nb * 512:(nb + 1) * 512], kk == 0, kk == 15, [s_, mx], [b])
                k.op(dve, lambda e: e.scalar_tensor_tensor(out=r[0:L, nb * 512:(nb + 1) * 512], in0=r[0:L, nb * 512:(nb + 1) * 512],
                                                           scalar=float(ALPHA), in1=b[0:L, :], op0=ALU.mult, op1=ALU.add),
                     reads=[r, b], writes=[r])
            if is_last:
                release(d["o"][3])
            yield
            layer_norm(ch, rowc[:, 0, :], rowc[:, 1, :])
            actf(xw[0:L, :], r[0:L, :], AF.Copy, [r], [xw])
            yield
            make_xT_b(ch)
            yield

        def B_group(t, l, q, d):
            if q == 3:
                load_rows(ln2_g[l:l + 1, :], ln2_b[l:l + 1, :])
            if t == 0:
                colr = [(0, 512, [0, 1, 2, 3]), (512, 16, [4])]
            elif q == 0:
                colr = [(0, 384, [0, 1, 2]), (384, 128, [3])]
            else:
                colr = [(0, 512, [0, 1, 2, 3])]
            for fb in range(8):
                s_, sv = W(d[f"f1_{q}"][fb // 4], 8)
                for (c0, n, cl) in colr:
                    b = bank()
                    for kk in range(8):
                        mm(b[:, 0:n], sv[:, kk, (fb % 4) * 128:(fb % 4 + 1) * 128], xT[:, kk, c0:c0 + n], kk == 0, kk == 7,
                           [s_] + [xT_c[c] for c in cl], [b])
                    rt = relu_tmp[relu_i[0] % 3]; relu_i[0] += 1
                    rtv = rt[:].rearrange("p a b -> p (a b)")
                    actf(rtv[:, 0:n], b[:, 0:n], AF.Relu, [b], [rt])
                    tt(hT[:, fb, c0:c0 + n], rtv[:, 0:n], rtv[:, 0:n], ALU.mult, [rt], [hT])
            release(d[f"f1_{q}"][1])

        def B_gen(t, l, q, ch, d, is_last):
            L, col = ch.L, ch.col
            r = resid[ch.rrow]
            for nb in range(2):
                b = bank()
                for fc in range(8):
                    s_, sv = W(d[f"f2_{q}"][fc // 4], 4)
                    mm(b[0:L, :], hT[:, fc, col:col + L], sv[:, fc % 4, nb * 512:(nb + 1) * 512], fc == 0, fc == 7, [s_, hT], [b])
                rr = r[0:L, nb * 512:(nb + 1) * 512]
                if q == 0:
                    k.op(dve, lambda e: e.scalar_tensor_tensor(out=rr, in0=rr, scalar=float(ALPHA), in1=b[0:L, :],
                                                               op0=ALU.mult, op1=ALU.add), reads=[r, b], writes=[r])
                else:
                    tt(rr, rr, b[0:L, :], ALU.add, [r, b], [r])
            if is_last:
                release(d[f"f2_{q}"][1])
            yield
            if q == 3:
                layer_norm(ch, rowc[:, 0, :], rowc[:, 1, :])
                if l != depth - 1:
                    actf(xw[0:L, :], r[0:L, :], AF.Copy, [r], [xw])
                yield
                if l == depth - 1:
                    if ch.seq == "s":
                        k.dma(sp, ys[:, :], r[0:L, :], reads=[r])
                    else:
                        k.dma(sp, yp[ch.tok0:ch.tok0 + L, :], r[0:L, :], reads=[r])
                else:
                    make_xT_b(ch)
                yield

        def chk(tag):
            if stop == tag:
                raise _Stop()
        try:
          for t in range(ntiles):
              chunks = [Chunk(128, c * 128, c, "p", t == 0 and c == 0, t == ntiles - 1 and c == 3, t * 512 + c * 128) for c in range(4)]
              if t == 0:
                  chunks.append(Chunk(16, 512, 4, "s", True, True, 0))
              for ch in chunks:
                  if t > 0:
                      k.dma(sp, resid[ch.rrow][0:ch.L, :], xp[ch.tok0:ch.tok0 + ch.L, :], writes=[resid[ch.rrow]])
                  make_xT(ch)
              n = len(chunks)
              if t == 0:
                  actf(aall[:], aall[:], AF.Exp, [aall], [aall])
                  k.op(dve, lambda e: e.tensor_scalar(out=aall[:], in0=aall[:], scalar1=-1.0, scalar2=None, op0=ALU.mult),
                       reads=[aall], writes=[aall])
              for l in range(depth):
                  d = plan[t * depth + l]
                  if l == 0:
                      layer_consts_a(l)
                  A1_group(t, l, d)
                  layer_consts_b(l)
                  gens = []
                  gens += [(A1_gen(t, l, ch, d, i == n - 1), 1, True) for i, ch in enumerate(chunks)]
                  for i, ch in enumerate(chunks):
                      gens.append((A2F_gen(t, l, ch, d, i == n - 1, i % 2), 4))
                      gens.append((A2B_gen(t, l, ch, d, i == n - 1, i % 2), 0))
                  gens += [(A3_gen(t, l, ch, d, i == 0, i == n - 1), 1, True) for i, ch in enumerate(chunks)]
                  run_pipeline(gens)
                  chk("A3")
                  for q in range(4):
                      B_group(t, l, q, d)
                      if q == 2 and l + 1 < depth:
                          layer_consts_a(l + 1)
                      run_pipeline([(B_gen(t, l, q, ch, d, i == n - 1), 1, True) for i, ch in enumerate(chunks)])
        except _Stop:
            pass
        k.finish(extra=wq + [pdq])
    return nc


_NC_CACHE = {}


def kernel(x_prompt, x_sample, state_ssm, state_conv, w_in, gmlp_ln_g, gmlp_ln_b, gmlp_ws, gmlp_bs,
           conv_w, conv_b, dt_bias, a_log, d_skip, ssd_norm_g, w_out, ln1_g, ln1_b, w_ff1, w_ff2,
           ln2_g, ln2_b):
    f = lambda a: np.ascontiguousarray(np.asarray(a, dtype=np.float32))
    if "nc" not in _NC_CACHE:
        _NC_CACHE["nc"] = build_program()
    nc = _NC_CACHE["nc"]
    shared = {
        "w_in": f(w_in), "gmlp_ln_g": f(gmlp_ln_g), "gmlp_ln_b": f(gmlp_ln_b), "gmlp_ws": f(gmlp_ws),
        "gmlp_bs": f(gmlp_bs).reshape(4, 1024), "conv_w": f(conv_w), "conv_b": f(conv_b),
        "dt_bias": f(dt_bias).reshape(1, 64), "a_log": f(a_log).reshape(1, 64), "d_skip": f(d_skip).reshape(1, 64),
        "ssd_norm_g": f(ssd_norm_g), "w_out": f(w_out), "ln1_g": f(ln1_g), "ln1_b": f(ln1_b),
        "w_ff1": f(w_ff1), "w_ff2": f(w_ff2), "ln2_g": f(ln2_g), "ln2_b": f(ln2_b),
    }
    xp = f(x_prompt); xs = f(x_sample); ss = f(state_ssm); sc = f(state_conv)
    in_maps = []
    for b in range(8):
        m = dict(shared)
        m["xp"] = xp[b]; m["xs"] = xs[b]
        m["sssm"] = np.ascontiguousarray(ss[:, b].reshape(4, 1024, 128))
        m["sconv"] = np.ascontiguousarray(sc[:, b])
        in_maps.append(m)
    res = run_bass_kernel_spmd(nc, in_maps, core_ids=list(range(8)))
    R = res.results
    y_prompt = np.stack([R[b]["yp"] for b in range(8)], axis=0)
    y_sample = np.stack([R[b]["ys"] for b in range(8)], axis=0)
    ssm_p = np.stack([R[b]["ssm_p"].reshape(4, 16, 64, 128) for b in range(8)], axis=1)
    conv_p = np.stack([R[b]["conv_p"] for b in range(8)], axis=1)
    ssm_s = np.stack([R[b]["ssm_s"].reshape(4, 16, 64, 128) for b in range(8)], axis=1)
    conv_s = np.stack([R[b]["conv_s"] for b in range(8)], axis=1)
    v_s = np.stack([R[b]["v_s"] for b in range(8)], axis=1)
    return (y_prompt, y_sample, ssm_p, conv_p, ssm_s, conv_s, v_s)
```

```python
import contextlib
import numpy as np
import concourse.bass as bass
import concourse.mybir as mybir
from concourse.bass_utils import run_bass_kernel_spmd

F32 = mybir.dt.float32
BF16 = mybir.dt.bfloat16
AF = mybir.ActivationFunctionType
ALU = mybir.AluOpType

DEPTH = 4
NTILES = 4
ALPHA = (2 * DEPTH) ** 0.25
NS = 7


class Eng:
    def __init__(self, name, h, sem, step=1, is_pe=False):
        self.name = name; self.h = h; self.sem = sem; self.step = step
        self.count = 0; self.is_pe = is_pe; self.waited = {}


class Buf:
    def __init__(self, t=None, name=""):
        self.t = t; self.name = name; self.w = None; self.r = {}

    def __getitem__(self, k):
        return self.t[k]


class AliasBuf(Buf):
    def __init__(self, base, view, name=""):
        self.base = base; self.view = view; self.name = name

    def __getitem__(self, k):
        return self.view[k]

    @property
    def w(self):
        return self.base.w

    @w.setter
    def w(self, v):
        self.base.w = v

    @property
    def r(self):
        return self.base.r

    @r.setter
    def r(self, v):
        self.base.r = v


class K:
    def __init__(self, nc, es):
        self.nc = nc; self.es = es
        self.pe = Eng("pe", nc.tensor, self.sem("s_pe"), is_pe=True)
        self.act = Eng("act", nc.scalar, self.sem("s_act"))
        self.dve = Eng("dve", nc.vector, self.sem("s_dve"))
        self.pool = Eng("pool", nc.gpsimd, self.sem("s_pool"))
        self.sp = Eng("sp", nc.sync, self.sem("s_sp"))
        self.dq = [Eng(f"dq{i}", None, self.sem(f"s_dq{i}"), step=16) for i in range(16)]
        self.dq_i = 0
        self.nbuf = 0

    def sem(self, n):
        return self.es.enter_context(self.nc.semaphore(n))

    def sb(self, shape, dt, name=None):
        self.nbuf += 1
        name = name or f"b{self.nbuf}"
        return Buf(self.es.enter_context(self.nc.sbuf_tensor(name, list(shape), dt)), name)

    def ps(self, shape, dt, name=None):
        self.nbuf += 1
        name = name or f"p{self.nbuf}"
        return Buf(self.es.enter_context(self.nc.psum_tensor(name, list(shape), dt)), name)

    def _wait(self, eng, e2, ts):
        if e2 is eng and eng.is_pe:
            return
        if eng.waited.get(e2.name, 0) >= ts:
            return
        eng.h.wait_ge(e2.sem, ts)
        eng.waited[e2.name] = ts

    def _deps(self, eng, reads, writes):
        deps = {}

        def add(e2, ts):
            if deps.get(e2.name, (None, 0))[1] < ts:
                deps[e2.name] = (e2, ts)
        for b in reads:
            if b.w: add(*b.w)
        for b in writes:
            if b.w: add(*b.w)
            for e2, ts in b.r.values(): add(e2, ts)
        for e2, ts in deps.values():
            self._wait(eng, e2, ts)

    def op(self, eng, fn, reads=(), writes=()):
        self._deps(eng, reads, writes)
        ins = fn(eng.h)
        eng.count += 1
        ins.then_inc(eng.sem, 1)
        ts = eng.count
        for b in reads: b.r[eng.name] = (eng, ts)
        for b in writes:
            b.w = (eng, ts); b.r = {}
        return ins

    def dma(self, issuer, out, in_, reads=(), writes=(), q=None, **kw):
        if q is None:
            q = self.dq[self.dq_i]; self.dq_i = (self.dq_i + 1) % len(self.dq)
        if q.count > 0:
            self._wait(issuer, q, q.count * 16)
        self._deps(issuer, reads, writes)
        ins = issuer.h.dma_start(out=out, in_=in_, **kw)
        q.count += 1
        ins.then_inc(q.sem, 16)
        ts = q.count * 16
        for b in reads: b.r[q.name] = (q, ts)
        for b in writes:
            b.w = (q, ts); b.r = {}

    def finish(self, extra=()):
        for q in list(self.dq) + list(extra):
            if q.count: self._wait(self.sp, q, q.count * 16)


class Chunk:
    def __init__(self, L, col, rrow, seq, first, last, tok0):
        self.L = L; self.col = col; self.rrow = rrow; self.seq = seq
        self.first = first; self.last = last; self.tok0 = tok0


class _Stop(Exception):
    pass


def build_program(ntiles=NTILES, depth=DEPTH, stop=None):
    nc = bass.Bass("TRN2", target_bir_lowering=False)

    def din(name, shape):
        return nc.dram_tensor(name, list(shape), F32, kind="ExternalInput").ap()

    def dout(name, shape):
        return nc.dram_tensor(name, list(shape), F32, kind="ExternalOutput").ap()

    xp = din("xp", [2048, 1024]); xs = din("xs", [16, 1024])
    sssm = din("sssm", [4, 1024, 128]); sconv = din("sconv", [4, 3, 1536])
    w_in = din("w_in", [4, 1024, 4624])
    gln_g = din("gmlp_ln_g", [4, 1024]); gln_b = din("gmlp_ln_b", [4, 1024])
    gws = din("gmlp_ws", [4, 8, 128, 128]); gbs = din("gmlp_bs", [4, 1024])
    conv_w = din("conv_w", [4, 4, 1536]); conv_b = din("conv_b", [4, 1536])
    dt_bias = din("dt_bias", [1, 64]); a_log = din("a_log", [1, 64]); d_skip = din("d_skip", [1, 64])
    ssd_g = din("ssd_norm_g", [4, 1024])
    w_out = din("w_out", [4, 2048, 1024])
    ln1_g = din("ln1_g", [4, 1024]); ln1_b = din("ln1_b", [4, 1024])
    w_ff1 = din("w_ff1", [4, 1024, 4096]); w_ff2 = din("w_ff2", [4, 4096, 1024])
    ln2_g = din("ln2_g", [4, 1024]); ln2_b = din("ln2_b", [4, 1024])

    yp = dout("yp", [2048, 1024]); ys = dout("ys", [16, 1024])
    ssm_p = dout("ssm_p", [4, 1024, 128]); conv_p = dout("conv_p", [4, 3, 1536])
    ssm_s = dout("ssm_s", [4, 1024, 128]); conv_s = dout("conv_s", [4, 3, 1536])
    v_s = dout("v_s", [4, 16, 1024])

    with contextlib.ExitStack() as es:
        k = K(nc, es)
        pe, act, dve, pool, sp = k.pe, k.act, k.dve, k.pool, k.sp
        wq = [Eng(f"wq{i}", None, k.sem(f"s_wq{i}"), step=16) for i in range(NS)]
        pdq = Eng("pdq", None, k.sem("s_pdq"), step=16)

        resid = [k.sb([128, 1024], F32, f"resid{i}") for i in range(5)]
        xT = k.sb([128, 8, 528], BF16, "xT")
        mixT = k.sb([128, 16, 528], BF16, "mixT")
        mixT_c = [Buf(mixT.t, f"mixT_c{i}") for i in range(5)]
        xT_c = [Buf(xT.t, f"xT_c{i}") for i in range(5)]
        hT = k.sb([128, 8, 528], BF16, "hT")
        slots = [k.sb([128, 4096], BF16, f"slot{i}") for i in range(NS)]
        stP = [k.sb([128, 1024], F32, f"stP{i}") for i in range(4)]
        stS = k.sb([128, 1024], F32, "stS")
        statebs = [k.sb([128, 1024], BF16, "stateb0"), k.sb([128, 1024], BF16, "stateb1")]
        tailsP = [k.sb([128, 12, 3], F32, f"tailsP{i}") for i in range(4)]
        tailsS = k.sb([128, 12, 3], F32, "tailsS")
        rowc = k.sb([128, 2, 1024], F32, "rowc")
        bsb = k.sb([128, 8, 128], F32, "bsb")
        WsT = k.sb([128, 8, 128], BF16, "WsT")
        identb = k.sb([128, 128], BF16, "identb"); identf = k.sb([128, 128], F32, "identf")
        Um = k.sb([128, 128], F32, "Um"); Ls = k.sb([128, 128], F32, "Ls"); ones = k.sb([128, 128], F32, "ones")
        convw = k.sb([128, 4, 12, 4], F32, "convw"); convb = k.sb([128, 4, 12], F32, "convb")
        gn = k.sb([128, 4, 8], F32, "gn")
        nbias = k.sb([128, 1], F32, "nbias")
        dtb = k.sb([128, 64], F32, "dtb"); aall = k.sb([128, 64], F32, "aall"); Dall = k.sb([128, 64], F32, "Dall")
        Wdt = k.sb([128, 4, 8, 16], BF16, "Wdt")
        F1s = [k.sb([128, 1024], F32, "F1a"), k.sb([128, 1024], F32, "F1b")]; F1 = F1s[0]
        F2 = k.sb([128, 1024], F32, "F2"); F3 = k.sb([128, 1024], F32, "F3")
        H1 = k.sb([128, 1024], BF16, "H1")
        st_t = k.sb([128, 12, 132], BF16, "st")
        diagw = k.sb([128, 48, 128], BF16, "diagw")
        xcTs = [k.sb([128, 12, 128], BF16, "xcT0"), k.sb([128, 12, 128], BF16, "xcT1")]
        xt = F2; wcol = k.sb([128, 16], F32, "wcol"); xdt = k.sb([128, 1024], BF16, "xdt"); xw = k.sb([128, 1024], BF16, "xw")
        Bt = k.sb([128, 256], BF16, "Bt")
        rhsU = [k.sb([128, 4, 128], F32, f"rhsU{i}") for i in range(2)]
        dec = k.sb([128, 16, 128], BF16, "dec")
        cbm = k.sb([128, 2, 128], F32, "cbm")
        tmpS = AliasBuf(F3, F3.t[:, 0:512].rearrange("p (a b) -> p a b", a=4), "tmpS")
        bst = k.sb([128, 2, 6], F32, "bst"); mv = k.sb([128, 2], F32, "mv"); rs = k.sb([128, 2], F32, "rs")
        dtvs = [k.sb([128, 16], F32, "dtv0"), k.sb([128, 16], F32, "dtv1")]; das = [k.sb([128, 16], F32, "da0"), k.sb([128, 16], F32, "da1")]
        expas = [k.sb([128, 16], F32, "expa0"), k.sb([128, 16], F32, "expa1")]; bdecs = [k.sb([128, 16], F32, "bdec0"), k.sb([128, 16], F32, "bdec1")]
        ssq = k.sb([128, 2], F32, "ssq")

        pf = [k.ps([128, 512], F32, f"pf{i}") for i in range(6)]
        pb = [k.ps([128, 1024], BF16, f"pb{i}") for i in range(2)]
        bank_i = [0]; pb_i = [0]
        relu_tmp = [tmpS, rhsU[0], rhsU[1]]; relu_i = [0]

        def bank():
            b = pf[bank_i[0]]; bank_i[0] = (bank_i[0] + 1) % len(pf); return b

        def pbank():
            b = pb[pb_i[0]]; pb_i[0] = (pb_i[0] + 1) % len(pb); return b

        def mm(out, lhsT, rhs, start, stop, reads, writes):
            k.op(pe, lambda e: e.matmul(out, lhsT=lhsT, rhs=rhs, start=start, stop=stop), reads=reads, writes=writes)

        def tr(out, in_, ident, reads, writes):
            k.op(pe, lambda e: e.transpose(out=out, in_=in_, identity=ident), reads=reads, writes=writes)

        def actf(out, in_, func, reads, writes, **kw):
            k.op(act, lambda e: e.activation(out=out, in_=in_, func=func, **kw), reads=reads, writes=writes)

        def tt(out, in0, in1, op, reads, writes, eng=None):
            k.op(eng or dve, lambda e: e.tensor_tensor(out=out, in0=in0, in1=in1, op=op), reads=reads, writes=writes)

        def v3(ap, a):
            return ap.rearrange("p (a b) -> p a b", a=a)

        k.op(dve, lambda e: e.memset(identf[:], 1.0), writes=[identf])
        k.op(dve, lambda e: e.memset(Um[:], 1.0), writes=[Um])
        k.op(dve, lambda e: e.memset(Ls[:], 1.0), writes=[Ls])
        k.op(dve, lambda e: e.memset(ones[:], 1.0), writes=[ones])
        k.op(pool, lambda e: e.affine_select(out=identf[:], in_=identf[:], pattern=[[-1, 128]], compare_op=ALU.is_equal,
                                             fill=0.0, base=0, channel_multiplier=1), reads=[identf], writes=[identf])
        k.op(pool, lambda e: e.affine_select(out=Um[:], in_=Um[:], pattern=[[1, 128]], compare_op=ALU.is_ge,
                                             fill=0.0, base=0, channel_multiplier=-1), reads=[Um], writes=[Um])
        k.op(pool, lambda e: e.affine_select(out=Ls[:], in_=Ls[:], pattern=[[-1, 128]], compare_op=ALU.is_ge,
                                             fill=0.0, base=-1, channel_multiplier=1), reads=[Ls], writes=[Ls])
        k.op(dve, lambda e: e.tensor_copy(out=identb[:], in_=identf[:]), reads=[identf], writes=[identb])
        for c in range(4):
            k.dma(sp, resid[c][0:128, :], xp[c * 128:(c + 1) * 128, :], writes=[resid[c]])
        k.dma(sp, resid[4][0:16, :], xs[:, :], writes=[resid[4]])
        for l in range(4):
            for kk in range(4):
                k.dma(sp, convw[:, l, :, kk], conv_w[l, kk].rearrange("(cb p) -> p cb", p=128), writes=[convw], allow_slow_non_contiguous=True)
            k.dma(sp, convb[:, l], conv_b[l].rearrange("(cb p) -> p cb", p=128), writes=[convb], allow_slow_non_contiguous=True)
            k.dma(sp, gn[:, l], ssd_g[l].rearrange("(eb p) -> p eb", p=128), writes=[gn], allow_slow_non_contiguous=True)
        k.dma(sp, dtb[:], dt_bias.partition_broadcast(128), writes=[dtb])
        k.dma(sp, aall[:], a_log.partition_broadcast(128), writes=[aall])
        k.dma(sp, Dall[:], d_skip.partition_broadcast(128), writes=[Dall])
        for l in range(4):
            k.op(dve, lambda e: e.memset(stP[l][:], 0.0), writes=[stP[l]])
            k.op(dve, lambda e: e.memset(tailsP[l][:], 0.0), writes=[tailsP[l]])

        pieces = []
        sub_of = {}
        sub_counter = [0]
        plan = []

        def add_piece(src, a):
            pieces.append((src, a)); return len(pieces) - 1

        def kp(ap):
            return ap.rearrange("(kk p) n -> p kk n", p=128)

        for t in range(ntiles):
            for l in range(depth):
                d = {}
                d["u"] = [add_piece(kp(w_in[l][:, i * 512:(i + 1) * 512]), 8) for i in range(2)]
                d["v"] = [add_piece(kp(w_in[l][:, 1024 + i * 512:1024 + (i + 1) * 512]), 8) for i in range(2)]
                d["z"] = [add_piece(kp(w_in[l][:, 2048 + i * 512:2048 + (i + 1) * 512]), 8) for i in range(2)]
                d["x"] = [add_piece(kp(w_in[l][:, 3072 + i * 512:3072 + (i + 1) * 512]), 8) for i in range(3)]
                d["o"] = [add_piece(kp(w_out[l][i * 512:(i + 1) * 512, :]), 4) for i in range(4)]
                for q in range(4):
                    d[f"f1_{q}"] = [add_piece(kp(w_ff1[l][:, q * 1024 + i * 512:q * 1024 + (i + 1) * 512]), 8) for i in range(2)]
                    d[f"f2_{q}"] = [add_piece(kp(w_ff2[l][q * 1024 + i * 512:q * 1024 + (i + 1) * 512, :]), 4) for i in range(2)]
                plan.append(d)
        issued = [0]
        done_upto = [-1]

        def pump():
            while issued[0] < len(pieces) and (issued[0] < NS or issued[0] - NS <= done_upto[0]):
                j = issued[0]
                src, a = pieces[j]
                s = slots[j % NS]
                k.dma(pool, s[:].rearrange("p (a b) -> p a b", a=a), src, writes=[s], q=wq[j % NS])
                issued[0] += 1

        def W(j, a):
            assert j < issued[0], "weight piece not issued before use"
            s = slots[j % NS]
            return s, s[:].rearrange("p (a b) -> p a b", a=a)

        def release(upto):
            done_upto[0] = max(done_upto[0], upto)
            pump()

        pump()
        for l in range(4):
            k.dma(pool, Wdt[:, l], w_in[l][:, 4608:4624].rearrange("(kk p) n -> p kk n", p=128), writes=[Wdt], q=pdq)

        def make_xT(ch):
            L, col = ch.L, ch.col
            r = resid[ch.rrow]
            actf(H1[0:L, :], r[0:L, :], AF.Copy, [r], [H1])
            p = pbank()
            pv = v3(p[:], 8)
            for kk in range(8):
                tr(pv[:, kk, 0:L], H1[0:L, kk * 128:(kk + 1) * 128], identb[0:L, 0:L], [H1, identb], [p])
            k.op(dve, lambda e: e.tensor_copy(out=xT[:, :, col:col + L], in_=pv[:, :, 0:L]), reads=[p], writes=[xT_c[ch.rrow]])

        def make_xT_b(ch):
            L, col = ch.L, ch.col
            p = pbank()
            pv = v3(p[:], 8)
            for kk in range(8):
                tr(pv[:, kk, 0:L], xw[0:L, kk * 128:(kk + 1) * 128], identb[0:L, 0:L], [xw, identb], [p])
            actf(xT[:, :, col:col + L], pv[:, :, 0:L], AF.Copy, [p], [xT_c[ch.rrow]])

        def layer_norm(ch, grow, brow):
            L = ch.L
            r = resid[ch.rrow]
            for i in range(2):
                k.op(dve, lambda e: e.bn_stats(out=bst[0:L, i, :], in_=r[0:L, i * 512:(i + 1) * 512]), reads=[r], writes=[bst])
            k.op(dve, lambda e: e.bn_aggr(out=mv[0:L, :], in_=bst[0:L, :, :]), reads=[bst], writes=[mv])
            actf(rs[0:L, 0:1], mv[0:L, 1:2], AF.Ln, [mv], [rs], bias=1e-5)
            actf(rs[0:L, 0:1], rs[0:L, 0:1], AF.Exp, [rs], [rs], scale=-0.5)
            k.op(dve, lambda e: e.tensor_scalar(out=nbias[0:L, :], in0=mv[0:L, 0:1], scalar1=rs[0:L, 0:1], scalar2=-1.0,
                                                op0=ALU.mult, op1=ALU.mult), reads=[mv, rs], writes=[nbias])
            actf(r[0:L, :], r[0:L, :], AF.Identity, [r, rs, nbias], [r], scale=rs[0:L, 0:1], bias=nbias[0:L, 0:1])
            tt(r[0:L, :], r[0:L, :], grow[0:L, :], ALU.mult, [r, rowc], [r])
            tt(r[0:L, :], r[0:L, :], brow[0:L, :], ALU.add, [r, rowc], [r])

        def load_rows(g_ap, b_ap):
            k.dma(sp, rowc[:, 0, :], g_ap.partition_broadcast(128), writes=[rowc])
            k.dma(sp, rowc[:, 1, :], b_ap.partition_broadcast(128), writes=[rowc])

        def layer_consts_a(l):
            k.dma(sp, v3(F1[:], 8), gws[l].rearrange("h t s -> t h s"), writes=[F1])
            actf(H1[:, :], F1[:, :], AF.Copy, [F1], [H1])
            k.dma(sp, bsb[:].rearrange("p a b -> p (a b)"), gbs[l:l + 1, :].partition_broadcast(128), writes=[bsb])
            for cb in range(12):
                for kk in range(4):
                    k.op(dve, lambda e: e.tensor_scalar(out=diagw[:, cb * 4 + kk, :], in0=identb[:, :], scalar1=convw[:, l, cb, kk:kk + 1],
                                                         scalar2=None, op0=ALU.mult), reads=[identb, convw], writes=[diagw])

        def layer_consts_b(l):
            p = pbank(); pv = v3(p[:], 8)
            for h in range(8):
                tr(pv[:, h, :], H1[:, h * 128:(h + 1) * 128], identb[:, :], [H1, identb], [p])
            tt(WsT[:], pv, Um[:].unsqueeze(1).to_broadcast([128, 8, 128]), ALU.mult, [p, Um], [WsT])

        def run_pipeline(gens):
            pending = [tuple(g) + (False,) * (3 - len(g)) for g in gens]; active = []
            while pending or active:
                if pending and (not active or active[-1][1] >= active[-1][2]):
                    g, lag, of = pending.pop(0)
                    active.append([g, 0, lag, of])
                order = [a for a in reversed(active) if not a[3]] + [a for a in active if a[3]]
                for a in order:
                    try:
                        next(a[0]); a[1] += 1
                    except StopIteration:
                        active.remove(a)

        par_ctr = [0]

        def A1_group(t, l, d):
            load_rows(gln_g[l:l + 1, :], gln_b[l:l + 1, :])
            colr = [(0, 512, [0, 1, 2, 3]), (512, 16, [4])] if t == 0 else [(0, 384, [0, 1, 2]), (384, 128, [3])]
            for ub in range(8):
                s_, sv = W(d["u"][ub // 4], 8)
                for (c0, n, cl) in colr:
                    b = bank()
                    for kk in range(8):
                        mm(b[:, 0:n], sv[:, kk, (ub % 4) * 128:(ub % 4 + 1) * 128], xT[:, kk, c0:c0 + n], kk == 0, kk == 7,
                           [s_] + [xT_c[c] for c in cl], [b])
                    actf(hT[:, ub, c0:c0 + n], b[:, 0:n], AF.Gelu_apprx_tanh, [b], [hT])
            release(d["u"][1])

        def A1_gen(t, l, ch, d, is_last):
            L, col = ch.L, ch.col
            par = par_ctr[0] % 2; par_ctr[0] += 1
            F1 = F1s[par]
            mx = mixT_c[ch.rrow]
            for nb in range(2):
                s_, sv = W(d["v"][nb], 8)
                b = bank()
                for kk in range(8):
                    mm(b[0:L, :], xT[:, kk, col:col + L], sv[:, kk, :], kk == 0, kk == 7, [s_, xT_c[ch.rrow]], [b])
                actf(F1[0:L, nb * 512:(nb + 1) * 512], b[0:L, :], AF.Gelu_apprx_tanh, [b], [F1])
            if is_last:
                release(d["v"][1])
            yield
            for i in range(2):
                k.op(dve, lambda e: e.bn_stats(out=bst[0:L, i, :], in_=F1[0:L, i * 512:(i + 1) * 512]), reads=[F1], writes=[bst])
            k.op(dve, lambda e: e.bn_aggr(out=mv[0:L, :], in_=bst[0:L, :, :]), reads=[bst], writes=[mv])
            actf(rs[0:L, 0:1], mv[0:L, 1:2], AF.Ln, [mv], [rs], bias=1e-5)
            actf(rs[0:L, 0:1], rs[0:L, 0:1], AF.Exp, [rs], [rs], scale=-0.5)
            k.op(dve, lambda e: e.tensor_scalar(out=nbias[0:L, :], in0=mv[0:L, 0:1], scalar1=rs[0:L, 0:1], scalar2=-1.0,
                                                op0=ALU.mult, op1=ALU.mult), reads=[mv, rs], writes=[nbias])
            actf(F1[0:L, :], F1[0:L, :], AF.Identity, [F1, rs, nbias], [F1], scale=rs[0:L, 0:1], bias=nbias[0:L, 0:1])
            tt(F1[0:L, :], F1[0:L, :], rowc[0:L, 0, :], ALU.mult, [F1, rowc], [F1])
            if ch.seq == "s":
                tt(F1[0:L, :], F1[0:L, :], rowc[0:L, 1, :], ALU.add, [F1, rowc], [F1])
                k.dma(sp, v_s[l], F1[0:L, :], reads=[F1])
                k.op(dve, lambda e: e.tensor_copy(out=H1[0:L, :], in_=F1[0:L, :]), reads=[F1], writes=[H1])
            else:
                tt(H1[0:L, :], F1[0:L, :], rowc[0:L, 1, :], ALU.add, [F1, rowc], [H1])
            yield
            bs2 = [bank(), bank()]
            for h in range(8):
                b = bs2[h // 4]
                mm(v3(b[:], 4)[:, h % 4, 0:L], H1[0:L, h * 128:(h + 1) * 128], WsT[0:L, h, 0:L], True, True, [H1, WsT], [b])
            for hb in range(2):
                b = bs2[hb]
                tt(tmpS[:, :, 0:L], v3(b[:], 4)[:, :, 0:L], bsb[:, 4 * hb:4 * hb + 4, 0:L], ALU.add, [b, bsb], [tmpS])
                tt(mixT[:, 4 * hb:4 * hb + 4, col:col + L], tmpS[:, :, 0:L], hT[:, 4 * hb:4 * hb + 4, col:col + L], ALU.mult,
                   [tmpS, hT], [mx])
            yield

        def A2F_gen(t, l, ch, d, is_last, par):
            L, col = ch.L, ch.col
            F1 = F1s[par]; expa = expas[par]; xcT = xcTs[par]; dtv = dtvs[par]; da = das[par]; bdec = bdecs[par]
            mx = mixT_c[ch.rrow]
            samp = ch.seq == "s"
            state = stS if samp else stP[l]
            tails = tailsS if samp else tailsP[l]
            has_state = samp or not ch.first
            if samp:
                stf = F1[:]
                k.dma(sp, v3(stf[:, 0:1024], 8), sssm[l].rearrange("(blk p) n -> p blk n", p=128), writes=[F1])
                for half in range(2):
                    b = bank()
                    for j in range(4):
                        blk = half * 4 + j
                        tr(b[:, j * 128:(j + 1) * 128], stf[:, blk * 128:(blk + 1) * 128], identf[:, :], [F1, identf], [b])
                    actf(stS[:, half * 512:(half + 1) * 512], b[:, :], AF.Copy, [b], [stS])
                for kk in range(3):
                    k.dma(sp, tailsS[:, :, kk], sconv[l, kk].rearrange("(cb p) -> p cb", p=128), writes=[tailsS], allow_slow_non_contiguous=True)
            bd = bank()
            for kk in range(8):
                mm(bd[0:L, 0:16], xT[:, kk, col:col + L], Wdt[:, l, kk, :], kk == 0, kk == 7, [xT_c[ch.rrow], Wdt], [bd])
            tt(dtv[0:L, :], bd[0:L, 0:16], dtb[0:L, l * 16:(l + 1) * 16], ALU.add, [bd, dtb], [dtv])
            actf(dtv[0:L, :], dtv[0:L, :], AF.Exp, [dtv], [dtv])
            actf(dtv[0:L, :], dtv[0:L, :], AF.Ln, [dtv], [dtv], bias=1.0)
            tt(da[0:L, :], dtv[0:L, :], aall[0:L, l * 16:(l + 1) * 16], ALU.mult, [dtv, aall], [da])
            mm(bd[0:L, 32:48], Um[0:L, 0:L], da[0:L, :], True, True, [Um, da], [bd])
            mm(bd[:, 64:80], ones[0:L, :], da[0:L, :], True, True, [ones, da], [bd])
            actf(expa[0:L, :], bd[0:L, 32:48], AF.Exp, [bd], [expa])
            actf(bdec[:, :], bd[:, 64:80], AF.Exp, [bd], [bdec])
            yield
            for nb in range(2):
                s_, sv = W(d["z"][nb], 8)
                b = bank()
                for kk in range(8):
                    mm(b[0:L, :], xT[:, kk, col:col + L], sv[:, kk, :], kk == 0, kk == 7, [s_, xT_c[ch.rrow]], [b])
                actf(F1[0:L, nb * 512:(nb + 1) * 512], b[0:L, :], AF.Silu, [b], [F1])
            yield
            xb_banks = [bank(), bank(), bank()]
            for cb in range(12):
                s_, sv = W(d["x"][cb // 4], 8)
                b = xb_banks[cb // 4]
                for kk in range(8):
                    mm(v3(b[:], 4)[:, cb % 4, 0:L], sv[:, kk, (cb % 4) * 128:(cb % 4 + 1) * 128], xT[:, kk, col:col + L],
                       kk == 0, kk == 7, [s_, xT_c[ch.rrow]], [b])
            if is_last:
                release(d["x"][2])
            k.op(dve, lambda e: e.tensor_copy(out=st_t[:, :, 0:3], in_=tails[:, :, :]), reads=[tails], writes=[st_t])
            for j in range(3):
                b = xb_banks[j]
                actf(st_t[:, 4 * j:4 * j + 4, 3:3 + L], v3(b[:], 4)[:, :, 0:L], AF.Copy, [b], [st_t])
                k.op(dve, lambda e: e.tensor_copy(out=tails[:, 4 * j:4 * j + 4, :], in_=v3(b[:], 4)[:, :, L - 3:L]), reads=[b, st_t], writes=[tails])
            yield
            cv_banks = [bank(), bank(), bank()]
            for cb in range(12):
                b = cv_banks[cb // 4]
                for kk in range(4):
                    mm(v3(b[:], 4)[:, cb % 4, 0:L], diagw[:, cb * 4 + kk, :], st_t[:, cb, kk:kk + L], kk == 0, kk == 3, [diagw, st_t], [b])
            for cb in range(12):
                b = cv_banks[cb // 4]
                actf(xcT[:, cb, 0:L], v3(b[:], 4)[:, cb % 4, 0:L], AF.Silu, [b, convb], [xcT], bias=convb[:, l, cb:cb + 1])
            if ch.last:
                for j in range(3):
                    b = bank()
                    for i in range(4):
                        tr(b[0:3, i * 128:(i + 1) * 128], tails[:, 4 * j + i, :], identf[:, :], [tails, identf], [b])
                    rt = relu_tmp[j]
                    rtv = rt[:].rearrange("p a b -> p (a b)")
                    actf(rtv[0:3, :], b[0:3, :], AF.Copy, [b], [rt])
                    k.dma(sp, (conv_s if samp else conv_p)[l][:, j * 512:(j + 1) * 512], rtv[0:3, :], reads=[rt])
            yield

        def A2B_gen(t, l, ch, d, is_last, par):
            L, col = ch.L, ch.col
            F1 = F1s[par]; expa = expas[par]; xcT = xcTs[par]; dtv = dtvs[par]; da = das[par]; bdec = bdecs[par]
            mx = mixT_c[ch.rrow]
            samp = ch.seq == "s"
            state = stS if samp else stP[l]
            tails = tailsS if samp else tailsP[l]
            has_state = samp or not ch.first
            stateb = statebs[par]
            if samp or (has_state and ch.rrow == 0):
                actf(stateb[:, :], state[:, :], AF.Copy, [state], [stateb])
            if has_state:
                tt(v3(state[:, :], 16), v3(state[:, :], 16), bdec[:, :].unsqueeze(2).to_broadcast([128, 16, 64]), ALU.mult,
                   [state, bdec], [state], eng=pool)
            pA = pbank(); pAv = v3(pA[:], 8)
            for hb in range(8):
                tr(pAv[0:L, hb, :], xcT[:, hb, 0:L], identb[:, :], [xcT, identb], [pA])
            actf(xt[0:L, :], pA[0:L, :], AF.Copy, [pA], [xt])
            tt(v3(xdt[0:L, :], 16), v3(xt[0:L, :], 16), dtv[0:L, :].unsqueeze(2).to_broadcast([L, 16, 64]), ALU.mult, [xt, dtv], [xdt])
            tt(v3(F3[0:L, :], 16), v3(xt[0:L, :], 16), Dall[0:L, l * 16:(l + 1) * 16].unsqueeze(2).to_broadcast([L, 16, 64]),
               ALU.mult, [xt, Dall], [F3], eng=pool)
            pB = pbank()
            for g in range(2):
                tr(pB[0:L, g * 128:(g + 1) * 128], xcT[:, 8 + g, 0:L], identb[:, :], [xcT, identb], [pB])
            actf(Bt[0:L, :], pB[0:L, 0:256], AF.Copy, [pB], [Bt])
            bc = bank()
            for g in range(2):
                mm(v3(bc[:], 4)[0:L, g, 0:L], xcT[:, 8 + g, 0:L], xcT[:, 10 + g, 0:L], True, True, [xcT], [bc])
            tt(cbm[0:L, :, 0:L], v3(bc[:], 4)[0:L, 0:2, 0:L], Um[0:L, 0:L].unsqueeze(1).to_broadcast([L, 2, L]), ALU.mult, [bc, Um], [cbm])
            yield
            for hq in range(4):
                ru = rhsU[hq % 2]
                tt(ru[0:L, :, 0:L], Um[0:L, 0:L].unsqueeze(1).to_broadcast([L, 4, L]),
                   da[0:L, hq * 4:hq * 4 + 4].unsqueeze(2).to_broadcast([L, 4, L]), ALU.mult, [Um, da], [ru])
                b = bank()
                if L == 128:
                    mm(v3(b[:], 4)[0:L, :, 0:L], Ls[0:L, 0:L], ru[0:L, :, 0:L], True, True, [Ls, ru], [b])
                else:
                    for hh in range(4):
                        mm(v3(b[:], 4)[0:L, hh, 0:L], Ls[0:L, 0:L], ru[0:L, hh, 0:L], True, True, [Ls, ru], [b])
                actf(dec[0:L, hq * 4:hq * 4 + 4, 0:L], v3(b[:], 4)[0:L, :, 0:L], AF.Exp, [b], [dec])
            tt(wcol[0:L, :].unsqueeze(2), dtv[0:L, :].unsqueeze(2), dec[0:L, :, L - 1:L], ALU.mult, [dtv, dec], [wcol])
            tt(v3(xw[0:L, :], 16), v3(xt[0:L, :], 16), wcol[0:L, :].unsqueeze(2).to_broadcast([L, 16, 64]), ALU.mult, [xt, wcol], [xw])
            yield
            for g in range(2):
                tt(dec[0:L, g * 8:(g + 1) * 8, 0:L], dec[0:L, g * 8:(g + 1) * 8, 0:L],
                   cbm[0:L, g:g + 1, 0:L].to_broadcast([L, 8, L]), ALU.mult, [dec, cbm], [dec])
            bS = [bank(), bank()]
            for g in range(2):
                mm(bS[g][:, :], Bt[0:L, g * 128:(g + 1) * 128], xw[0:L, g * 512:(g + 1) * 512], True, True, [Bt, xw], [bS[g]])
            if has_state:
                for g in range(2):
                    tt(state[:, g * 512:(g + 1) * 512], state[:, g * 512:(g + 1) * 512], bS[g][:, :], ALU.add, [state, bS[g]], [state])
            else:
                for g in range(2):
                    actf(state[:, g * 512:(g + 1) * 512], bS[g][:, :], AF.Copy, [bS[g]], [state])
            if not samp and ch.rrow < 3:
                actf(statebs[1 - par][:, :], state[:, :], AF.Copy, [state], [statebs[1 - par]])
            yield
            bY = [bank(), bank()]
            for h in range(16):
                b = bY[h // 8]
                mm(b[0:L, (h % 8) * 64:(h % 8 + 1) * 64], dec[0:L, h, 0:L], xdt[0:L, h * 64:(h + 1) * 64], True, True, [dec, xdt], [b])
            if has_state:
                bO = [bank(), bank()]
                for g in range(2):
                    mm(bO[g][0:L, :], xcT[:, 10 + g, 0:L], stateb[:, g * 512:(g + 1) * 512], True, True, [xcT, stateb], [bO[g]])
                for g in range(2):
                    tt(v3(F2[0:L, g * 512:(g + 1) * 512], 8), v3(bO[g][0:L, :], 8),
                       expa[0:L, g * 8:(g + 1) * 8].unsqueeze(2).to_broadcast([L, 8, 64]), ALU.mult, [bO[g], expa], [F2])
                    tt(F2[0:L, g * 512:(g + 1) * 512], F2[0:L, g * 512:(g + 1) * 512], bY[g][0:L, :], ALU.add, [F2, bY[g]], [F2])
            else:
                for g in range(2):
                    actf(F2[0:L, g * 512:(g + 1) * 512], bY[g][0:L, :], AF.Copy, [bY[g]], [F2])
            tt(F2[0:L, :], F2[0:L, :], F3[0:L, :], ALU.add, [F2, F3], [F2])
            tt(F2[0:L, :], F2[0:L, :], F1[0:L, :], ALU.mult, [F2, F1], [F2])
            yield
            k.op(dve, lambda e: e.memset(ssq[:], 0.0), writes=[ssq])
            actf(rs[0:1, 0:1], ones[0:1, 0:1], AF.Exp, [ones], [rs])
            for g in range(2):
                actf(H1[0:L, g * 512:(g + 1) * 512], F2[0:L, g * 512:(g + 1) * 512], AF.Square, [F2, ssq], [H1, ssq],
                     accum_out=ssq[0:L, g:g + 1])
            actf(rs[0:L, :], ssq[0:L, :], AF.Ln, [ssq], [rs], scale=1.0 / 512.0, bias=1e-5)
            actf(rs[0:L, :], rs[0:L, :], AF.Exp, [rs], [rs], scale=-0.5)
            for g in range(2):
                k.op(dve, lambda e: e.tensor_scalar(out=H1[0:L, g * 512:(g + 1) * 512], in0=F2[0:L, g * 512:(g + 1) * 512],
                                                    scalar1=rs[0:L, g:g + 1], scalar2=None, op0=ALU.mult), reads=[F2, rs], writes=[H1])
            pC = pbank(); pCv = v3(pC[:], 8)
            for eb in range(8):
                tr(pCv[:, eb, 0:L], H1[0:L, eb * 128:(eb + 1) * 128], identb[0:L, 0:L], [H1, identb], [pC])
            k.op(dve, lambda e: e.tensor_copy(out=mixT[:, 8:16, col:col + L], in_=pCv[:, :, 0:L]), reads=[pC], writes=[mx])
            if ch.last:
                for half in range(2):
                    b = bank()
                    for j in range(4):
                        blk = half * 4 + j
                        tr(b[:, j * 128:(j + 1) * 128], state[:, blk * 128:(blk + 1) * 128], identf[:, :], [state, identf], [b])
                    actf(F3[:, half * 512:(half + 1) * 512], b[:, :], AF.Copy, [b], [F3])
                k.dma(sp, (ssm_s if samp else ssm_p)[l].rearrange("(blk p) n -> p blk n", p=128), v3(F3[:], 8), reads=[F3])
            yield

        def A3_gen(t, l, ch, d, is_first, is_last):
            L, col = ch.L, ch.col
            r = resid[ch.rrow]
            mx = mixT_c[ch.rrow]
            if is_first:
                load_rows(ln1_g[l:l + 1, :], ln1_b[l:l + 1, :])
                for pi in (2, 3):
                    s_, sv = W(d["o"][pi], 4)
                    for j in range(4):
                        eb = (pi - 2) * 4 + j
                        k.op(dve, lambda e: e.tensor_scalar(out=sv[:, j, :], in0=sv[:, j, :], scalar1=gn[:, l, eb:eb + 1], scalar2=None,
                                                            op0=ALU.mult), reads=[s_, gn], writes=[s_])
            for nb in range(2):
                b = bank()
                for kk in range(16):
                    s_, sv = W(d["o"][kk // 4], 4)
                    mm(b[0:L, :], mixT[:, kk, col:col + L], sv[:, kk % 4, nb * 512:(nb + 1) * 512], kk == 0, kk == 15, [s_, mx], [b])
                k.op(dve, lambda e: e.scalar_tensor_tensor(out=r[0:L, nb * 512:(nb + 1) * 512], in0=r[0:L, nb * 512:(nb + 1) * 512],
                                                           scalar=float(ALPHA), in1=b[0:L, :], op0=ALU.mult, op1=ALU.add),
                     reads=[r, b], writes=[r])
            if is_last:
                release(d["o"][3])
            yield
            layer_norm(ch, rowc[:, 0, :], rowc[:, 1, :])
            actf(xw[0:L, :], r[0:L, :], AF.Copy, [r], [xw])
            yield
            make_xT_b(ch)
            yield

        def B_group(t, l, q, d):
            if q == 3:
                load_rows(ln2_g[l:l + 1, :], ln2_b[l:l + 1, :])
            if t == 0:
                colr = [(0, 512, [0, 1, 2, 3]), (512, 16, [4])]
            elif q == 0:
                colr = [(0, 384, [0, 1, 2]), (384, 128, [3])]
            else:
                colr = [(0, 512, [0, 1, 2, 3])]
            for fb in range(8):
                s_, sv = W(d[f"f1_{q}"][fb // 4], 8)
                for (c0, n, cl) in colr:
                    b = bank()
                    for kk in range(8):
                        mm(b[:, 0:n], sv[:, kk, (fb % 4) * 128:(fb % 4 + 1) * 128], xT[:, kk, c0:c0 + n], kk == 0, kk == 7,
                           [s_] + [xT_c[c] for c in cl], [b])
                    rt = relu_tmp[relu_i[0] % 3]; relu_i[0] += 1
                    rtv = rt[:].rearrange("p a b -> p (a b)")
                    actf(rtv[:, 0:n], b[:, 0:n], AF.Relu, [b], [rt])
                    tt(hT[:, fb, c0:c0 + n], rtv[:, 0:n], rtv[:, 0:n], ALU.mult, [rt], [hT])
            release(d[f"f1_{q}"][1])

        def B_gen(t, l, q, ch, d, is_last):
            L, col = ch.L, ch.col
            r = resid[ch.rrow]
            for nb in range(2):
                b = bank()
                for fc in range(8):
                    s_, sv = W(d[f"f2_{q}"][fc // 4], 4)
                    mm(b[0:L, :], hT[:, fc, col:col + L], sv[:, fc % 4, nb * 512:(nb + 1) * 512], fc == 0, fc == 7, [s_, hT], [b])
                rr = r[0:L, nb * 512:(nb + 1) * 512]
                if q == 0:
                    k.op(dve, lambda e: e.scalar_tensor_tensor(out=rr, in0=rr, scalar=float(ALPHA), in1=b[0:L, :],
                                                               op0=ALU.mult, op1=ALU.add), reads=[r, b], writes=[r])
                else:
                    tt(rr, rr, b[0:L, :], ALU.add, [r, b], [r])
            if is_last:
                release(d[f"f2_{q}"][1])
            yield
            if q == 3:
                layer_norm(ch, rowc[:, 0, :], rowc[:, 1, :])
                if l != depth - 1:
                    actf(xw[0:L, :], r[0:L, :], AF.Copy, [r], [xw])
                yield
                if l == depth - 1:
                    if ch.seq == "s":
                        k.dma(sp, ys[:, :], r[0:L, :], reads=[r])
                    else:
                        k.dma(sp, yp[ch.tok0:ch.tok0 + L, :], r[0:L, :], reads=[r])
                else:
                    make_xT_b(ch)
                yield

        def chk(tag):
            if stop == tag:
                raise _Stop()
        try:
          for t in range(ntiles):
              chunks = [Chunk(128, c * 128, c, "p", t == 0 and c == 0, t == ntiles - 1 and c == 3, t * 512 + c * 128) for c in range(4)]
              if t == 0:
                  chunks.append(Chunk(16, 512, 4, "s", True, True, 0))
              for ch in chunks:
                  if t > 0:
                      k.dma(sp, resid[ch.rrow][0:ch.L, :], xp[ch.tok0:ch.tok0 + ch.L, :], writes=[resid[ch.rrow]])
                  make_xT(ch)
              n = len(chunks)
              if t == 0:
                  actf(aall[:], aall[:], AF.Exp, [aall], [aall])
                  k.op(dve, lambda e: e.tensor_scalar(out=aall[:], in0=aall[:], scalar1=-1.0, scalar2=None, op0=ALU.mult),
                       reads=[aall], writes=[aall])
              for l in range(depth):
                  d = plan[t * depth + l]
                  layer_consts_a(l)
                  A1_group(t, l, d)
                  layer_consts_b(l)
                  gens = []
                  gens += [(A1_gen(t, l, ch, d, i == n - 1), 1, True) for i, ch in enumerate(chunks)]
                  for i, ch in enumerate(chunks):
                      gens.append((A2F_gen(t, l, ch, d, i == n - 1, i % 2), 4))
                      gens.append((A2B_gen(t, l, ch, d, i == n - 1, i % 2), 0))
                  gens += [(A3_gen(t, l, ch, d, i == 0, i == n - 1), 1, True) for i, ch in enumerate(chunks)]
                  run_pipeline(gens)
                  chk("A3")
                  for q in range(4):
                      B_group(t, l, q, d)
                      run_pipeline([(B_gen(t, l, q, ch, d, i == n - 1), 1, True) for i, ch in enumerate(chunks)])
        except _Stop:
            pass
        k.finish(extra=wq + [pdq])
    return nc


_NC_CACHE = {}


def kernel(x_prompt, x_sample, state_ssm, state_conv, w_in, gmlp_ln_g, gmlp_ln_b, gmlp_ws, gmlp_bs,
           conv_w, conv_b, dt_bias, a_log, d_skip, ssd_norm_g, w_out, ln1_g, ln1_b, w_ff1, w_ff2,
           ln2_g, ln2_b):
    f = lambda a: np.ascontiguousarray(np.asarray(a, dtype=np.float32))
    if "nc" not in _NC_CACHE:
        _NC_CACHE["nc"] = build_program()
    nc = _NC_CACHE["nc"]
    shared = {
        "w_in": f(w_in), "gmlp_ln_g": f(gmlp_ln_g), "gmlp_ln_b": f(gmlp_ln_b), "gmlp_ws": f(gmlp_ws),
        "gmlp_bs": f(gmlp_bs).reshape(4, 1024), "conv_w": f(conv_w), "conv_b": f(conv_b),
        "dt_bias": f(dt_bias).reshape(1, 64), "a_log": f(a_log).reshape(1, 64), "d_skip": f(d_skip).reshape(1, 64),
        "ssd_norm_g": f(ssd_norm_g), "w_out": f(w_out), "ln1_g": f(ln1_g), "ln1_b": f(ln1_b),
        "w_ff1": f(w_ff1), "w_ff2": f(w_ff2), "ln2_g": f(ln2_g), "ln2_b": f(ln2_b),
    }
    xp = f(x_prompt); xs = f(x_sample); ss = f(state_ssm); sc = f(state_conv)
    in_maps = []
    for b in range(8):
        m = dict(shared)
        m["xp"] = xp[b]; m["xs"] = xs[b]
        m["sssm"] = np.ascontiguousarray(ss[:, b].reshape(4, 1024, 128))
        m["sconv"] = np.ascontiguousarray(sc[:, b])
        in_maps.append(m)
    res = run_bass_kernel_spmd(nc, in_maps, core_ids=list(range(8)))
    R = res.results
    y_prompt = np.stack([R[b]["yp"] for b in range(8)], axis=0)
    y_sample = np.stack([R[b]["ys"] for b in range(8)], axis=0)
    ssm_p = np.stack([R[b]["ssm_p"].reshape(4, 16, 64, 128) for b in range(8)], axis=1)
    conv_p = np.stack([R[b]["conv_p"] for b in range(8)], axis=1)
    ssm_s = np.stack([R[b]["ssm_s"].reshape(4, 16, 64, 128) for b in range(8)], axis=1)
    conv_s = np.stack([R[b]["conv_s"] for b in range(8)], axis=1)
    v_s = np.stack([R[b]["v_s"] for b in range(8)], axis=1)
    return (y_prompt, y_sample, ssm_p, conv_p, ssm_s, conv_s, v_s)
```

```python
import contextlib
import numpy as np
import concourse.bass as bass
import concourse.mybir as mybir
from concourse.bass_utils import run_bass_kernel_spmd

F32 = mybir.dt.float32
BF16 = mybir.dt.bfloat16
AF = mybir.ActivationFunctionType
ALU = mybir.AluOpType

DEPTH = 4
NTILES = 4
ALPHA = (2 * DEPTH) ** 0.25
NS = 7


class Eng:
    def __init__(self, name, h, sem, step=1, is_pe=False):
        self.name = name; self.h = h; self.sem = sem; self.step = step
        self.count = 0; self.is_pe = is_pe; self.waited = {}


class Buf:
    def __init__(self, t=None, name=""):
        self.t = t; self.name = name; self.w = None; self.r = {}

    def __getitem__(self, k):
        return self.t[k]


class AliasBuf(Buf):
    def __init__(self, base, view, name=""):
        self.base = base; self.view = view; self.name = name

    def __getitem__(self, k):
        return self.view[k]

    @property
    def w(self):
        return self.base.w

    @w.setter
    def w(self, v):
        self.base.w = v

    @property
    def r(self):
        return self.base.r

    @r.setter
    def r(self, v):
        self.base.r = v


class K:
    def __init__(self, nc, es):
        self.nc = nc; self.es = es
        self.pe = Eng("pe", nc.tensor, self.sem("s_pe"), is_pe=True)
        self.act = Eng("act", nc.scalar, self.sem("s_act"))
        self.dve = Eng("dve", nc.vector, self.sem("s_dve"))
        self.pool = Eng("pool", nc.gpsimd, self.sem("s_pool"))
        self.sp = Eng("sp", nc.sync, self.sem("s_sp"))
        self.dq = [Eng(f"dq{i}", None, self.sem(f"s_dq{i}"), step=16) for i in range(16)]
        self.dq_i = 0
        self.nbuf = 0

    def sem(self, n):
        return self.es.enter_context(self.nc.semaphore(n))

    def sb(self, shape, dt, name=None):
        self.nbuf += 1
        name = name or f"b{self.nbuf}"
        return Buf(self.es.enter_context(self.nc.sbuf_tensor(name, list(shape), dt)), name)

    def ps(self, shape, dt, name=None):
        self.nbuf += 1
        name = name or f"p{self.nbuf}"
        return Buf(self.es.enter_context(self.nc.psum_tensor(name, list(shape), dt)), name)

    def _wait(self, eng, e2, ts):
        if e2 is eng and eng.is_pe:
            return
        if eng.waited.get(e2.name, 0) >= ts:
            return
        eng.h.wait_ge(e2.sem, ts)
        eng.waited[e2.name] = ts

    def _deps(self, eng, reads, writes):
        deps = {}

        def add(e2, ts):
            if deps.get(e2.name, (None, 0))[1] < ts:
                deps[e2.name] = (e2, ts)
        for b in reads:
            if b.w: add(*b.w)
        for b in writes:
            if b.w: add(*b.w)
            for e2, ts in b.r.values(): add(e2, ts)
        for e2, ts in deps.values():
            self._wait(eng, e2, ts)

    def op(self, eng, fn, reads=(), writes=()):
        self._deps(eng, reads, writes)
        ins = fn(eng.h)
        eng.count += 1
        ins.then_inc(eng.sem, 1)
        ts = eng.count
        for b in reads: b.r[eng.name] = (eng, ts)
        for b in writes:
            b.w = (eng, ts); b.r = {}
        return ins

    def dma(self, issuer, out, in_, reads=(), writes=(), q=None, **kw):
        if q is None:
            q = self.dq[self.dq_i]; self.dq_i = (self.dq_i + 1) % len(self.dq)
        if q.count > 0:
            self._wait(issuer, q, q.count * 16)
        self._deps(issuer, reads, writes)
        ins = issuer.h.dma_start(out=out, in_=in_, **kw)
        q.count += 1
        ins.then_inc(q.sem, 16)
        ts = q.count * 16
        for b in reads: b.r[q.name] = (q, ts)
        for b in writes:
            b.w = (q, ts); b.r = {}

    def finish(self, extra=()):
        for q in list(self.dq) + list(extra):
            if q.count: self._wait(self.sp, q, q.count * 16)


class Chunk:
    def __init__(self, L, col, rrow, seq, first, last, tok0):
        self.L = L; self.col = col; self.rrow = rrow; self.seq = seq
        self.first = first; self.last = last; self.tok0 = tok0


class _Stop(Exception):
    pass


def build_program(ntiles=NTILES, depth=DEPTH, stop=None):
    nc = bass.Bass("TRN2", target_bir_lowering=False)

    def din(name, shape):
        return nc.dram_tensor(name, list(shape), F32, kind="ExternalInput").ap()

    def dout(name, shape):
        return nc.dram_tensor(name, list(shape), F32, kind="ExternalOutput").ap()

    xp = din("xp", [2048, 1024]); xs = din("xs", [16, 1024])
    sssm = din("sssm", [4, 1024, 128]); sconv = din("sconv", [4, 3, 1536])
    w_in = din("w_in", [4, 1024, 4624])
    gln_g = din("gmlp_ln_g", [4, 1024]); gln_b = din("gmlp_ln_b", [4, 1024])
    gws = din("gmlp_ws", [4, 8, 128, 128]); gbs = din("gmlp_bs", [4, 1024])
    conv_w = din("conv_w", [4, 4, 1536]); conv_b = din("conv_b", [4, 1536])
    dt_bias = din("dt_bias", [1, 64]); a_log = din("a_log", [1, 64]); d_skip = din("d_skip", [1, 64])
    ssd_g = din("ssd_norm_g", [4, 1024])
    w_out = din("w_out", [4, 2048, 1024])
    ln1_g = din("ln1_g", [4, 1024]); ln1_b = din("ln1_b", [4, 1024])
    w_ff1 = din("w_ff1", [4, 1024, 4096]); w_ff2 = din("w_ff2", [4, 4096, 1024])
    ln2_g = din("ln2_g", [4, 1024]); ln2_b = din("ln2_b", [4, 1024])

    yp = dout("yp", [2048, 1024]); ys = dout("ys", [16, 1024])
    ssm_p = dout("ssm_p", [4, 1024, 128]); conv_p = dout("conv_p", [4, 3, 1536])
    ssm_s = dout("ssm_s", [4, 1024, 128]); conv_s = dout("conv_s", [4, 3, 1536])
    v_s = dout("v_s", [4, 16, 1024])

    with contextlib.ExitStack() as es:
        k = K(nc, es)
        pe, act, dve, pool, sp = k.pe, k.act, k.dve, k.pool, k.sp
        wq = [Eng(f"wq{i}", None, k.sem(f"s_wq{i}"), step=16) for i in range(NS)]
        pdq = Eng("pdq", None, k.sem("s_pdq"), step=16)

        resid = [k.sb([128, 1024], F32, f"resid{i}") for i in range(5)]
        xT = k.sb([128, 8, 528], BF16, "xT")
        mixT = k.sb([128, 16, 528], BF16, "mixT")
        mixT_c = [Buf(mixT.t, f"mixT_c{i}") for i in range(5)]
        xT_c = [Buf(xT.t, f"xT_c{i}") for i in range(5)]
        hT = k.sb([128, 8, 528], BF16, "hT")
        slots = [k.sb([128, 4096], BF16, f"slot{i}") for i in range(NS)]
        stP = [k.sb([128, 1024], F32, f"stP{i}") for i in range(4)]
        stS = k.sb([128, 1024], F32, "stS")
        statebs = [k.sb([128, 1024], BF16, "stateb0"), k.sb([128, 1024], BF16, "stateb1")]
        tailsP = [k.sb([128, 12, 3], F32, f"tailsP{i}") for i in range(4)]
        tailsS = k.sb([128, 12, 3], F32, "tailsS")
        rowc = k.sb([128, 2, 1024], F32, "rowc")
        bsb = k.sb([128, 8, 128], F32, "bsb")
        WsT = k.sb([128, 8, 128], BF16, "WsT")
        identb = k.sb([128, 128], BF16, "identb"); identf = k.sb([128, 128], F32, "identf")
        Um = k.sb([128, 128], F32, "Um"); Ls = k.sb([128, 128], F32, "Ls"); ones = k.sb([128, 128], F32, "ones")
        convw = k.sb([128, 4, 12, 4], F32, "convw"); convb = k.sb([128, 4, 12], F32, "convb")
        gn = k.sb([128, 4, 8], F32, "gn")
        dtb = k.sb([128, 64], F32, "dtb"); aall = k.sb([128, 64], F32, "aall"); Dall = k.sb([128, 64], F32, "Dall")
        Wdt = k.sb([128, 4, 8, 16], BF16, "Wdt")
        F1s = [k.sb([128, 1024], F32, "F1a"), k.sb([128, 1024], F32, "F1b")]; F1 = F1s[0]
        F2 = k.sb([128, 1024], F32, "F2"); F3 = k.sb([128, 1024], F32, "F3")
        H1 = k.sb([128, 1024], BF16, "H1")
        st_t = k.sb([128, 12, 132], BF16, "st")
        diagw = k.sb([128, 48, 128], BF16, "diagw")
        xcTs = [k.sb([128, 12, 128], BF16, "xcT0"), k.sb([128, 12, 128], BF16, "xcT1")]
        xt = F2; wcol = k.sb([128, 16], F32, "wcol"); xdt = k.sb([128, 1024], BF16, "xdt"); xw = k.sb([128, 1024], BF16, "xw")
        Bt = k.sb([128, 256], BF16, "Bt")
        rhsU = [k.sb([128, 4, 128], F32, f"rhsU{i}") for i in range(2)]
        dec = k.sb([128, 16, 128], BF16, "dec")
        cbm = k.sb([128, 2, 128], BF16, "cbm")
        tmpS = AliasBuf(F3, F3.t[:, 0:512].rearrange("p (a b) -> p a b", a=4), "tmpS")
        bst = k.sb([128, 2, 6], F32, "bst"); mv = k.sb([128, 2], F32, "mv"); rs = k.sb([128, 2], F32, "rs")
        dtvs = [k.sb([128, 16], F32, "dtv0"), k.sb([128, 16], F32, "dtv1")]; das = [k.sb([128, 16], F32, "da0"), k.sb([128, 16], F32, "da1")]
        expas = [k.sb([128, 16], F32, "expa0"), k.sb([128, 16], F32, "expa1")]; bdecs = [k.sb([128, 16], F32, "bdec0"), k.sb([128, 16], F32, "bdec1")]
        ssq = k.sb([128, 2], F32, "ssq")

        pf = [k.ps([128, 512], F32, f"pf{i}") for i in range(6)]
        pb = [k.ps([128, 1024], BF16, f"pb{i}") for i in range(2)]
        bank_i = [0]; pb_i = [0]
        relu_tmp = [tmpS, rhsU[0], rhsU[1]]; relu_i = [0]

        def bank():
            b = pf[bank_i[0]]; bank_i[0] = (bank_i[0] + 1) % len(pf); return b

        def pbank():
            b = pb[pb_i[0]]; pb_i[0] = (pb_i[0] + 1) % len(pb); return b

        def mm(out, lhsT, rhs, start, stop, reads, writes):
            k.op(pe, lambda e: e.matmul(out, lhsT=lhsT, rhs=rhs, start=start, stop=stop), reads=reads, writes=writes)

        def tr(out, in_, ident, reads, writes):
            k.op(pe, lambda e: e.transpose(out=out, in_=in_, identity=ident), reads=reads, writes=writes)

        def actf(out, in_, func, reads, writes, **kw):
            k.op(act, lambda e: e.activation(out=out, in_=in_, func=func, **kw), reads=reads, writes=writes)

        def tt(out, in0, in1, op, reads, writes, eng=None):
            k.op(eng or dve, lambda e: e.tensor_tensor(out=out, in0=in0, in1=in1, op=op), reads=reads, writes=writes)

        def v3(ap, a):
            return ap.rearrange("p (a b) -> p a b", a=a)

        k.op(dve, lambda e: e.memset(identf[:], 1.0), writes=[identf])
        k.op(dve, lambda e: e.memset(Um[:], 1.0), writes=[Um])
        k.op(dve, lambda e: e.memset(Ls[:], 1.0), writes=[Ls])
        k.op(dve, lambda e: e.memset(ones[:], 1.0), writes=[ones])
        k.op(pool, lambda e: e.affine_select(out=identf[:], in_=identf[:], pattern=[[-1, 128]], compare_op=ALU.is_equal,
                                             fill=0.0, base=0, channel_multiplier=1), reads=[identf], writes=[identf])
        k.op(pool, lambda e: e.affine_select(out=Um[:], in_=Um[:], pattern=[[1, 128]], compare_op=ALU.is_ge,
                                             fill=0.0, base=0, channel_multiplier=-1), reads=[Um], writes=[Um])
        k.op(pool, lambda e: e.affine_select(out=Ls[:], in_=Ls[:], pattern=[[-1, 128]], compare_op=ALU.is_ge,
                                             fill=0.0, base=-1, channel_multiplier=1), reads=[Ls], writes=[Ls])
        k.op(dve, lambda e: e.tensor_copy(out=identb[:], in_=identf[:]), reads=[identf], writes=[identb])
        for c in range(4):
            k.dma(sp, resid[c][0:128, :], xp[c * 128:(c + 1) * 128, :], writes=[resid[c]])
        k.dma(sp, resid[4][0:16, :], xs[:, :], writes=[resid[4]])
        for l in range(4):
            for kk in range(4):
                k.dma(sp, convw[:, l, :, kk], conv_w[l, kk].rearrange("(cb p) -> p cb", p=128), writes=[convw], allow_slow_non_contiguous=True)
            k.dma(sp, convb[:, l], conv_b[l].rearrange("(cb p) -> p cb", p=128), writes=[convb], allow_slow_non_contiguous=True)
            k.dma(sp, gn[:, l], ssd_g[l].rearrange("(eb p) -> p eb", p=128), writes=[gn], allow_slow_non_contiguous=True)
        k.dma(sp, dtb[:], dt_bias.partition_broadcast(128), writes=[dtb])
        k.dma(sp, aall[:], a_log.partition_broadcast(128), writes=[aall])
        k.dma(sp, Dall[:], d_skip.partition_broadcast(128), writes=[Dall])
        for l in range(4):
            k.op(dve, lambda e: e.memset(stP[l][:], 0.0), writes=[stP[l]])
            k.op(dve, lambda e: e.memset(tailsP[l][:], 0.0), writes=[tailsP[l]])

        pieces = []
        sub_of = {}
        sub_counter = [0]
        plan = []

        def add_piece(src, a):
            pieces.append((src, a)); return len(pieces) - 1

        def kp(ap):
            return ap.rearrange("(kk p) n -> p kk n", p=128)

        for t in range(ntiles):
            for l in range(depth):
                d = {}
                d["u"] = [add_piece(kp(w_in[l][:, i * 512:(i + 1) * 512]), 8) for i in range(2)]
                d["v"] = [add_piece(kp(w_in[l][:, 1024 + i * 512:1024 + (i + 1) * 512]), 8) for i in range(2)]
                d["z"] = [add_piece(kp(w_in[l][:, 2048 + i * 512:2048 + (i + 1) * 512]), 8) for i in range(2)]
                d["x"] = [add_piece(kp(w_in[l][:, 3072 + i * 512:3072 + (i + 1) * 512]), 8) for i in range(3)]
                d["o"] = [add_piece(kp(w_out[l][i * 512:(i + 1) * 512, :]), 4) for i in range(4)]
                for q in range(4):
                    d[f"f1_{q}"] = [add_piece(kp(w_ff1[l][:, q * 1024 + i * 512:q * 1024 + (i + 1) * 512]), 8) for i in range(2)]
                    d[f"f2_{q}"] = [add_piece(kp(w_ff2[l][q * 1024 + i * 512:q * 1024 + (i + 1) * 512, :]), 4) for i in range(2)]
                plan.append(d)
        issued = [0]
        done_upto = [-1]

        def pump():
            while issued[0] < len(pieces) and (issued[0] < NS or issued[0] - NS <= done_upto[0]):
                j = issued[0]
                src, a = pieces[j]
                s = slots[j % NS]
                k.dma(pool, s[:].rearrange("p (a b) -> p a b", a=a), src, writes=[s], q=wq[j % NS])
                issued[0] += 1

        def W(j, a):
            assert j < issued[0], "weight piece not issued before use"
            s = slots[j % NS]
            return s, s[:].rearrange("p (a b) -> p a b", a=a)

        def release(upto):
            done_upto[0] = max(done_upto[0], upto)
            pump()

        pump()
        for l in range(4):
            k.dma(pool, Wdt[:, l], w_in[l][:, 4608:4624].rearrange("(kk p) n -> p kk n", p=128), writes=[Wdt], q=pdq)

        def make_xT(ch):
            L, col = ch.L, ch.col
            r = resid[ch.rrow]
            actf(H1[0:L, :], r[0:L, :], AF.Copy, [r], [H1])
            p = pbank()
            pv = v3(p[:], 8)
            for kk in range(8):
                tr(pv[:, kk, 0:L], H1[0:L, kk * 128:(kk + 1) * 128], identb[0:L, 0:L], [H1, identb], [p])
            k.op(dve, lambda e: e.tensor_copy(out=xT[:, :, col:col + L], in_=pv[:, :, 0:L]), reads=[p], writes=[xT_c[ch.rrow]])

        def make_xT_b(ch):
            L, col = ch.L, ch.col
            p = pbank()
            pv = v3(p[:], 8)
            for kk in range(8):
                tr(pv[:, kk, 0:L], xw[0:L, kk * 128:(kk + 1) * 128], identb[0:L, 0:L], [xw, identb], [p])
            k.op(dve, lambda e: e.tensor_copy(out=xT[:, :, col:col + L], in_=pv[:, :, 0:L]), reads=[p], writes=[xT_c[ch.rrow]])

        def layer_norm(ch, grow, brow):
            L = ch.L
            r = resid[ch.rrow]
            for i in range(2):
                k.op(dve, lambda e: e.bn_stats(out=bst[0:L, i, :], in_=r[0:L, i * 512:(i + 1) * 512]), reads=[r], writes=[bst])
            k.op(dve, lambda e: e.bn_aggr(out=mv[0:L, :], in_=bst[0:L, :, :]), reads=[bst], writes=[mv])
            actf(rs[0:L, 0:1], mv[0:L, 1:2], AF.Ln, [mv], [rs], bias=1e-5)
            actf(rs[0:L, 0:1], rs[0:L, 0:1], AF.Exp, [rs], [rs], scale=-0.5)
            k.op(dve, lambda e: e.tensor_scalar(out=r[0:L, :], in0=r[0:L, :], scalar1=mv[0:L, 0:1], scalar2=rs[0:L, 0:1],
                                                op0=ALU.subtract, op1=ALU.mult), reads=[r, mv, rs], writes=[r])
            tt(r[0:L, :], r[0:L, :], grow[0:L, :], ALU.mult, [r, rowc], [r])
            tt(r[0:L, :], r[0:L, :], brow[0:L, :], ALU.add, [r, rowc], [r])

        def load_rows(g_ap, b_ap):
            k.dma(sp, rowc[:, 0, :], g_ap.partition_broadcast(128), writes=[rowc])
            k.dma(sp, rowc[:, 1, :], b_ap.partition_broadcast(128), writes=[rowc])

        def layer_consts_a(l):
            k.dma(sp, v3(F1[:], 8), gws[l].rearrange("h t s -> t h s"), writes=[F1])
            actf(H1[:, :], F1[:, :], AF.Copy, [F1], [H1])
            k.dma(sp, bsb[:].rearrange("p a b -> p (a b)"), gbs[l:l + 1, :].partition_broadcast(128), writes=[bsb])
            for cb in range(12):
                for kk in range(4):
                    k.op(dve, lambda e: e.tensor_scalar(out=diagw[:, cb * 4 + kk, :], in0=identb[:, :], scalar1=convw[:, l, cb, kk:kk + 1],
                                                         scalar2=None, op0=ALU.mult), reads=[identb, convw], writes=[diagw])

        def layer_consts_b(l):
            p = pbank(); pv = v3(p[:], 8)
            for h in range(8):
                tr(pv[:, h, :], H1[:, h * 128:(h + 1) * 128], identb[:, :], [H1, identb], [p])
            tt(WsT[:], pv, Um[:].unsqueeze(1).to_broadcast([128, 8, 128]), ALU.mult, [p, Um], [WsT])

        def run_pipeline(gens):
            pending = [tuple(g) + (False,) * (3 - len(g)) for g in gens]; active = []
            while pending or active:
                if pending and (not active or active[-1][1] >= active[-1][2]):
                    g, lag, of = pending.pop(0)
                    active.append([g, 0, lag, of])
                order = [a for a in reversed(active) if not a[3]] + [a for a in active if a[3]]
                for a in order:
                    try:
                        next(a[0]); a[1] += 1
                    except StopIteration:
                        active.remove(a)

        par_ctr = [0]

        def A1_group(t, l, d):
            load_rows(gln_g[l:l + 1, :], gln_b[l:l + 1, :])
            colr = [(0, 512, [0, 1, 2, 3]), (512, 16, [4])] if t == 0 else [(0, 384, [0, 1, 2]), (384, 128, [3])]
            for ub in range(8):
                s_, sv = W(d["u"][ub // 4], 8)
                for (c0, n, cl) in colr:
                    b = bank()
                    for kk in range(8):
                        mm(b[:, 0:n], sv[:, kk, (ub % 4) * 128:(ub % 4 + 1) * 128], xT[:, kk, c0:c0 + n], kk == 0, kk == 7,
                           [s_] + [xT_c[c] for c in cl], [b])
                    actf(hT[:, ub, c0:c0 + n], b[:, 0:n], AF.Gelu_apprx_tanh, [b], [hT])
            release(d["u"][1])

        def A1_gen(t, l, ch, d, is_last):
            L, col = ch.L, ch.col
            par = par_ctr[0] % 2; par_ctr[0] += 1
            F1 = F1s[par]
            mx = mixT_c[ch.rrow]
            for nb in range(2):
                s_, sv = W(d["v"][nb], 8)
                b = bank()
                for kk in range(8):
                    mm(b[0:L, :], xT[:, kk, col:col + L], sv[:, kk, :], kk == 0, kk == 7, [s_, xT_c[ch.rrow]], [b])
                actf(F1[0:L, nb * 512:(nb + 1) * 512], b[0:L, :], AF.Gelu_apprx_tanh, [b], [F1])
            if is_last:
                release(d["v"][1])
            yield
            for i in range(2):
                k.op(dve, lambda e: e.bn_stats(out=bst[0:L, i, :], in_=F1[0:L, i * 512:(i + 1) * 512]), reads=[F1], writes=[bst])
            k.op(dve, lambda e: e.bn_aggr(out=mv[0:L, :], in_=bst[0:L, :, :]), reads=[bst], writes=[mv])
            actf(rs[0:L, 0:1], mv[0:L, 1:2], AF.Ln, [mv], [rs], bias=1e-5)
            actf(rs[0:L, 0:1], rs[0:L, 0:1], AF.Exp, [rs], [rs], scale=-0.5)
            k.op(dve, lambda e: e.tensor_scalar(out=F1[0:L, :], in0=F1[0:L, :], scalar1=mv[0:L, 0:1], scalar2=rs[0:L, 0:1],
                                                op0=ALU.subtract, op1=ALU.mult), reads=[F1, mv, rs], writes=[F1])
            tt(F1[0:L, :], F1[0:L, :], rowc[0:L, 0, :], ALU.mult, [F1, rowc], [F1])
            if ch.seq == "s":
                tt(F1[0:L, :], F1[0:L, :], rowc[0:L, 1, :], ALU.add, [F1, rowc], [F1])
                k.dma(sp, v_s[l], F1[0:L, :], reads=[F1])
                k.op(dve, lambda e: e.tensor_copy(out=H1[0:L, :], in_=F1[0:L, :]), reads=[F1], writes=[H1])
            else:
                tt(H1[0:L, :], F1[0:L, :], rowc[0:L, 1, :], ALU.add, [F1, rowc], [H1])
            yield
            bs2 = [bank(), bank()]
            for h in range(8):
                b = bs2[h // 4]
                mm(v3(b[:], 4)[:, h % 4, 0:L], H1[0:L, h * 128:(h + 1) * 128], WsT[0:L, h, 0:L], True, True, [H1, WsT], [b])
            for hb in range(2):
                b = bs2[hb]
                tt(tmpS[:, :, 0:L], v3(b[:], 4)[:, :, 0:L], bsb[:, 4 * hb:4 * hb + 4, 0:L], ALU.add, [b, bsb], [tmpS])
                tt(mixT[:, 4 * hb:4 * hb + 4, col:col + L], tmpS[:, :, 0:L], hT[:, 4 * hb:4 * hb + 4, col:col + L], ALU.mult,
                   [tmpS, hT], [mx])
            yield

        def A2F_gen(t, l, ch, d, is_last, par):
            L, col = ch.L, ch.col
            F1 = F1s[par]; expa = expas[par]; xcT = xcTs[par]; dtv = dtvs[par]; da = das[par]; bdec = bdecs[par]
            mx = mixT_c[ch.rrow]
            samp = ch.seq == "s"
            state = stS if samp else stP[l]
            tails = tailsS if samp else tailsP[l]
            has_state = samp or not ch.first
            if samp:
                stf = F1[:]
                k.dma(sp, v3(stf[:, 0:1024], 8), sssm[l].rearrange("(blk p) n -> p blk n", p=128), writes=[F1])
                for half in range(2):
                    b = bank()
                    for j in range(4):
                        blk = half * 4 + j
                        tr(b[:, j * 128:(j + 1) * 128], stf[:, blk * 128:(blk + 1) * 128], identf[:, :], [F1, identf], [b])
                    actf(stS[:, half * 512:(half + 1) * 512], b[:, :], AF.Copy, [b], [stS])
                for kk in range(3):
                    k.dma(sp, tailsS[:, :, kk], sconv[l, kk].rearrange("(cb p) -> p cb", p=128), writes=[tailsS], allow_slow_non_contiguous=True)
            bd = bank()
            for kk in range(8):
                mm(bd[0:L, 0:16], xT[:, kk, col:col + L], Wdt[:, l, kk, :], kk == 0, kk == 7, [xT_c[ch.rrow], Wdt], [bd])
            tt(dtv[0:L, :], bd[0:L, 0:16], dtb[0:L, l * 16:(l + 1) * 16], ALU.add, [bd, dtb], [dtv])
            actf(dtv[0:L, :], dtv[0:L, :], AF.Exp, [dtv], [dtv])
            actf(dtv[0:L, :], dtv[0:L, :], AF.Ln, [dtv], [dtv], bias=1.0)
            tt(da[0:L, :], dtv[0:L, :], aall[0:L, l * 16:(l + 1) * 16], ALU.mult, [dtv, aall], [da])
            mm(bd[0:L, 32:48], Um[0:L, 0:L], da[0:L, :], True, True, [Um, da], [bd])
            mm(bd[:, 64:80], ones[0:L, :], da[0:L, :], True, True, [ones, da], [bd])
            actf(expa[0:L, :], bd[0:L, 32:48], AF.Exp, [bd], [expa])
            actf(bdec[:, :], bd[:, 64:80], AF.Exp, [bd], [bdec])
            yield
            for nb in range(2):
                s_, sv = W(d["z"][nb], 8)
                b = bank()
                for kk in range(8):
                    mm(b[0:L, :], xT[:, kk, col:col + L], sv[:, kk, :], kk == 0, kk == 7, [s_, xT_c[ch.rrow]], [b])
                actf(F1[0:L, nb * 512:(nb + 1) * 512], b[0:L, :], AF.Silu, [b], [F1])
            yield
            xb_banks = [bank(), bank(), bank()]
            for cb in range(12):
                s_, sv = W(d["x"][cb // 4], 8)
                b = xb_banks[cb // 4]
                for kk in range(8):
                    mm(v3(b[:], 4)[:, cb % 4, 0:L], sv[:, kk, (cb % 4) * 128:(cb % 4 + 1) * 128], xT[:, kk, col:col + L],
                       kk == 0, kk == 7, [s_, xT_c[ch.rrow]], [b])
            if is_last:
                release(d["x"][2])
            k.op(dve, lambda e: e.tensor_copy(out=st_t[:, :, 0:3], in_=tails[:, :, :]), reads=[tails], writes=[st_t])
            for j in range(3):
                b = xb_banks[j]
                k.op(dve, lambda e: e.tensor_copy(out=st_t[:, 4 * j:4 * j + 4, 3:3 + L], in_=v3(b[:], 4)[:, :, 0:L]), reads=[b], writes=[st_t])
                k.op(dve, lambda e: e.tensor_copy(out=tails[:, 4 * j:4 * j + 4, :], in_=v3(b[:], 4)[:, :, L - 3:L]), reads=[b, st_t], writes=[tails])
            yield
            cv_banks = [bank(), bank(), bank()]
            for cb in range(12):
                b = cv_banks[cb // 4]
                for kk in range(4):
                    mm(v3(b[:], 4)[:, cb % 4, 0:L], diagw[:, cb * 4 + kk, :], st_t[:, cb, kk:kk + L], kk == 0, kk == 3, [diagw, st_t], [b])
            for cb in range(12):
                b = cv_banks[cb // 4]
                actf(xcT[:, cb, 0:L], v3(b[:], 4)[:, cb % 4, 0:L], AF.Silu, [b, convb], [xcT], bias=convb[:, l, cb:cb + 1])
            if ch.last:
                for j in range(3):
                    b = bank()
                    for i in range(4):
                        tr(b[0:3, i * 128:(i + 1) * 128], tails[:, 4 * j + i, :], identf[:, :], [tails, identf], [b])
                    rt = relu_tmp[j]
                    rtv = rt[:].rearrange("p a b -> p (a b)")
                    actf(rtv[0:3, :], b[0:3, :], AF.Copy, [b], [rt])
                    k.dma(sp, (conv_s if samp else conv_p)[l][:, j * 512:(j + 1) * 512], rtv[0:3, :], reads=[rt])
            yield

        def A2B_gen(t, l, ch, d, is_last, par):
            L, col = ch.L, ch.col
            F1 = F1s[par]; expa = expas[par]; xcT = xcTs[par]; dtv = dtvs[par]; da = das[par]; bdec = bdecs[par]
            mx = mixT_c[ch.rrow]
            samp = ch.seq == "s"
            state = stS if samp else stP[l]
            tails = tailsS if samp else tailsP[l]
            has_state = samp or not ch.first
            stateb = statebs[par]
            if samp or (has_state and ch.rrow == 0):
                actf(stateb[:, :], state[:, :], AF.Copy, [state], [stateb])
            if has_state:
                tt(v3(state[:, :], 16), v3(state[:, :], 16), bdec[:, :].unsqueeze(2).to_broadcast([128, 16, 64]), ALU.mult,
                   [state, bdec], [state], eng=pool)
            pA = pbank(); pAv = v3(pA[:], 8)
            for hb in range(8):
                tr(pAv[0:L, hb, :], xcT[:, hb, 0:L], identb[:, :], [xcT, identb], [pA])
            actf(xt[0:L, :], pA[0:L, :], AF.Copy, [pA], [xt])
            tt(v3(xdt[0:L, :], 16), v3(xt[0:L, :], 16), dtv[0:L, :].unsqueeze(2).to_broadcast([L, 16, 64]), ALU.mult, [xt, dtv], [xdt])
            tt(v3(F3[0:L, :], 16), v3(xt[0:L, :], 16), Dall[0:L, l * 16:(l + 1) * 16].unsqueeze(2).to_broadcast([L, 16, 64]),
               ALU.mult, [xt, Dall], [F3], eng=pool)
            pB = pbank()
            for g in range(2):
                tr(pB[0:L, g * 128:(g + 1) * 128], xcT[:, 8 + g, 0:L], identb[:, :], [xcT, identb], [pB])
            actf(Bt[0:L, :], pB[0:L, 0:256], AF.Copy, [pB], [Bt])
            bc = bank()
            for g in range(2):
                mm(v3(bc[:], 4)[0:L, g, 0:L], xcT[:, 8 + g, 0:L], xcT[:, 10 + g, 0:L], True, True, [xcT], [bc])
            tt(cbm[0:L, :, 0:L], v3(bc[:], 4)[0:L, 0:2, 0:L], Um[0:L, 0:L].unsqueeze(1).to_broadcast([L, 2, L]), ALU.mult, [bc, Um], [cbm])
            yield
            for hq in range(4):
                ru = rhsU[hq % 2]
                tt(ru[0:L, :, 0:L], Um[0:L, 0:L].unsqueeze(1).to_broadcast([L, 4, L]),
                   da[0:L, hq * 4:hq * 4 + 4].unsqueeze(2).to_broadcast([L, 4, L]), ALU.mult, [Um, da], [ru])
                b = bank()
                if L == 128:
                    mm(v3(b[:], 4)[0:L, :, 0:L], Ls[0:L, 0:L], ru[0:L, :, 0:L], True, True, [Ls, ru], [b])
                else:
                    for hh in range(4):
                        mm(v3(b[:], 4)[0:L, hh, 0:L], Ls[0:L, 0:L], ru[0:L, hh, 0:L], True, True, [Ls, ru], [b])
                actf(dec[0:L, hq * 4:hq * 4 + 4, 0:L], v3(b[:], 4)[0:L, :, 0:L], AF.Exp, [b], [dec])
            tt(wcol[0:L, :].unsqueeze(2), dtv[0:L, :].unsqueeze(2), dec[0:L, :, L - 1:L], ALU.mult, [dtv, dec], [wcol])
            tt(v3(xw[0:L, :], 16), v3(xt[0:L, :], 16), wcol[0:L, :].unsqueeze(2).to_broadcast([L, 16, 64]), ALU.mult, [xt, wcol], [xw])
            yield
            for g in range(2):
                tt(dec[0:L, g * 8:(g + 1) * 8, 0:L], dec[0:L, g * 8:(g + 1) * 8, 0:L],
                   cbm[0:L, g:g + 1, 0:L].to_broadcast([L, 8, L]), ALU.mult, [dec, cbm], [dec])
            bS = [bank(), bank()]
            for g in range(2):
                mm(bS[g][:, :], Bt[0:L, g * 128:(g + 1) * 128], xw[0:L, g * 512:(g + 1) * 512], True, True, [Bt, xw], [bS[g]])
            if has_state:
                for g in range(2):
                    tt(state[:, g * 512:(g + 1) * 512], state[:, g * 512:(g + 1) * 512], bS[g][:, :], ALU.add, [state, bS[g]], [state])
            else:
                for g in range(2):
                    actf(state[:, g * 512:(g + 1) * 512], bS[g][:, :], AF.Copy, [bS[g]], [state])
            if not samp and ch.rrow < 3:
                actf(statebs[1 - par][:, :], state[:, :], AF.Copy, [state], [statebs[1 - par]])
            yield
            bY = [bank(), bank()]
            for h in range(16):
                b = bY[h // 8]
                mm(b[0:L, (h % 8) * 64:(h % 8 + 1) * 64], dec[0:L, h, 0:L], xdt[0:L, h * 64:(h + 1) * 64], True, True, [dec, xdt], [b])
            if has_state:
                bO = [bank(), bank()]
                for g in range(2):
                    mm(bO[g][0:L, :], xcT[:, 10 + g, 0:L], stateb[:, g * 512:(g + 1) * 512], True, True, [xcT, stateb], [bO[g]])
                for g in range(2):
                    tt(v3(F2[0:L, g * 512:(g + 1) * 512], 8), v3(bO[g][0:L, :], 8),
                       expa[0:L, g * 8:(g + 1) * 8].unsqueeze(2).to_broadcast([L, 8, 64]), ALU.mult, [bO[g], expa], [F2])
                    tt(F2[0:L, g * 512:(g + 1) * 512], F2[0:L, g * 512:(g + 1) * 512], bY[g][0:L, :], ALU.add, [F2, bY[g]], [F2])
            else:
                for g in range(2):
                    actf(F2[0:L, g * 512:(g + 1) * 512], bY[g][0:L, :], AF.Copy, [bY[g]], [F2])
            tt(F2[0:L, :], F2[0:L, :], F3[0:L, :], ALU.add, [F2, F3], [F2])
            tt(F2[0:L, :], F2[0:L, :], F1[0:L, :], ALU.mult, [F2, F1], [F2])
            yield
            k.op(dve, lambda e: e.memset(ssq[:], 0.0), writes=[ssq])
            actf(rs[0:1, 0:1], ones[0:1, 0:1], AF.Exp, [ones], [rs])
            for g in range(2):
                actf(H1[0:L, g * 512:(g + 1) * 512], F2[0:L, g * 512:(g + 1) * 512], AF.Square, [F2, ssq], [H1, ssq],
                     accum_out=ssq[0:L, g:g + 1])
            actf(rs[0:L, :], ssq[0:L, :], AF.Ln, [ssq], [rs], scale=1.0 / 512.0, bias=1e-5)
            actf(rs[0:L, :], rs[0:L, :], AF.Exp, [rs], [rs], scale=-0.5)
            for g in range(2):
                k.op(dve, lambda e: e.tensor_scalar(out=H1[0:L, g * 512:(g + 1) * 512], in0=F2[0:L, g * 512:(g + 1) * 512],
                                                    scalar1=rs[0:L, g:g + 1], scalar2=None, op0=ALU.mult), reads=[F2, rs], writes=[H1])
            pC = pbank(); pCv = v3(pC[:], 8)
            for eb in range(8):
                tr(pCv[:, eb, 0:L], H1[0:L, eb * 128:(eb + 1) * 128], identb[0:L, 0:L], [H1, identb], [pC])
            k.op(dve, lambda e: e.tensor_copy(out=mixT[:, 8:16, col:col + L], in_=pCv[:, :, 0:L]), reads=[pC], writes=[mx])
            if ch.last:
                for half in range(2):
                    b = bank()
                    for j in range(4):
                        blk = half * 4 + j
                        tr(b[:, j * 128:(j + 1) * 128], state[:, blk * 128:(blk + 1) * 128], identf[:, :], [state, identf], [b])
                    actf(F3[:, half * 512:(half + 1) * 512], b[:, :], AF.Copy, [b], [F3])
                k.dma(sp, (ssm_s if samp else ssm_p)[l].rearrange("(blk p) n -> p blk n", p=128), v3(F3[:], 8), reads=[F3])
            yield

        def A3_gen(t, l, ch, d, is_first, is_last):
            L, col = ch.L, ch.col
            r = resid[ch.rrow]
            mx = mixT_c[ch.rrow]
            if is_first:
                load_rows(ln1_g[l:l + 1, :], ln1_b[l:l + 1, :])
                for pi in (2, 3):
                    s_, sv = W(d["o"][pi], 4)
                    for j in range(4):
                        eb = (pi - 2) * 4 + j
                        k.op(dve, lambda e: e.tensor_scalar(out=sv[:, j, :], in0=sv[:, j, :], scalar1=gn[:, l, eb:eb + 1], scalar2=None,
                                                            op0=ALU.mult), reads=[s_, gn], writes=[s_])
            for nb in range(2):
                b = bank()
                for kk in range(16):
                    s_, sv = W(d["o"][kk // 4], 4)
                    mm(b[0:L, :], mixT[:, kk, col:col + L], sv[:, kk % 4, nb * 512:(nb + 1) * 512], kk == 0, kk == 15, [s_, mx], [b])
                k.op(dve, lambda e: e.scalar_tensor_tensor(out=r[0:L, nb * 512:(nb + 1) * 512], in0=r[0:L, nb * 512:(nb + 1) * 512],
                                                           scalar=float(ALPHA), in1=b[0:L, :], op0=ALU.mult, op1=ALU.add),
                     reads=[r, b], writes=[r])
            if is_last:
                release(d["o"][3])
            yield
            layer_norm(ch, rowc[:, 0, :], rowc[:, 1, :])
            actf(xw[0:L, :], r[0:L, :], AF.Copy, [r], [xw])
            yield
            make_xT_b(ch)
            yield

        def B_group(t, l, q, d):
            if q == 3:
                load_rows(ln2_g[l:l + 1, :], ln2_b[l:l + 1, :])
            if t == 0:
                colr = [(0, 512, [0, 1, 2, 3]), (512, 16, [4])]
            elif q == 0:
                colr = [(0, 384, [0, 1, 2]), (384, 128, [3])]
            else:
                colr = [(0, 512, [0, 1, 2, 3])]
            for fb in range(8):
                s_, sv = W(d[f"f1_{q}"][fb // 4], 8)
                for (c0, n, cl) in colr:
                    b = bank()
                    for kk in range(8):
                        mm(b[:, 0:n], sv[:, kk, (fb % 4) * 128:(fb % 4 + 1) * 128], xT[:, kk, c0:c0 + n], kk == 0, kk == 7,
                           [s_] + [xT_c[c] for c in cl], [b])
                    rt = relu_tmp[relu_i[0] % 3]; relu_i[0] += 1
                    rtv = rt[:].rearrange("p a b -> p (a b)")
                    actf(rtv[:, 0:n], b[:, 0:n], AF.Relu, [b], [rt])
                    tt(hT[:, fb, c0:c0 + n], rtv[:, 0:n], rtv[:, 0:n], ALU.mult, [rt], [hT])
            release(d[f"f1_{q}"][1])

        def B_gen(t, l, q, ch, d, is_last):
            L, col = ch.L, ch.col
            r = resid[ch.rrow]
            for nb in range(2):
                b = bank()
                for fc in range(8):
                    s_, sv = W(d[f"f2_{q}"][fc // 4], 4)
                    mm(b[0:L, :], hT[:, fc, col:col + L], sv[:, fc % 4, nb * 512:(nb + 1) * 512], fc == 0, fc == 7, [s_, hT], [b])
                rr = r[0:L, nb * 512:(nb + 1) * 512]
                if q == 0:
                    k.op(dve, lambda e: e.scalar_tensor_tensor(out=rr, in0=rr, scalar=float(ALPHA), in1=b[0:L, :],
                                                               op0=ALU.mult, op1=ALU.add), reads=[r, b], writes=[r])
                else:
                    tt(rr, rr, b[0:L, :], ALU.add, [r, b], [r])
            if is_last:
                release(d[f"f2_{q}"][1])
            yield
            if q == 3:
                layer_norm(ch, rowc[:, 0, :], rowc[:, 1, :])
                if l != depth - 1:
                    actf(xw[0:L, :], r[0:L, :], AF.Copy, [r], [xw])
                yield
                if l == depth - 1:
                    if ch.seq == "s":
                        k.dma(sp, ys[:, :], r[0:L, :], reads=[r])
                    else:
                        k.dma(sp, yp[ch.tok0:ch.tok0 + L, :], r[0:L, :], reads=[r])
                else:
                    make_xT_b(ch)
                yield

        def chk(tag):
            if stop == tag:
                raise _Stop()
        try:
          for t in range(ntiles):
              chunks = [Chunk(128, c * 128, c, "p", t == 0 and c == 0, t == ntiles - 1 and c == 3, t * 512 + c * 128) for c in range(4)]
              if t == 0:
                  chunks.append(Chunk(16, 512, 4, "s", True, True, 0))
              for ch in chunks:
                  if t > 0:
                      k.dma(sp, resid[ch.rrow][0:ch.L, :], xp[ch.tok0:ch.tok0 + ch.L, :], writes=[resid[ch.rrow]])
                  make_xT(ch)
              n = len(chunks)
              if t == 0:
                  actf(aall[:], aall[:], AF.Exp, [aall], [aall])
                  k.op(dve, lambda e: e.tensor_scalar(out=aall[:], in0=aall[:], scalar1=-1.0, scalar2=None, op0=ALU.mult),
                       reads=[aall], writes=[aall])
              for l in range(depth):
                  d = plan[t * depth + l]
                  layer_consts_a(l)
                  A1_group(t, l, d)
                  layer_consts_b(l)
                  gens = []
                  gens += [(A1_gen(t, l, ch, d, i == n - 1), 1, True) for i, ch in enumerate(chunks)]
                  for i, ch in enumerate(chunks):
                      gens.append((A2F_gen(t, l, ch, d, i == n - 1, i % 2), 4))
                      gens.append((A2B_gen(t, l, ch, d, i == n - 1, i % 2), 0))
                  gens += [(A3_gen(t, l, ch, d, i == 0, i == n - 1), 1, True) for i, ch in enumerate(chunks)]
                  run_pipeline(gens)
                  chk("A3")
                  for q in range(4):
                      B_group(t, l, q, d)
                      run_pipeline([(B_gen(t, l, q, ch, d, i == n - 1), 1, True) for i, ch in enumerate(chunks)])
        except _Stop:
            pass
        k.finish(extra=wq + [pdq])
    return nc


_NC_CACHE = {}


def kernel(x_prompt, x_sample, state_ssm, state_conv, w_in, gmlp_ln_g, gmlp_ln_b, gmlp_ws, gmlp_bs,
           conv_w, conv_b, dt_bias, a_log, d_skip, ssd_norm_g, w_out, ln1_g, ln1_b, w_ff1, w_ff2,
           ln2_g, ln2_b):
    f = lambda a: np.ascontiguousarray(np.asarray(a, dtype=np.float32))
    if "nc" not in _NC_CACHE:
        _NC_CACHE["nc"] = build_program()
    nc = _NC_CACHE["nc"]
    shared = {
        "w_in": f(w_in), "gmlp_ln_g": f(gmlp_ln_g), "gmlp_ln_b": f(gmlp_ln_b), "gmlp_ws": f(gmlp_ws),
        "gmlp_bs": f(gmlp_bs).reshape(4, 1024), "conv_w": f(conv_w), "conv_b": f(conv_b),
        "dt_bias": f(dt_bias).reshape(1, 64), "a_log": f(a_log).reshape(1, 64), "d_skip": f(d_skip).reshape(1, 64),
        "ssd_norm_g": f(ssd_norm_g), "w_out": f(w_out), "ln1_g": f(ln1_g), "ln1_b": f(ln1_b),
        "w_ff1": f(w_ff1), "w_ff2": f(w_ff2), "ln2_g": f(ln2_g), "ln2_b": f(ln2_b),
    }
    xp = f(x_prompt); xs = f(x_sample); ss = f(state_ssm); sc = f(state_conv)
    in_maps = []
    for b in range(8):
        m = dict(shared)
        m["xp"] = xp[b]; m["xs"] = xs[b]
        m["sssm"] = np.ascontiguousarray(ss[:, b].reshape(4, 1024, 128))
        m["sconv"] = np.ascontiguousarray(sc[:, b])
        in_maps.append(m)
    res = run_bass_kernel_spmd(nc, in_maps, core_ids=list(range(8)))
    R = res.results
    y_prompt = np.stack([R[b]["yp"] for b in range(8)], axis=0)
    y_sample = np.stack([R[b]["ys"] for b in range(8)], axis=0)
    ssm_p = np.stack([R[b]["ssm_p"].reshape(4, 16, 64, 128) for b in range(8)], axis=1)
    conv_p = np.stack([R[b]["conv_p"] for b in range(8)], axis=1)
    ssm_s = np.stack([R[b]["ssm_s"].reshape(4, 16, 64, 128) for b in range(8)], axis=1)
    conv_s = np.stack([R[b]["conv_s"] for b in range(8)], axis=1)
    v_s = np.stack([R[b]["v_s"] for b in range(8)], axis=1)
    return (y_prompt, y_sample, ssm_p, conv_p, ssm_s, conv_s, v_s)
```

```python
import contextlib
import numpy as np
import concourse.bass as bass
import concourse.mybir as mybir
from concourse.bass_utils import run_bass_kernel_spmd

F32 = mybir.dt.float32
BF16 = mybir.dt.bfloat16
AF = mybir.ActivationFunctionType
ALU = mybir.AluOpType

DEPTH = 4
NTILES = 4
ALPHA = (2 * DEPTH) ** 0.25
NS = 7


class Eng:
    def __init__(self, name, h, sem, step=1, is_pe=False):
        self.name = name; self.h = h; self.sem = sem; self.step = step
        self.count = 0; self.is_pe = is_pe; self.waited = {}


class Buf:
    def __init__(self, t=None, name=""):
        self.t = t; self.name = name; self.w = None; self.r = {}

    def __getitem__(self, k):
        return self.t[k]


class AliasBuf(Buf):
    def __init__(self, base, view, name=""):
        self.base = base; self.view = view; self.name = name

    def __getitem__(self, k):
        return self.view[k]

    @property
    def w(self):
        return self.base.w

    @w.setter
    def w(self, v):
        self.base.w = v

    @property
    def r(self):
        return self.base.r

    @r.setter
    def r(self, v):
        self.base.r = v


class K:
    def __init__(self, nc, es):
        self.nc = nc; self.es = es
        self.pe = Eng("pe", nc.tensor, self.sem("s_pe"), is_pe=True)
        self.act = Eng("act", nc.scalar, self.sem("s_act"))
        self.dve = Eng("dve", nc.vector, self.sem("s_dve"))
        self.pool = Eng("pool", nc.gpsimd, self.sem("s_pool"))
        self.sp = Eng("sp", nc.sync, self.sem("s_sp"))
        self.dq = [Eng(f"dq{i}", None, self.sem(f"s_dq{i}"), step=16) for i in range(16)]
        self.dq_i = 0
        self.nbuf = 0

    def sem(self, n):
        return self.es.enter_context(self.nc.semaphore(n))

    def sb(self, shape, dt, name=None):
        self.nbuf += 1
        name = name or f"b{self.nbuf}"
        return Buf(self.es.enter_context(self.nc.sbuf_tensor(name, list(shape), dt)), name)

    def ps(self, shape, dt, name=None):
        self.nbuf += 1
        name = name or f"p{self.nbuf}"
        return Buf(self.es.enter_context(self.nc.psum_tensor(name, list(shape), dt)), name)

    def _wait(self, eng, e2, ts):
        if e2 is eng and eng.is_pe:
            return
        if eng.waited.get(e2.name, 0) >= ts:
            return
        eng.h.wait_ge(e2.sem, ts)
        eng.waited[e2.name] = ts

    def _deps(self, eng, reads, writes):
        deps = {}

        def add(e2, ts):
            if deps.get(e2.name, (None, 0))[1] < ts:
                deps[e2.name] = (e2, ts)
        for b in reads:
            if b.w: add(*b.w)
        for b in writes:
            if b.w: add(*b.w)
            for e2, ts in b.r.values(): add(e2, ts)
        for e2, ts in deps.values():
            self._wait(eng, e2, ts)

    def op(self, eng, fn, reads=(), writes=()):
        self._deps(eng, reads, writes)
        ins = fn(eng.h)
        eng.count += 1
        ins.then_inc(eng.sem, 1)
        ts = eng.count
        for b in reads: b.r[eng.name] = (eng, ts)
        for b in writes:
            b.w = (eng, ts); b.r = {}
        return ins

    def dma(self, issuer, out, in_, reads=(), writes=(), q=None, **kw):
        if q is None:
            q = self.dq[self.dq_i]; self.dq_i = (self.dq_i + 1) % len(self.dq)
        if q.count > 0:
            self._wait(issuer, q, q.count * 16)
        self._deps(issuer, reads, writes)
        ins = issuer.h.dma_start(out=out, in_=in_, **kw)
        q.count += 1
        ins.then_inc(q.sem, 16)
        ts = q.count * 16
        for b in reads: b.r[q.name] = (q, ts)
        for b in writes:
            b.w = (q, ts); b.r = {}

    def finish(self, extra=()):
        for q in list(self.dq) + list(extra):
            if q.count: self._wait(self.sp, q, q.count * 16)


class Chunk:
    def __init__(self, L, col, rrow, seq, first, last, tok0):
        self.L = L; self.col = col; self.rrow = rrow; self.seq = seq
        self.first = first; self.last = last; self.tok0 = tok0


class _Stop(Exception):
    pass


def build_program(ntiles=NTILES, depth=DEPTH, stop=None):
    nc = bass.Bass("TRN2", target_bir_lowering=False)

    def din(name, shape):
        return nc.dram_tensor(name, list(shape), F32, kind="ExternalInput").ap()

    def dout(name, shape):
        return nc.dram_tensor(name, list(shape), F32, kind="ExternalOutput").ap()

    xp = din("xp", [2048, 1024]); xs = din("xs", [16, 1024])
    sssm = din("sssm", [4, 1024, 128]); sconv = din("sconv", [4, 3, 1536])
    w_in = din("w_in", [4, 1024, 4624])
    gln_g = din("gmlp_ln_g", [4, 1024]); gln_b = din("gmlp_ln_b", [4, 1024])
    gws = din("gmlp_ws", [4, 8, 128, 128]); gbs = din("gmlp_bs", [4, 1024])
    conv_w = din("conv_w", [4, 4, 1536]); conv_b = din("conv_b", [4, 1536])
    dt_bias = din("dt_bias", [1, 64]); a_log = din("a_log", [1, 64]); d_skip = din("d_skip", [1, 64])
    ssd_g = din("ssd_norm_g", [4, 1024])
    w_out = din("w_out", [4, 2048, 1024])
    ln1_g = din("ln1_g", [4, 1024]); ln1_b = din("ln1_b", [4, 1024])
    w_ff1 = din("w_ff1", [4, 1024, 4096]); w_ff2 = din("w_ff2", [4, 4096, 1024])
    ln2_g = din("ln2_g", [4, 1024]); ln2_b = din("ln2_b", [4, 1024])

    yp = dout("yp", [2048, 1024]); ys = dout("ys", [16, 1024])
    ssm_p = dout("ssm_p", [4, 1024, 128]); conv_p = dout("conv_p", [4, 3, 1536])
    ssm_s = dout("ssm_s", [4, 1024, 128]); conv_s = dout("conv_s", [4, 3, 1536])
    v_s = dout("v_s", [4, 16, 1024])

    with contextlib.ExitStack() as es:
        k = K(nc, es)
        pe, act, dve, pool, sp = k.pe, k.act, k.dve, k.pool, k.sp
        wq = [Eng(f"wq{i}", None, k.sem(f"s_wq{i}"), step=16) for i in range(NS)]
        pdq = Eng("pdq", None, k.sem("s_pdq"), step=16)

        resid = [k.sb([128, 1024], F32, f"resid{i}") for i in range(5)]
        xT = k.sb([128, 8, 528], BF16, "xT")
        mixT = k.sb([128, 16, 528], BF16, "mixT")
        mixT_c = [Buf(mixT.t, f"mixT_c{i}") for i in range(5)]
        xT_c = [Buf(xT.t, f"xT_c{i}") for i in range(5)]
        hT = k.sb([128, 8, 528], BF16, "hT")
        slots = [k.sb([128, 4096], BF16, f"slot{i}") for i in range(NS)]
        stP = [k.sb([128, 1024], F32, f"stP{i}") for i in range(4)]
        stS = k.sb([128, 1024], F32, "stS")
        statebs = [k.sb([128, 1024], BF16, "stateb0"), k.sb([128, 1024], BF16, "stateb1")]
        tailsP = [k.sb([128, 12, 3], F32, f"tailsP{i}") for i in range(4)]
        tailsS = k.sb([128, 12, 3], F32, "tailsS")
        rowc = k.sb([128, 2, 1024], F32, "rowc")
        bsb = k.sb([128, 8, 128], F32, "bsb")
        WsT = k.sb([128, 8, 128], BF16, "WsT")
        identb = k.sb([128, 128], BF16, "identb"); identf = k.sb([128, 128], F32, "identf")
        Um = k.sb([128, 128], F32, "Um"); Ls = k.sb([128, 128], F32, "Ls"); ones = k.sb([128, 128], F32, "ones")
        convw = k.sb([128, 4, 12, 4], F32, "convw"); convb = k.sb([128, 4, 12], F32, "convb")
        gn = k.sb([128, 4, 8], F32, "gn")
        dtb = k.sb([128, 64], F32, "dtb"); aall = k.sb([128, 64], F32, "aall"); Dall = k.sb([128, 64], F32, "Dall")
        Wdt = k.sb([128, 4, 8, 16], BF16, "Wdt")
        F1s = [k.sb([128, 1024], F32, "F1a"), k.sb([128, 1024], F32, "F1b")]; F1 = F1s[0]
        F2 = k.sb([128, 1024], F32, "F2"); F3 = k.sb([128, 1024], F32, "F3")
        H1 = k.sb([128, 1024], BF16, "H1")
        st_t = k.sb([128, 12, 132], BF16, "st")
        diagw = k.sb([128, 48, 128], BF16, "diagw")
        xcTs = [k.sb([128, 12, 128], BF16, "xcT0"), k.sb([128, 12, 128], BF16, "xcT1")]
        xt = F2; wcol = k.sb([128, 16], F32, "wcol"); xdt = k.sb([128, 1024], BF16, "xdt"); xw = k.sb([128, 1024], BF16, "xw")
        Bt = k.sb([128, 256], BF16, "Bt")
        rhsU = [k.sb([128, 4, 128], F32, f"rhsU{i}") for i in range(2)]
        dec = k.sb([128, 16, 128], BF16, "dec")
        cbm = k.sb([128, 2, 128], BF16, "cbm")
        tmpS = AliasBuf(F3, F3.t[:, 0:512].rearrange("p (a b) -> p a b", a=4), "tmpS")
        bst = k.sb([128, 2, 6], F32, "bst"); mv = k.sb([128, 2], F32, "mv"); rs = k.sb([128, 2], F32, "rs")
        dtvs = [k.sb([128, 16], F32, "dtv0"), k.sb([128, 16], F32, "dtv1")]; das = [k.sb([128, 16], F32, "da0"), k.sb([128, 16], F32, "da1")]
        expas = [k.sb([128, 16], F32, "expa0"), k.sb([128, 16], F32, "expa1")]; bdecs = [k.sb([128, 16], F32, "bdec0"), k.sb([128, 16], F32, "bdec1")]
        ssq = k.sb([128, 2], F32, "ssq")

        pf = [k.ps([128, 512], F32, f"pf{i}") for i in range(6)]
        pb = [k.ps([128, 1024], BF16, f"pb{i}") for i in range(2)]
        bank_i = [0]; pb_i = [0]
        relu_tmp = [tmpS, rhsU[0], rhsU[1]]; relu_i = [0]

        def bank():
            b = pf[bank_i[0]]; bank_i[0] = (bank_i[0] + 1) % len(pf); return b

        def pbank():
            b = pb[pb_i[0]]; pb_i[0] = (pb_i[0] + 1) % len(pb); return b

        def mm(out, lhsT, rhs, start, stop, reads, writes):
            k.op(pe, lambda e: e.matmul(out, lhsT=lhsT, rhs=rhs, start=start, stop=stop), reads=reads, writes=writes)

        def tr(out, in_, ident, reads, writes):
            k.op(pe, lambda e: e.transpose(out=out, in_=in_, identity=ident), reads=reads, writes=writes)

        def actf(out, in_, func, reads, writes, **kw):
            k.op(act, lambda e: e.activation(out=out, in_=in_, func=func, **kw), reads=reads, writes=writes)

        def tt(out, in0, in1, op, reads, writes, eng=None):
            k.op(eng or dve, lambda e: e.tensor_tensor(out=out, in0=in0, in1=in1, op=op), reads=reads, writes=writes)

        def v3(ap, a):
            return ap.rearrange("p (a b) -> p a b", a=a)

        k.op(dve, lambda e: e.memset(identf[:], 1.0), writes=[identf])
        k.op(dve, lambda e: e.memset(Um[:], 1.0), writes=[Um])
        k.op(dve, lambda e: e.memset(Ls[:], 1.0), writes=[Ls])
        k.op(dve, lambda e: e.memset(ones[:], 1.0), writes=[ones])
        k.op(pool, lambda e: e.affine_select(out=identf[:], in_=identf[:], pattern=[[-1, 128]], compare_op=ALU.is_equal,
                                             fill=0.0, base=0, channel_multiplier=1), reads=[identf], writes=[identf])
        k.op(pool, lambda e: e.affine_select(out=Um[:], in_=Um[:], pattern=[[1, 128]], compare_op=ALU.is_ge,
                                             fill=0.0, base=0, channel_multiplier=-1), reads=[Um], writes=[Um])
        k.op(pool, lambda e: e.affine_select(out=Ls[:], in_=Ls[:], pattern=[[-1, 128]], compare_op=ALU.is_ge,
                                             fill=0.0, base=-1, channel_multiplier=1), reads=[Ls], writes=[Ls])
        k.op(dve, lambda e: e.tensor_copy(out=identb[:], in_=identf[:]), reads=[identf], writes=[identb])
        for c in range(4):
            k.dma(sp, resid[c][0:128, :], xp[c * 128:(c + 1) * 128, :], writes=[resid[c]])
        k.dma(sp, resid[4][0:16, :], xs[:, :], writes=[resid[4]])
        for l in range(4):
            for kk in range(4):
                k.dma(sp, convw[:, l, :, kk], conv_w[l, kk].rearrange("(cb p) -> p cb", p=128), writes=[convw], allow_slow_non_contiguous=True)
            k.dma(sp, convb[:, l], conv_b[l].rearrange("(cb p) -> p cb", p=128), writes=[convb], allow_slow_non_contiguous=True)
            k.dma(sp, gn[:, l], ssd_g[l].rearrange("(eb p) -> p eb", p=128), writes=[gn], allow_slow_non_contiguous=True)
        k.dma(sp, dtb[:], dt_bias.partition_broadcast(128), writes=[dtb])
        k.dma(sp, aall[:], a_log.partition_broadcast(128), writes=[aall])
        k.dma(sp, Dall[:], d_skip.partition_broadcast(128), writes=[Dall])
        for l in range(4):
            k.op(dve, lambda e: e.memset(stP[l][:], 0.0), writes=[stP[l]])
            k.op(dve, lambda e: e.memset(tailsP[l][:], 0.0), writes=[tailsP[l]])

        pieces = []
        sub_of = {}
        sub_counter = [0]
        plan = []

        def add_piece(src, a):
            pieces.append((src, a)); return len(pieces) - 1

        def kp(ap):
            return ap.rearrange("(kk p) n -> p kk n", p=128)

        for t in range(ntiles):
            for l in range(depth):
                d = {}
                d["u"] = [add_piece(kp(w_in[l][:, i * 512:(i + 1) * 512]), 8) for i in range(2)]
                d["v"] = [add_piece(kp(w_in[l][:, 1024 + i * 512:1024 + (i + 1) * 512]), 8) for i in range(2)]
                d["z"] = [add_piece(kp(w_in[l][:, 2048 + i * 512:2048 + (i + 1) * 512]), 8) for i in range(2)]
                d["x"] = [add_piece(kp(w_in[l][:, 3072 + i * 512:3072 + (i + 1) * 512]), 8) for i in range(3)]
                d["o"] = [add_piece(kp(w_out[l][i * 512:(i + 1) * 512, :]), 4) for i in range(4)]
                for q in range(4):
                    d[f"f1_{q}"] = [add_piece(kp(w_ff1[l][:, q * 1024 + i * 512:q * 1024 + (i + 1) * 512]), 8) for i in range(2)]
                    d[f"f2_{q}"] = [add_piece(kp(w_ff2[l][q * 1024 + i * 512:q * 1024 + (i + 1) * 512, :]), 4) for i in range(2)]
                plan.append(d)
        issued = [0]
        done_upto = [-1]

        def pump():
            while issued[0] < len(pieces) and (issued[0] < NS or issued[0] - NS <= done_upto[0]):
                j = issued[0]
                src, a = pieces[j]
                s = slots[j % NS]
                k.dma(pool, s[:].rearrange("p (a b) -> p a b", a=a), src, writes=[s], q=wq[j % NS])
                issued[0] += 1

        def W(j, a):
            assert j < issued[0], "weight piece not issued before use"
            s = slots[j % NS]
            return s, s[:].rearrange("p (a b) -> p a b", a=a)

        def release(upto):
            done_upto[0] = max(done_upto[0], upto)
            pump()

        pump()
        for l in range(4):
            k.dma(pool, Wdt[:, l], w_in[l][:, 4608:4624].rearrange("(kk p) n -> p kk n", p=128), writes=[Wdt], q=pdq)

        def make_xT(ch):
            L, col = ch.L, ch.col
            r = resid[ch.rrow]
            actf(H1[0:L, :], r[0:L, :], AF.Copy, [r], [H1])
            p = pbank()
            pv = v3(p[:], 8)
            for kk in range(8):
                tr(pv[:, kk, 0:L], H1[0:L, kk * 128:(kk + 1) * 128], identb[0:L, 0:L], [H1, identb], [p])
            k.op(dve, lambda e: e.tensor_copy(out=xT[:, :, col:col + L], in_=pv[:, :, 0:L]), reads=[p], writes=[xT_c[ch.rrow]])

        def make_xT_b(ch):
            L, col = ch.L, ch.col
            p = pbank()
            pv = v3(p[:], 8)
            for kk in range(8):
                tr(pv[:, kk, 0:L], xw[0:L, kk * 128:(kk + 1) * 128], identb[0:L, 0:L], [xw, identb], [p])
            k.op(dve, lambda e: e.tensor_copy(out=xT[:, :, col:col + L], in_=pv[:, :, 0:L]), reads=[p], writes=[xT_c[ch.rrow]])

        def layer_norm(ch, grow, brow):
            L = ch.L
            r = resid[ch.rrow]
            for i in range(2):
                k.op(dve, lambda e: e.bn_stats(out=bst[0:L, i, :], in_=r[0:L, i * 512:(i + 1) * 512]), reads=[r], writes=[bst])
            k.op(dve, lambda e: e.bn_aggr(out=mv[0:L, :], in_=bst[0:L, :, :]), reads=[bst], writes=[mv])
            actf(rs[0:L, 0:1], mv[0:L, 1:2], AF.Ln, [mv], [rs], bias=1e-5)
            actf(rs[0:L, 0:1], rs[0:L, 0:1], AF.Exp, [rs], [rs], scale=-0.5)
            k.op(dve, lambda e: e.tensor_scalar(out=r[0:L, :], in0=r[0:L, :], scalar1=mv[0:L, 0:1], scalar2=rs[0:L, 0:1],
                                                op0=ALU.subtract, op1=ALU.mult), reads=[r, mv, rs], writes=[r])
            tt(r[0:L, :], r[0:L, :], grow[0:L, :], ALU.mult, [r, rowc], [r])
            tt(r[0:L, :], r[0:L, :], brow[0:L, :], ALU.add, [r, rowc], [r])

        def load_rows(g_ap, b_ap):
            k.dma(sp, rowc[:, 0, :], g_ap.partition_broadcast(128), writes=[rowc])
            k.dma(sp, rowc[:, 1, :], b_ap.partition_broadcast(128), writes=[rowc])

        def layer_consts_a(l):
            k.dma(sp, v3(F1[:], 8), gws[l].rearrange("h t s -> t h s"), writes=[F1])
            actf(H1[:, :], F1[:, :], AF.Copy, [F1], [H1])
            k.dma(sp, bsb[:].rearrange("p a b -> p (a b)"), gbs[l:l + 1, :].partition_broadcast(128), writes=[bsb])
            for cb in range(12):
                for kk in range(4):
                    k.op(dve, lambda e: e.tensor_scalar(out=diagw[:, cb * 4 + kk, :], in0=identb[:, :], scalar1=convw[:, l, cb, kk:kk + 1],
                                                         scalar2=None, op0=ALU.mult), reads=[identb, convw], writes=[diagw])

        def layer_consts_b(l):
            p = pbank(); pv = v3(p[:], 8)
            for h in range(8):
                tr(pv[:, h, :], H1[:, h * 128:(h + 1) * 128], identb[:, :], [H1, identb], [p])
            tt(WsT[:], pv, Um[:].unsqueeze(1).to_broadcast([128, 8, 128]), ALU.mult, [p, Um], [WsT])

        def run_pipeline(gens):
            pending = [tuple(g) + (False,) * (3 - len(g)) for g in gens]; active = []
            while pending or active:
                if pending and (not active or active[-1][1] >= active[-1][2]):
                    g, lag, of = pending.pop(0)
                    active.append([g, 0, lag, of])
                order = [a for a in reversed(active) if not a[3]] + [a for a in active if a[3]]
                for a in order:
                    try:
                        next(a[0]); a[1] += 1
                    except StopIteration:
                        active.remove(a)

        par_ctr = [0]

        def A1_group(t, l, d):
            load_rows(gln_g[l:l + 1, :], gln_b[l:l + 1, :])
            colr = [(0, 512, [0, 1, 2, 3]), (512, 16, [4])] if t == 0 else [(0, 384, [0, 1, 2]), (384, 128, [3])]
            for ub in range(8):
                s_, sv = W(d["u"][ub // 4], 8)
                for (c0, n, cl) in colr:
                    b = bank()
                    for kk in range(8):
                        mm(b[:, 0:n], sv[:, kk, (ub % 4) * 128:(ub % 4 + 1) * 128], xT[:, kk, c0:c0 + n], kk == 0, kk == 7,
                           [s_] + [xT_c[c] for c in cl], [b])
                    actf(hT[:, ub, c0:c0 + n], b[:, 0:n], AF.Gelu_apprx_tanh, [b], [hT])
            release(d["u"][1])

        def A1_gen(t, l, ch, d, is_last):
            L, col = ch.L, ch.col
            par = par_ctr[0] % 2; par_ctr[0] += 1
            F1 = F1s[par]
            mx = mixT_c[ch.rrow]
            for nb in range(2):
                s_, sv = W(d["v"][nb], 8)
                b = bank()
                for kk in range(8):
                    mm(b[0:L, :], xT[:, kk, col:col + L], sv[:, kk, :], kk == 0, kk == 7, [s_, xT_c[ch.rrow]], [b])
                actf(F1[0:L, nb * 512:(nb + 1) * 512], b[0:L, :], AF.Gelu_apprx_tanh, [b], [F1])
            if is_last:
                release(d["v"][1])
            yield
            for i in range(2):
                k.op(dve, lambda e: e.bn_stats(out=bst[0:L, i, :], in_=F1[0:L, i * 512:(i + 1) * 512]), reads=[F1], writes=[bst])
            k.op(dve, lambda e: e.bn_aggr(out=mv[0:L, :], in_=bst[0:L, :, :]), reads=[bst], writes=[mv])
            actf(rs[0:L, 0:1], mv[0:L, 1:2], AF.Ln, [mv], [rs], bias=1e-5)
            actf(rs[0:L, 0:1], rs[0:L, 0:1], AF.Exp, [rs], [rs], scale=-0.5)
            k.op(dve, lambda e: e.tensor_scalar(out=F1[0:L, :], in0=F1[0:L, :], scalar1=mv[0:L, 0:1], scalar2=rs[0:L, 0:1],
                                                op0=ALU.subtract, op1=ALU.mult), reads=[F1, mv, rs], writes=[F1])
            tt(F1[0:L, :], F1[0:L, :], rowc[0:L, 0, :], ALU.mult, [F1, rowc], [F1])
            if ch.seq == "s":
                tt(F1[0:L, :], F1[0:L, :], rowc[0:L, 1, :], ALU.add, [F1, rowc], [F1])
                k.dma(sp, v_s[l], F1[0:L, :], reads=[F1])
                k.op(dve, lambda e: e.tensor_copy(out=H1[0:L, :], in_=F1[0:L, :]), reads=[F1], writes=[H1])
            else:
                tt(H1[0:L, :], F1[0:L, :], rowc[0:L, 1, :], ALU.add, [F1, rowc], [H1])
            yield
            bs2 = [bank(), bank()]
            for h in range(8):
                b = bs2[h // 4]
                mm(v3(b[:], 4)[:, h % 4, 0:L], H1[0:L, h * 128:(h + 1) * 128], WsT[0:L, h, 0:L], True, True, [H1, WsT], [b])
            for hb in range(2):
                b = bs2[hb]
                tt(tmpS[:, :, 0:L], v3(b[:], 4)[:, :, 0:L], bsb[:, 4 * hb:4 * hb + 4, 0:L], ALU.add, [b, bsb], [tmpS])
                tt(mixT[:, 4 * hb:4 * hb + 4, col:col + L], tmpS[:, :, 0:L], hT[:, 4 * hb:4 * hb + 4, col:col + L], ALU.mult,
                   [tmpS, hT], [mx])
            yield

        def A2F_gen(t, l, ch, d, is_last, par):
            L, col = ch.L, ch.col
            F1 = F1s[par]; expa = expas[par]; xcT = xcTs[par]; dtv = dtvs[par]; da = das[par]; bdec = bdecs[par]
            mx = mixT_c[ch.rrow]
            samp = ch.seq == "s"
            state = stS if samp else stP[l]
            tails = tailsS if samp else tailsP[l]
            has_state = samp or not ch.first
            if samp:
                stf = F1[:]
                k.dma(sp, v3(stf[:, 0:1024], 8), sssm[l].rearrange("(blk p) n -> p blk n", p=128), writes=[F1])
                for half in range(2):
                    b = bank()
                    for j in range(4):
                        blk = half * 4 + j
                        tr(b[:, j * 128:(j + 1) * 128], stf[:, blk * 128:(blk + 1) * 128], identf[:, :], [F1, identf], [b])
                    actf(stS[:, half * 512:(half + 1) * 512], b[:, :], AF.Copy, [b], [stS])
                for kk in range(3):
                    k.dma(sp, tailsS[:, :, kk], sconv[l, kk].rearrange("(cb p) -> p cb", p=128), writes=[tailsS], allow_slow_non_contiguous=True)
            bd = bank()
            for kk in range(8):
                mm(bd[0:L, 0:16], xT[:, kk, col:col + L], Wdt[:, l, kk, :], kk == 0, kk == 7, [xT_c[ch.rrow], Wdt], [bd])
            tt(dtv[0:L, :], bd[0:L, 0:16], dtb[0:L, l * 16:(l + 1) * 16], ALU.add, [bd, dtb], [dtv])
            actf(dtv[0:L, :], dtv[0:L, :], AF.Exp, [dtv], [dtv])
            actf(dtv[0:L, :], dtv[0:L, :], AF.Ln, [dtv], [dtv], bias=1.0)
            tt(da[0:L, :], dtv[0:L, :], aall[0:L, l * 16:(l + 1) * 16], ALU.mult, [dtv, aall], [da])
            mm(bd[0:L, 32:48], Um[0:L, 0:L], da[0:L, :], True, True, [Um, da], [bd])
            mm(bd[:, 64:80], ones[0:L, :], da[0:L, :], True, True, [ones, da], [bd])
            actf(expa[0:L, :], bd[0:L, 32:48], AF.Exp, [bd], [expa])
            actf(bdec[:, :], bd[:, 64:80], AF.Exp, [bd], [bdec])
            yield
            for nb in range(2):
                s_, sv = W(d["z"][nb], 8)
                b = bank()
                for kk in range(8):
                    mm(b[0:L, :], xT[:, kk, col:col + L], sv[:, kk, :], kk == 0, kk == 7, [s_, xT_c[ch.rrow]], [b])
                actf(F1[0:L, nb * 512:(nb + 1) * 512], b[0:L, :], AF.Silu, [b], [F1])
            yield
            xb_banks = [bank(), bank(), bank()]
            for cb in range(12):
                s_, sv = W(d["x"][cb // 4], 8)
                b = xb_banks[cb // 4]
                for kk in range(8):
                    mm(v3(b[:], 4)[:, cb % 4, 0:L], sv[:, kk, (cb % 4) * 128:(cb % 4 + 1) * 128], xT[:, kk, col:col + L],
                       kk == 0, kk == 7, [s_, xT_c[ch.rrow]], [b])
            if is_last:
                release(d["x"][2])
            k.op(dve, lambda e: e.tensor_copy(out=st_t[:, :, 0:3], in_=tails[:, :, :]), reads=[tails], writes=[st_t])
            for j in range(3):
                b = xb_banks[j]
                k.op(dve, lambda e: e.tensor_copy(out=st_t[:, 4 * j:4 * j + 4, 3:3 + L], in_=v3(b[:], 4)[:, :, 0:L]), reads=[b], writes=[st_t])
                k.op(dve, lambda e: e.tensor_copy(out=tails[:, 4 * j:4 * j + 4, :], in_=v3(b[:], 4)[:, :, L - 3:L]), reads=[b, st_t], writes=[tails])
            yield
            cv_banks = [bank(), bank(), bank()]
            for cb in range(12):
                b = cv_banks[cb // 4]
                for kk in range(4):
                    mm(v3(b[:], 4)[:, cb % 4, 0:L], diagw[:, cb * 4 + kk, :], st_t[:, cb, kk:kk + L], kk == 0, kk == 3, [diagw, st_t], [b])
            for cb in range(12):
                b = cv_banks[cb // 4]
                actf(xcT[:, cb, 0:L], v3(b[:], 4)[:, cb % 4, 0:L], AF.Silu, [b, convb], [xcT] if cb in (0, 11) else [],
                     bias=convb[:, l, cb:cb + 1])
            if ch.last:
                for j in range(3):
                    b = bank()
                    for i in range(4):
                        tr(b[0:3, i * 128:(i + 1) * 128], tails[:, 4 * j + i, :], identf[:, :], [tails, identf], [b])
                    rt = relu_tmp[j]
                    rtv = rt[:].rearrange("p a b -> p (a b)")
                    actf(rtv[0:3, :], b[0:3, :], AF.Copy, [b], [rt])
                    k.dma(sp, (conv_s if samp else conv_p)[l][:, j * 512:(j + 1) * 512], rtv[0:3, :], reads=[rt])
            yield

        def A2B_gen(t, l, ch, d, is_last, par):
            L, col = ch.L, ch.col
            F1 = F1s[par]; expa = expas[par]; xcT = xcTs[par]; dtv = dtvs[par]; da = das[par]; bdec = bdecs[par]
            mx = mixT_c[ch.rrow]
            samp = ch.seq == "s"
            state = stS if samp else stP[l]
            tails = tailsS if samp else tailsP[l]
            has_state = samp or not ch.first
            stateb = statebs[par]
            if samp or (has_state and ch.rrow == 0):
                actf(stateb[:, :], state[:, :], AF.Copy, [state], [stateb])
            if has_state:
                tt(v3(state[:, :], 16), v3(state[:, :], 16), bdec[:, :].unsqueeze(2).to_broadcast([128, 16, 64]), ALU.mult,
                   [state, bdec], [state], eng=pool)
            pA = pbank(); pAv = v3(pA[:], 8)
            for hb in range(8):
                tr(pAv[0:L, hb, :], xcT[:, hb, 0:L], identb[:, :], [xcT, identb], [pA])
            actf(xt[0:L, :], pA[0:L, :], AF.Copy, [pA], [xt])
            tt(v3(xdt[0:L, :], 16), v3(xt[0:L, :], 16), dtv[0:L, :].unsqueeze(2).to_broadcast([L, 16, 64]), ALU.mult, [xt, dtv], [xdt])
            tt(v3(F3[0:L, :], 16), v3(xt[0:L, :], 16), Dall[0:L, l * 16:(l + 1) * 16].unsqueeze(2).to_broadcast([L, 16, 64]),
               ALU.mult, [xt, Dall], [F3], eng=pool)
            pB = pbank()
            for g in range(2):
                tr(pB[0:L, g * 128:(g + 1) * 128], xcT[:, 8 + g, 0:L], identb[:, :], [xcT, identb], [pB])
            actf(Bt[0:L, :], pB[0:L, 0:256], AF.Copy, [pB], [Bt])
            bc = bank()
            for g in range(2):
                mm(v3(bc[:], 4)[0:L, g, 0:L], xcT[:, 8 + g, 0:L], xcT[:, 10 + g, 0:L], True, True, [xcT], [bc])
            tt(cbm[0:L, :, 0:L], v3(bc[:], 4)[0:L, 0:2, 0:L], Um[0:L, 0:L].unsqueeze(1).to_broadcast([L, 2, L]), ALU.mult, [bc, Um], [cbm])
            yield
            for hq in range(4):
                ru = rhsU[hq % 2]
                tt(ru[0:L, :, 0:L], Um[0:L, 0:L].unsqueeze(1).to_broadcast([L, 4, L]),
                   da[0:L, hq * 4:hq * 4 + 4].unsqueeze(2).to_broadcast([L, 4, L]), ALU.mult, [Um, da], [ru])
                b = bank()
                if L == 128:
                    mm(v3(b[:], 4)[0:L, :, 0:L], Ls[0:L, 0:L], ru[0:L, :, 0:L], True, True, [Ls, ru], [b])
                else:
                    for hh in range(4):
                        mm(v3(b[:], 4)[0:L, hh, 0:L], Ls[0:L, 0:L], ru[0:L, hh, 0:L], True, True, [Ls, ru], [b])
                actf(dec[0:L, hq * 4:hq * 4 + 4, 0:L], v3(b[:], 4)[0:L, :, 0:L], AF.Exp, [b], [dec])
            tt(wcol[0:L, :].unsqueeze(2), dtv[0:L, :].unsqueeze(2), dec[0:L, :, L - 1:L], ALU.mult, [dtv, dec], [wcol])
            tt(v3(xw[0:L, :], 16), v3(xt[0:L, :], 16), wcol[0:L, :].unsqueeze(2).to_broadcast([L, 16, 64]), ALU.mult, [xt, wcol], [xw])
            yield
            for g in range(2):
                tt(dec[0:L, g * 8:(g + 1) * 8, 0:L], dec[0:L, g * 8:(g + 1) * 8, 0:L],
                   cbm[0:L, g:g + 1, 0:L].to_broadcast([L, 8, L]), ALU.mult, [dec, cbm], [dec])
            bS = [bank(), bank()]
            for g in range(2):
                mm(bS[g][:, :], Bt[0:L, g * 128:(g + 1) * 128], xw[0:L, g * 512:(g + 1) * 512], True, True, [Bt, xw], [bS[g]])
            if has_state:
                for g in range(2):
                    tt(state[:, g * 512:(g + 1) * 512], state[:, g * 512:(g + 1) * 512], bS[g][:, :], ALU.add, [state, bS[g]], [state])
            else:
                for g in range(2):
                    actf(state[:, g * 512:(g + 1) * 512], bS[g][:, :], AF.Copy, [bS[g]], [state])
            if not samp and ch.rrow < 3:
                actf(statebs[1 - par][:, :], state[:, :], AF.Copy, [state], [statebs[1 - par]])
            yield
            bY = [bank(), bank()]
            for h in range(16):
                b = bY[h // 8]
                mm(b[0:L, (h % 8) * 64:(h % 8 + 1) * 64], dec[0:L, h, 0:L], xdt[0:L, h * 64:(h + 1) * 64], True, True, [dec, xdt], [b])
            if has_state:
                bO = [bank(), bank()]
                for g in range(2):
                    mm(bO[g][0:L, :], xcT[:, 10 + g, 0:L], stateb[:, g * 512:(g + 1) * 512], True, True, [xcT, stateb], [bO[g]])
                for g in range(2):
                    tt(v3(F2[0:L, g * 512:(g + 1) * 512], 8), v3(bO[g][0:L, :], 8),
                       expa[0:L, g * 8:(g + 1) * 8].unsqueeze(2).to_broadcast([L, 8, 64]), ALU.mult, [bO[g], expa], [F2])
                    tt(F2[0:L, g * 512:(g + 1) * 512], F2[0:L, g * 512:(g + 1) * 512], bY[g][0:L, :], ALU.add, [F2, bY[g]], [F2])
            else:
                for g in range(2):
                    actf(F2[0:L, g * 512:(g + 1) * 512], bY[g][0:L, :], AF.Copy, [bY[g]], [F2])
            tt(F2[0:L, :], F2[0:L, :], F3[0:L, :], ALU.add, [F2, F3], [F2])
            tt(F2[0:L, :], F2[0:L, :], F1[0:L, :], ALU.mult, [F2, F1], [F2])
            yield
            k.op(dve, lambda e: e.memset(ssq[:], 0.0), writes=[ssq])
            actf(rs[0:1, 0:1], ones[0:1, 0:1], AF.Exp, [ones], [rs])
            for g in range(2):
                actf(H1[0:L, g * 512:(g + 1) * 512], F2[0:L, g * 512:(g + 1) * 512], AF.Square, [F2, ssq], [H1, ssq],
                     accum_out=ssq[0:L, g:g + 1])
            actf(rs[0:L, :], ssq[0:L, :], AF.Ln, [ssq], [rs], scale=1.0 / 512.0, bias=1e-5)
            actf(rs[0:L, :], rs[0:L, :], AF.Exp, [rs], [rs], scale=-0.5)
            for g in range(2):
                k.op(dve, lambda e: e.tensor_scalar(out=H1[0:L, g * 512:(g + 1) * 512], in0=F2[0:L, g * 512:(g + 1) * 512],
                                                    scalar1=rs[0:L, g:g + 1], scalar2=None, op0=ALU.mult), reads=[F2, rs], writes=[H1])
            pC = pbank(); pCv = v3(pC[:], 8)
            for eb in range(8):
                tr(pCv[:, eb, 0:L], H1[0:L, eb * 128:(eb + 1) * 128], identb[0:L, 0:L], [H1, identb], [pC])
            k.op(dve, lambda e: e.tensor_copy(out=mixT[:, 8:16, col:col + L], in_=pCv[:, :, 0:L]), reads=[pC], writes=[mx])
            if ch.last:
                for half in range(2):
                    b = bank()
                    for j in range(4):
                        blk = half * 4 + j
                        tr(b[:, j * 128:(j + 1) * 128], state[:, blk * 128:(blk + 1) * 128], identf[:, :], [state, identf], [b])
                    actf(F3[:, half * 512:(half + 1) * 512], b[:, :], AF.Copy, [b], [F3])
                k.dma(sp, (ssm_s if samp else ssm_p)[l].rearrange("(blk p) n -> p blk n", p=128), v3(F3[:], 8), reads=[F3])
            yield

        def A3_gen(t, l, ch, d, is_first, is_last):
            L, col = ch.L, ch.col
            r = resid[ch.rrow]
            mx = mixT_c[ch.rrow]
            if is_first:
                load_rows(ln1_g[l:l + 1, :], ln1_b[l:l + 1, :])
                for pi in (2, 3):
                    s_, sv = W(d["o"][pi], 4)
                    for j in range(4):
                        eb = (pi - 2) * 4 + j
                        k.op(dve, lambda e: e.tensor_scalar(out=sv[:, j, :], in0=sv[:, j, :], scalar1=gn[:, l, eb:eb + 1], scalar2=None,
                                                            op0=ALU.mult), reads=[s_, gn], writes=[s_])
            for nb in range(2):
                b = bank()
                for kk in range(16):
                    s_, sv = W(d["o"][kk // 4], 4)
                    mm(b[0:L, :], mixT[:, kk, col:col + L], sv[:, kk % 4, nb * 512:(nb + 1) * 512], kk == 0, kk == 15, [s_, mx], [b])
                k.op(dve, lambda e: e.scalar_tensor_tensor(out=r[0:L, nb * 512:(nb + 1) * 512], in0=r[0:L, nb * 512:(nb + 1) * 512],
                                                           scalar=float(ALPHA), in1=b[0:L, :], op0=ALU.mult, op1=ALU.add),
                     reads=[r, b], writes=[r])
            if is_last:
                release(d["o"][3])
            yield
            layer_norm(ch, rowc[:, 0, :], rowc[:, 1, :])
            actf(xw[0:L, :], r[0:L, :], AF.Copy, [r], [xw])
            yield
            make_xT_b(ch)
            yield

        def B_group(t, l, q, d):
            if q == 3:
                load_rows(ln2_g[l:l + 1, :], ln2_b[l:l + 1, :])
            if t == 0:
                colr = [(0, 512, [0, 1, 2, 3]), (512, 16, [4])]
            elif q == 0:
                colr = [(0, 384, [0, 1, 2]), (384, 128, [3])]
            else:
                colr = [(0, 512, [0, 1, 2, 3])]
            for fb in range(8):
                s_, sv = W(d[f"f1_{q}"][fb // 4], 8)
                for (c0, n, cl) in colr:
                    b = bank()
                    for kk in range(8):
                        mm(b[:, 0:n], sv[:, kk, (fb % 4) * 128:(fb % 4 + 1) * 128], xT[:, kk, c0:c0 + n], kk == 0, kk == 7,
                           [s_] + [xT_c[c] for c in cl], [b])
                    rt = relu_tmp[relu_i[0] % 3]; relu_i[0] += 1
                    rtv = rt[:].rearrange("p a b -> p (a b)")
                    actf(rtv[:, 0:n], b[:, 0:n], AF.Relu, [b], [rt])
                    tt(hT[:, fb, c0:c0 + n], rtv[:, 0:n], rtv[:, 0:n], ALU.mult, [rt], [hT])
            release(d[f"f1_{q}"][1])

        def B_gen(t, l, q, ch, d, is_last):
            L, col = ch.L, ch.col
            r = resid[ch.rrow]
            for nb in range(2):
                b = bank()
                for fc in range(8):
                    s_, sv = W(d[f"f2_{q}"][fc // 4], 4)
                    mm(b[0:L, :], hT[:, fc, col:col + L], sv[:, fc % 4, nb * 512:(nb + 1) * 512], fc == 0, fc == 7, [s_, hT], [b])
                rr = r[0:L, nb * 512:(nb + 1) * 512]
                if q == 0:
                    k.op(dve, lambda e: e.scalar_tensor_tensor(out=rr, in0=rr, scalar=float(ALPHA), in1=b[0:L, :],
                                                               op0=ALU.mult, op1=ALU.add), reads=[r, b], writes=[r])
                else:
                    tt(rr, rr, b[0:L, :], ALU.add, [r, b], [r])
            if is_last:
                release(d[f"f2_{q}"][1])
            yield
            if q == 3:
                layer_norm(ch, rowc[:, 0, :], rowc[:, 1, :])
                if l != depth - 1:
                    actf(xw[0:L, :], r[0:L, :], AF.Copy, [r], [xw])
                yield
                if l == depth - 1:
                    if ch.seq == "s":
                        k.dma(sp, ys[:, :], r[0:L, :], reads=[r])
                    else:
                        k.dma(sp, yp[ch.tok0:ch.tok0 + L, :], r[0:L, :], reads=[r])
                else:
                    make_xT_b(ch)
                yield

        def chk(tag):
            if stop == tag:
                raise _Stop()
        try:
          for t in range(ntiles):
              chunks = [Chunk(128, c * 128, c, "p", t == 0 and c == 0, t == ntiles - 1 and c == 3, t * 512 + c * 128) for c in range(4)]
              if t == 0:
                  chunks.append(Chunk(16, 512, 4, "s", True, True, 0))
              for ch in chunks:
                  if t > 0:
                      k.dma(sp, resid[ch.rrow][0:ch.L, :], xp[ch.tok0:ch.tok0 + ch.L, :], writes=[resid[ch.rrow]])
                  make_xT(ch)
              n = len(chunks)
              if t == 0:
                  actf(aall[:], aall[:], AF.Exp, [aall], [aall])
                  k.op(dve, lambda e: e.tensor_scalar(out=aall[:], in0=aall[:], scalar1=-1.0, scalar2=None, op0=ALU.mult),
                       reads=[aall], writes=[aall])
              for l in range(depth):
                  d = plan[t * depth + l]
                  layer_consts_a(l)
                  A1_group(t, l, d)
                  layer_consts_b(l)
                  gens = []
                  gens += [(A1_gen(t, l, ch, d, i == n - 1), 1, True) for i, ch in enumerate(chunks)]
                  for i, ch in enumerate(chunks):
                      gens.append((A2F_gen(t, l, ch, d, i == n - 1, i % 2), 4))
                      gens.append((A2B_gen(t, l, ch, d, i == n - 1, i % 2), 0))
                  gens += [(A3_gen(t, l, ch, d, i == 0, i == n - 1), 1, True) for i, ch in enumerate(chunks)]
                  run_pipeline(gens)
                  chk("A3")
                  for q in range(4):
                      B_group(t, l, q, d)
                      run_pipeline([(B_gen(t, l, q, ch, d, i == n - 1), 1, True) for i, ch in enumerate(chunks)])
        except _Stop:
            pass
        k.finish(extra=wq + [pdq])
    return nc


_NC_CACHE = {}


def kernel(x_prompt, x_sample, state_ssm, state_conv, w_in, gmlp_ln_g, gmlp_ln_b, gmlp_ws, gmlp_bs,
           conv_w, conv_b, dt_bias, a_log, d_skip, ssd_norm_g, w_out, ln1_g, ln1_b, w_ff1, w_ff2,
           ln2_g, ln2_b):
    f = lambda a: np.ascontiguousarray(np.asarray(a, dtype=np.float32))
    if "nc" not in _NC_CACHE:
        _NC_CACHE["nc"] = build_program()
    nc = _NC_CACHE["nc"]
    shared = {
        "w_in": f(w_in), "gmlp_ln_g": f(gmlp_ln_g), "gmlp_ln_b": f(gmlp_ln_b), "gmlp_ws": f(gmlp_ws),
        "gmlp_bs": f(gmlp_bs).reshape(4, 1024), "conv_w": f(conv_w), "conv_b": f(conv_b),
        "dt_bias": f(dt_bias).reshape(1, 64), "a_log": f(a_log).reshape(1, 64), "d_skip": f(d_skip).reshape(1, 64),
        "ssd_norm_g": f(ssd_norm_g), "w_out": f(w_out), "ln1_g": f(ln1_g), "ln1_b": f(ln1_b),
        "w_ff1": f(w_ff1), "w_ff2": f(w_ff2), "ln2_g": f(ln2_g), "ln2_b": f(ln2_b),
    }
    xp = f(x_prompt); xs = f(x_sample); ss = f(state_ssm); sc = f(state_conv)
    in_maps = []
    for b in range(8):
        m = dict(shared)
        m["xp"] = xp[b]; m["xs"] = xs[b]
        m["sssm"] = np.ascontiguousarray(ss[:, b].reshape(4, 1024, 128))
        m["sconv"] = np.ascontiguousarray(sc[:, b])
        in_maps.append(m)
    res = run_bass_kernel_spmd(nc, in_maps, core_ids=list(range(8)))
    R = res.results
    y_prompt = np.stack([R[b]["yp"] for b in range(8)], axis=0)
    y_sample = np.stack([R[b]["ys"] for b in range(8)], axis=0)
    ssm_p = np.stack([R[b]["ssm_p"].reshape(4, 16, 64, 128) for b in range(8)], axis=1)
    conv_p = np.stack([R[b]["conv_p"] for b in range(8)], axis=1)
    ssm_s = np.stack([R[b]["ssm_s"].reshape(4, 16, 64, 128) for b in range(8)], axis=1)
    conv_s = np.stack([R[b]["conv_s"] for b in range(8)], axis=1)
    v_s = np.stack([R[b]["v_s"] for b in range(8)], axis=1)
    return (y_prompt, y_sample, ssm_p, conv_p, ssm_s, conv_s, v_s)
```

```python
import contextlib
import numpy as np
import concourse.bass as bass
import concourse.mybir as mybir
from concourse.bass_utils import run_bass_kernel_spmd

F32 = mybir.dt.float32
BF16 = mybir.dt.bfloat16
AF = mybir.ActivationFunctionType
ALU = mybir.AluOpType

DEPTH = 4
NTILES = 4
ALPHA = (2 * DEPTH) ** 0.25
NS = 7


class Eng:
    def __init__(self, name, h, sem, step=1, is_pe=False):
        self.name = name; self.h = h; self.sem = sem; self.step = step
        self.count = 0; self.is_pe = is_pe; self.waited = {}


class Buf:
    def __init__(self, t=None, name=""):
        self.t = t; self.name = name; self.w = None; self.r = {}

    def __getitem__(self, k):
        return self.t[k]


class AliasBuf(Buf):
    def __init__(self, base, view, name=""):
        self.base = base; self.view = view; self.name = name

    def __getitem__(self, k):
        return self.view[k]

    @property
    def w(self):
        return self.base.w

    @w.setter
    def w(self, v):
        self.base.w = v

    @property
    def r(self):
        return self.base.r

    @r.setter
    def r(self, v):
        self.base.r = v


class K:
    def __init__(self, nc, es):
        self.nc = nc; self.es = es
        self.pe = Eng("pe", nc.tensor, self.sem("s_pe"), is_pe=True)
        self.act = Eng("act", nc.scalar, self.sem("s_act"))
        self.dve = Eng("dve", nc.vector, self.sem("s_dve"))
        self.pool = Eng("pool", nc.gpsimd, self.sem("s_pool"))
        self.sp = Eng("sp", nc.sync, self.sem("s_sp"))
        self.dq = [Eng(f"dq{i}", None, self.sem(f"s_dq{i}"), step=16) for i in range(16)]
        self.dq_i = 0
        self.nbuf = 0

    def sem(self, n):
        return self.es.enter_context(self.nc.semaphore(n))

    def sb(self, shape, dt, name=None):
        self.nbuf += 1
        name = name or f"b{self.nbuf}"
        return Buf(self.es.enter_context(self.nc.sbuf_tensor(name, list(shape), dt)), name)

    def ps(self, shape, dt, name=None):
        self.nbuf += 1
        name = name or f"p{self.nbuf}"
        return Buf(self.es.enter_context(self.nc.psum_tensor(name, list(shape), dt)), name)

    def _wait(self, eng, e2, ts):
        if e2 is eng and eng.is_pe:
            return
        if eng.waited.get(e2.name, 0) >= ts:
            return
        eng.h.wait_ge(e2.sem, ts)
        eng.waited[e2.name] = ts

    def _deps(self, eng, reads, writes):
        deps = {}

        def add(e2, ts):
            if deps.get(e2.name, (None, 0))[1] < ts:
                deps[e2.name] = (e2, ts)
        for b in reads:
            if b.w: add(*b.w)
        for b in writes:
            if b.w: add(*b.w)
            for e2, ts in b.r.values(): add(e2, ts)
        for e2, ts in deps.values():
            self._wait(eng, e2, ts)

    def op(self, eng, fn, reads=(), writes=()):
        self._deps(eng, reads, writes)
        ins = fn(eng.h)
        eng.count += 1
        ins.then_inc(eng.sem, 1)
        ts = eng.count
        for b in reads: b.r[eng.name] = (eng, ts)
        for b in writes:
            b.w = (eng, ts); b.r = {}
        return ins

    def dma(self, issuer, out, in_, reads=(), writes=(), q=None, **kw):
        if q is None:
            q = self.dq[self.dq_i]; self.dq_i = (self.dq_i + 1) % len(self.dq)
        if q.count > 0:
            self._wait(issuer, q, q.count * 16)
        self._deps(issuer, reads, writes)
        ins = issuer.h.dma_start(out=out, in_=in_, **kw)
        q.count += 1
        ins.then_inc(q.sem, 16)
        ts = q.count * 16
        for b in reads: b.r[q.name] = (q, ts)
        for b in writes:
            b.w = (q, ts); b.r = {}

    def finish(self, extra=()):
        for q in list(self.dq) + list(extra):
            if q.count: self._wait(self.sp, q, q.count * 16)


class Chunk:
    def __init__(self, L, col, rrow, seq, first, last, tok0):
        self.L = L; self.col = col; self.rrow = rrow; self.seq = seq
        self.first = first; self.last = last; self.tok0 = tok0


class _Stop(Exception):
    pass


def build_program(ntiles=NTILES, depth=DEPTH, stop=None):
    nc = bass.Bass("TRN2", target_bir_lowering=False)

    def din(name, shape):
        return nc.dram_tensor(name, list(shape), F32, kind="ExternalInput").ap()

    def dout(name, shape):
        return nc.dram_tensor(name, list(shape), F32, kind="ExternalOutput").ap()

    xp = din("xp", [2048, 1024]); xs = din("xs", [16, 1024])
    sssm = din("sssm", [4, 1024, 128]); sconv = din("sconv", [4, 3, 1536])
    w_in = din("w_in", [4, 1024, 4624])
    gln_g = din("gmlp_ln_g", [4, 1024]); gln_b = din("gmlp_ln_b", [4, 1024])
    gws = din("gmlp_ws", [4, 8, 128, 128]); gbs = din("gmlp_bs", [4, 1024])
    conv_w = din("conv_w", [4, 4, 1536]); conv_b = din("conv_b", [4, 1536])
    dt_bias = din("dt_bias", [1, 64]); a_log = din("a_log", [1, 64]); d_skip = din("d_skip", [1, 64])
    ssd_g = din("ssd_norm_g", [4, 1024])
    w_out = din("w_out", [4, 2048, 1024])
    ln1_g = din("ln1_g", [4, 1024]); ln1_b = din("ln1_b", [4, 1024])
    w_ff1 = din("w_ff1", [4, 1024, 4096]); w_ff2 = din("w_ff2", [4, 4096, 1024])
    ln2_g = din("ln2_g", [4, 1024]); ln2_b = din("ln2_b", [4, 1024])

    yp = dout("yp", [2048, 1024]); ys = dout("ys", [16, 1024])
    ssm_p = dout("ssm_p", [4, 1024, 128]); conv_p = dout("conv_p", [4, 3, 1536])
    ssm_s = dout("ssm_s", [4, 1024, 128]); conv_s = dout("conv_s", [4, 3, 1536])
    v_s = dout("v_s", [4, 16, 1024])

    with contextlib.ExitStack() as es:
        k = K(nc, es)
        pe, act, dve, pool, sp = k.pe, k.act, k.dve, k.pool, k.sp
        wq = [Eng(f"wq{i}", None, k.sem(f"s_wq{i}"), step=16) for i in range(NS)]
        pdq = Eng("pdq", None, k.sem("s_pdq"), step=16)

        resid = [k.sb([128, 1024], F32, f"resid{i}") for i in range(5)]
        xT = k.sb([128, 8, 528], BF16, "xT")
        mixT = k.sb([128, 16, 528], BF16, "mixT")
        mixT_c = [Buf(mixT.t, f"mixT_c{i}") for i in range(5)]
        xT_c = [Buf(xT.t, f"xT_c{i}") for i in range(5)]
        hT = k.sb([128, 8, 528], BF16, "hT")
        slots = [k.sb([128, 4096], BF16, f"slot{i}") for i in range(NS)]
        stP = [k.sb([128, 1024], F32, f"stP{i}") for i in range(4)]
        stS = k.sb([128, 1024], F32, "stS")
        statebs = [k.sb([128, 1024], BF16, "stateb0"), k.sb([128, 1024], BF16, "stateb1")]
        tailsP = [k.sb([128, 12, 3], F32, f"tailsP{i}") for i in range(4)]
        tailsS = k.sb([128, 12, 3], F32, "tailsS")
        rowc = k.sb([128, 2, 1024], F32, "rowc")
        bsb = k.sb([128, 8, 128], F32, "bsb")
        WsT = k.sb([128, 8, 128], BF16, "WsT")
        identb = k.sb([128, 128], BF16, "identb"); identf = k.sb([128, 128], F32, "identf")
        Um = k.sb([128, 128], F32, "Um"); Ls = k.sb([128, 128], F32, "Ls"); ones = k.sb([128, 128], F32, "ones")
        convw = k.sb([128, 4, 12, 4], F32, "convw"); convb = k.sb([128, 4, 12], F32, "convb")
        gn = k.sb([128, 4, 8], F32, "gn")
        dtb = k.sb([128, 64], F32, "dtb"); aall = k.sb([128, 64], F32, "aall"); Dall = k.sb([128, 64], F32, "Dall")
        Wdt = k.sb([128, 4, 8, 16], BF16, "Wdt")
        F1s = [k.sb([128, 1024], F32, "F1a"), k.sb([128, 1024], F32, "F1b")]; F1 = F1s[0]
        F2 = k.sb([128, 1024], F32, "F2"); F3 = k.sb([128, 1024], F32, "F3")
        H1 = k.sb([128, 1024], BF16, "H1")
        st_t = k.sb([128, 12, 132], BF16, "st")
        diagw = k.sb([128, 48, 128], BF16, "diagw")
        xcTs = [k.sb([128, 12, 128], BF16, "xcT0"), k.sb([128, 12, 128], BF16, "xcT1")]
        xt = F2; wcol = k.sb([128, 16], F32, "wcol"); xdt = k.sb([128, 1024], BF16, "xdt"); xw = k.sb([128, 1024], BF16, "xw")
        Bt = k.sb([128, 256], BF16, "Bt")
        rhsU = [k.sb([128, 4, 128], F32, f"rhsU{i}") for i in range(2)]
        dec = k.sb([128, 16, 128], BF16, "dec")
        cbm = k.sb([128, 2, 128], BF16, "cbm")
        tmpS = AliasBuf(F3, F3.t[:, 0:512].rearrange("p (a b) -> p a b", a=4), "tmpS")
        bst = k.sb([128, 2, 6], F32, "bst"); mv = k.sb([128, 2], F32, "mv"); rs = k.sb([128, 2], F32, "rs")
        dtvs = [k.sb([128, 16], F32, "dtv0"), k.sb([128, 16], F32, "dtv1")]; das = [k.sb([128, 16], F32, "da0"), k.sb([128, 16], F32, "da1")]
        expas = [k.sb([128, 16], F32, "expa0"), k.sb([128, 16], F32, "expa1")]; bdecs = [k.sb([128, 16], F32, "bdec0"), k.sb([128, 16], F32, "bdec1")]
        ssq = k.sb([128, 2], F32, "ssq")

        pf = [k.ps([128, 512], F32, f"pf{i}") for i in range(6)]
        pb = [k.ps([128, 1024], BF16, f"pb{i}") for i in range(2)]
        bank_i = [0]; pb_i = [0]
        relu_tmp = [tmpS, rhsU[0], rhsU[1]]; relu_i = [0]

        def bank():
            b = pf[bank_i[0]]; bank_i[0] = (bank_i[0] + 1) % len(pf); return b

        def pbank():
            b = pb[pb_i[0]]; pb_i[0] = (pb_i[0] + 1) % len(pb); return b

        def mm(out, lhsT, rhs, start, stop, reads, writes):
            k.op(pe, lambda e: e.matmul(out, lhsT=lhsT, rhs=rhs, start=start, stop=stop), reads=reads, writes=writes)

        def tr(out, in_, ident, reads, writes):
            k.op(pe, lambda e: e.transpose(out=out, in_=in_, identity=ident), reads=reads, writes=writes)

        def actf(out, in_, func, reads, writes, **kw):
            k.op(act, lambda e: e.activation(out=out, in_=in_, func=func, **kw), reads=reads, writes=writes)

        def tt(out, in0, in1, op, reads, writes, eng=None):
            k.op(eng or dve, lambda e: e.tensor_tensor(out=out, in0=in0, in1=in1, op=op), reads=reads, writes=writes)

        def v3(ap, a):
            return ap.rearrange("p (a b) -> p a b", a=a)

        k.op(dve, lambda e: e.memset(identf[:], 1.0), writes=[identf])
        k.op(dve, lambda e: e.memset(Um[:], 1.0), writes=[Um])
        k.op(dve, lambda e: e.memset(Ls[:], 1.0), writes=[Ls])
        k.op(dve, lambda e: e.memset(ones[:], 1.0), writes=[ones])
        k.op(pool, lambda e: e.affine_select(out=identf[:], in_=identf[:], pattern=[[-1, 128]], compare_op=ALU.is_equal,
                                             fill=0.0, base=0, channel_multiplier=1), reads=[identf], writes=[identf])
        k.op(pool, lambda e: e.affine_select(out=Um[:], in_=Um[:], pattern=[[1, 128]], compare_op=ALU.is_ge,
                                             fill=0.0, base=0, channel_multiplier=-1), reads=[Um], writes=[Um])
        k.op(pool, lambda e: e.affine_select(out=Ls[:], in_=Ls[:], pattern=[[-1, 128]], compare_op=ALU.is_ge,
                                             fill=0.0, base=-1, channel_multiplier=1), reads=[Ls], writes=[Ls])
        k.op(dve, lambda e: e.tensor_copy(out=identb[:], in_=identf[:]), reads=[identf], writes=[identb])
        for c in range(4):
            k.dma(sp, resid[c][0:128, :], xp[c * 128:(c + 1) * 128, :], writes=[resid[c]])
        k.dma(sp, resid[4][0:16, :], xs[:, :], writes=[resid[4]])
        for l in range(4):
            for kk in range(4):
                k.dma(sp, convw[:, l, :, kk], conv_w[l, kk].rearrange("(cb p) -> p cb", p=128), writes=[convw], allow_slow_non_contiguous=True)
            k.dma(sp, convb[:, l], conv_b[l].rearrange("(cb p) -> p cb", p=128), writes=[convb], allow_slow_non_contiguous=True)
            k.dma(sp, gn[:, l], ssd_g[l].rearrange("(eb p) -> p eb", p=128), writes=[gn], allow_slow_non_contiguous=True)
        k.dma(sp, dtb[:], dt_bias.partition_broadcast(128), writes=[dtb])
        k.dma(sp, aall[:], a_log.partition_broadcast(128), writes=[aall])
        k.dma(sp, Dall[:], d_skip.partition_broadcast(128), writes=[Dall])
        for l in range(4):
            k.op(dve, lambda e: e.memset(stP[l][:], 0.0), writes=[stP[l]])
            k.op(dve, lambda e: e.memset(tailsP[l][:], 0.0), writes=[tailsP[l]])

        pieces = []
        sub_of = {}
        sub_counter = [0]
        plan = []

        def add_piece(src, a):
            pieces.append((src, a)); return len(pieces) - 1

        def kp(ap):
            return ap.rearrange("(kk p) n -> p kk n", p=128)

        for t in range(ntiles):
            for l in range(depth):
                d = {}
                d["u"] = [add_piece(kp(w_in[l][:, i * 512:(i + 1) * 512]), 8) for i in range(2)]
                d["v"] = [add_piece(kp(w_in[l][:, 1024 + i * 512:1024 + (i + 1) * 512]), 8) for i in range(2)]
                d["z"] = [add_piece(kp(w_in[l][:, 2048 + i * 512:2048 + (i + 1) * 512]), 8) for i in range(2)]
                d["x"] = [add_piece(kp(w_in[l][:, 3072 + i * 512:3072 + (i + 1) * 512]), 8) for i in range(3)]
                d["o"] = [add_piece(kp(w_out[l][i * 512:(i + 1) * 512, :]), 4) for i in range(4)]
                for q in range(4):
                    d[f"f1_{q}"] = [add_piece(kp(w_ff1[l][:, q * 1024 + i * 512:q * 1024 + (i + 1) * 512]), 8) for i in range(2)]
                    d[f"f2_{q}"] = [add_piece(kp(w_ff2[l][q * 1024 + i * 512:q * 1024 + (i + 1) * 512, :]), 4) for i in range(2)]
                plan.append(d)
        issued = [0]
        done_upto = [-1]

        def pump():
            while issued[0] < len(pieces) and (issued[0] < NS or issued[0] - NS <= done_upto[0]):
                j = issued[0]
                src, a = pieces[j]
                s = slots[j % NS]
                k.dma(pool, s[:].rearrange("p (a b) -> p a b", a=a), src, writes=[s], q=wq[j % NS])
                issued[0] += 1

        def W(j, a):
            assert j < issued[0], "weight piece not issued before use"
            s = slots[j % NS]
            return s, s[:].rearrange("p (a b) -> p a b", a=a)

        def release(upto):
            done_upto[0] = max(done_upto[0], upto)
            pump()

        pump()
        for l in range(4):
            k.dma(pool, Wdt[:, l], w_in[l][:, 4608:4624].rearrange("(kk p) n -> p kk n", p=128), writes=[Wdt], q=pdq)

        def make_xT(ch):
            L, col = ch.L, ch.col
            r = resid[ch.rrow]
            actf(H1[0:L, :], r[0:L, :], AF.Copy, [r], [H1])
            p = pbank()
            pv = v3(p[:], 8)
            for kk in range(8):
                tr(pv[:, kk, 0:L], H1[0:L, kk * 128:(kk + 1) * 128], identb[0:L, 0:L], [H1, identb], [p])
            k.op(dve, lambda e: e.tensor_copy(out=xT[:, :, col:col + L], in_=pv[:, :, 0:L]), reads=[p], writes=[xT_c[ch.rrow]])

        def make_xT_b(ch):
            L, col = ch.L, ch.col
            p = pbank()
            pv = v3(p[:], 8)
            for kk in range(8):
                tr(pv[:, kk, 0:L], xw[0:L, kk * 128:(kk + 1) * 128], identb[0:L, 0:L], [xw, identb], [p])
            k.op(dve, lambda e: e.tensor_copy(out=xT[:, :, col:col + L], in_=pv[:, :, 0:L]), reads=[p], writes=[xT_c[ch.rrow]])

        def layer_norm(ch, grow, brow):
            L = ch.L
            r = resid[ch.rrow]
            for i in range(2):
                k.op(dve, lambda e: e.bn_stats(out=bst[0:L, i, :], in_=r[0:L, i * 512:(i + 1) * 512]), reads=[r], writes=[bst])
            k.op(dve, lambda e: e.bn_aggr(out=mv[0:L, :], in_=bst[0:L, :, :]), reads=[bst], writes=[mv])
            actf(rs[0:L, 0:1], mv[0:L, 1:2], AF.Ln, [mv], [rs], bias=1e-5)
            actf(rs[0:L, 0:1], rs[0:L, 0:1], AF.Exp, [rs], [rs], scale=-0.5)
            k.op(dve, lambda e: e.tensor_scalar(out=r[0:L, :], in0=r[0:L, :], scalar1=mv[0:L, 0:1], scalar2=rs[0:L, 0:1],
                                                op0=ALU.subtract, op1=ALU.mult), reads=[r, mv, rs], writes=[r])
            tt(r[0:L, :], r[0:L, :], grow[0:L, :], ALU.mult, [r, rowc], [r])
            tt(r[0:L, :], r[0:L, :], brow[0:L, :], ALU.add, [r, rowc], [r])

        def load_rows(g_ap, b_ap):
            k.dma(sp, rowc[:, 0, :], g_ap.partition_broadcast(128), writes=[rowc])
            k.dma(sp, rowc[:, 1, :], b_ap.partition_broadcast(128), writes=[rowc])

        def layer_consts_a(l):
            k.dma(sp, v3(F1[:], 8), gws[l].rearrange("h t s -> t h s"), writes=[F1])
            actf(H1[:, :], F1[:, :], AF.Copy, [F1], [H1])
            k.dma(sp, bsb[:].rearrange("p a b -> p (a b)"), gbs[l:l + 1, :].partition_broadcast(128), writes=[bsb])
            for cb in range(12):
                for kk in range(4):
                    k.op(dve, lambda e: e.tensor_scalar(out=diagw[:, cb * 4 + kk, :], in0=identb[:, :], scalar1=convw[:, l, cb, kk:kk + 1],
                                                         scalar2=None, op0=ALU.mult), reads=[identb, convw],
                         writes=[diagw] if (cb, kk) in ((0, 0), (11, 3)) else [])

        def layer_consts_b(l):
            p = pbank(); pv = v3(p[:], 8)
            for h in range(8):
                tr(pv[:, h, :], H1[:, h * 128:(h + 1) * 128], identb[:, :], [H1, identb], [p])
            tt(WsT[:], pv, Um[:].unsqueeze(1).to_broadcast([128, 8, 128]), ALU.mult, [p, Um], [WsT])

        def run_pipeline(gens):
            pending = [tuple(g) + (False,) * (3 - len(g)) for g in gens]; active = []
            while pending or active:
                if pending and (not active or active[-1][1] >= active[-1][2]):
                    g, lag, of = pending.pop(0)
                    active.append([g, 0, lag, of])
                order = [a for a in reversed(active) if not a[3]] + [a for a in active if a[3]]
                for a in order:
                    try:
                        next(a[0]); a[1] += 1
                    except StopIteration:
                        active.remove(a)

        par_ctr = [0]

        def A1_group(t, l, d):
            load_rows(gln_g[l:l + 1, :], gln_b[l:l + 1, :])
            colr = [(0, 512, [0, 1, 2, 3]), (512, 16, [4])] if t == 0 else [(0, 384, [0, 1, 2]), (384, 128, [3])]
            for ub in range(8):
                s_, sv = W(d["u"][ub // 4], 8)
                for (c0, n, cl) in colr:
                    b = bank()
                    for kk in range(8):
                        mm(b[:, 0:n], sv[:, kk, (ub % 4) * 128:(ub % 4 + 1) * 128], xT[:, kk, c0:c0 + n], kk == 0, kk == 7,
                           [s_] + [xT_c[c] for c in cl], [b])
                    actf(hT[:, ub, c0:c0 + n], b[:, 0:n], AF.Gelu_apprx_tanh, [b],
                         [hT] if (ub == 0 and c0 == colr[0][0]) or (ub == 7 and c0 == colr[-1][0]) else [])
            release(d["u"][1])

        def A1_gen(t, l, ch, d, is_last):
            L, col = ch.L, ch.col
            par = par_ctr[0] % 2; par_ctr[0] += 1
            F1 = F1s[par]
            mx = mixT_c[ch.rrow]
            for nb in range(2):
                s_, sv = W(d["v"][nb], 8)
                b = bank()
                for kk in range(8):
                    mm(b[0:L, :], xT[:, kk, col:col + L], sv[:, kk, :], kk == 0, kk == 7, [s_, xT_c[ch.rrow]], [b])
                actf(F1[0:L, nb * 512:(nb + 1) * 512], b[0:L, :], AF.Gelu_apprx_tanh, [b], [F1])
            if is_last:
                release(d["v"][1])
            yield
            for i in range(2):
                k.op(dve, lambda e: e.bn_stats(out=bst[0:L, i, :], in_=F1[0:L, i * 512:(i + 1) * 512]), reads=[F1], writes=[bst])
            k.op(dve, lambda e: e.bn_aggr(out=mv[0:L, :], in_=bst[0:L, :, :]), reads=[bst], writes=[mv])
            actf(rs[0:L, 0:1], mv[0:L, 1:2], AF.Ln, [mv], [rs], bias=1e-5)
            actf(rs[0:L, 0:1], rs[0:L, 0:1], AF.Exp, [rs], [rs], scale=-0.5)
            k.op(dve, lambda e: e.tensor_scalar(out=F1[0:L, :], in0=F1[0:L, :], scalar1=mv[0:L, 0:1], scalar2=rs[0:L, 0:1],
                                                op0=ALU.subtract, op1=ALU.mult), reads=[F1, mv, rs], writes=[F1])
            tt(F1[0:L, :], F1[0:L, :], rowc[0:L, 0, :], ALU.mult, [F1, rowc], [F1])
            if ch.seq == "s":
                tt(F1[0:L, :], F1[0:L, :], rowc[0:L, 1, :], ALU.add, [F1, rowc], [F1])
                k.dma(sp, v_s[l], F1[0:L, :], reads=[F1])
                k.op(dve, lambda e: e.tensor_copy(out=H1[0:L, :], in_=F1[0:L, :]), reads=[F1], writes=[H1])
            else:
                tt(H1[0:L, :], F1[0:L, :], rowc[0:L, 1, :], ALU.add, [F1, rowc], [H1])
            yield
            bs2 = [bank(), bank()]
            for h in range(8):
                b = bs2[h // 4]
                mm(v3(b[:], 4)[:, h % 4, 0:L], H1[0:L, h * 128:(h + 1) * 128], WsT[0:L, h, 0:L], True, True, [H1, WsT], [b])
            for hb in range(2):
                b = bs2[hb]
                tt(tmpS[:, :, 0:L], v3(b[:], 4)[:, :, 0:L], bsb[:, 4 * hb:4 * hb + 4, 0:L], ALU.add, [b, bsb], [tmpS])
                tt(mixT[:, 4 * hb:4 * hb + 4, col:col + L], tmpS[:, :, 0:L], hT[:, 4 * hb:4 * hb + 4, col:col + L], ALU.mult,
                   [tmpS, hT], [mx])
            yield

        def A2F_gen(t, l, ch, d, is_last, par):
            L, col = ch.L, ch.col
            F1 = F1s[par]; expa = expas[par]; xcT = xcTs[par]; dtv = dtvs[par]; da = das[par]; bdec = bdecs[par]
            mx = mixT_c[ch.rrow]
            samp = ch.seq == "s"
            state = stS if samp else stP[l]
            tails = tailsS if samp else tailsP[l]
            has_state = samp or not ch.first
            if samp:
                stf = F1[:]
                k.dma(sp, v3(stf[:, 0:1024], 8), sssm[l].rearrange("(blk p) n -> p blk n", p=128), writes=[F1])
                for half in range(2):
                    b = bank()
                    for j in range(4):
                        blk = half * 4 + j
                        tr(b[:, j * 128:(j + 1) * 128], stf[:, blk * 128:(blk + 1) * 128], identf[:, :], [F1, identf], [b])
                    actf(stS[:, half * 512:(half + 1) * 512], b[:, :], AF.Copy, [b], [stS])
                for kk in range(3):
                    k.dma(sp, tailsS[:, :, kk], sconv[l, kk].rearrange("(cb p) -> p cb", p=128), writes=[tailsS], allow_slow_non_contiguous=True)
            bd = bank()
            for kk in range(8):
                mm(bd[0:L, 0:16], xT[:, kk, col:col + L], Wdt[:, l, kk, :], kk == 0, kk == 7, [xT_c[ch.rrow], Wdt], [bd])
            tt(dtv[0:L, :], bd[0:L, 0:16], dtb[0:L, l * 16:(l + 1) * 16], ALU.add, [bd, dtb], [dtv])
            actf(dtv[0:L, :], dtv[0:L, :], AF.Exp, [dtv], [dtv])
            actf(dtv[0:L, :], dtv[0:L, :], AF.Ln, [dtv], [dtv], bias=1.0)
            tt(da[0:L, :], dtv[0:L, :], aall[0:L, l * 16:(l + 1) * 16], ALU.mult, [dtv, aall], [da])
            mm(bd[0:L, 32:48], Um[0:L, 0:L], da[0:L, :], True, True, [Um, da], [bd])
            mm(bd[:, 64:80], ones[0:L, :], da[0:L, :], True, True, [ones, da], [bd])
            actf(expa[0:L, :], bd[0:L, 32:48], AF.Exp, [bd], [expa])
            actf(bdec[:, :], bd[:, 64:80], AF.Exp, [bd], [bdec])
            yield
            for nb in range(2):
                s_, sv = W(d["z"][nb], 8)
                b = bank()
                for kk in range(8):
                    mm(b[0:L, :], xT[:, kk, col:col + L], sv[:, kk, :], kk == 0, kk == 7, [s_, xT_c[ch.rrow]], [b])
                actf(F1[0:L, nb * 512:(nb + 1) * 512], b[0:L, :], AF.Silu, [b], [F1])
            yield
            xb_banks = [bank(), bank(), bank()]
            for cb in range(12):
                s_, sv = W(d["x"][cb // 4], 8)
                b = xb_banks[cb // 4]
                for kk in range(8):
                    mm(v3(b[:], 4)[:, cb % 4, 0:L], sv[:, kk, (cb % 4) * 128:(cb % 4 + 1) * 128], xT[:, kk, col:col + L],
                       kk == 0, kk == 7, [s_, xT_c[ch.rrow]], [b])
            if is_last:
                release(d["x"][2])
            k.op(dve, lambda e: e.tensor_copy(out=st_t[:, :, 0:3], in_=tails[:, :, :]), reads=[tails], writes=[st_t])
            for j in range(3):
                b = xb_banks[j]
                k.op(dve, lambda e: e.tensor_copy(out=st_t[:, 4 * j:4 * j + 4, 3:3 + L], in_=v3(b[:], 4)[:, :, 0:L]), reads=[b], writes=[st_t])
                k.op(dve, lambda e: e.tensor_copy(out=tails[:, 4 * j:4 * j + 4, :], in_=v3(b[:], 4)[:, :, L - 3:L]), reads=[b, st_t], writes=[tails])
            yield
            cv_banks = [bank(), bank(), bank()]
            for cb in range(12):
                b = cv_banks[cb // 4]
                for kk in range(4):
                    mm(v3(b[:], 4)[:, cb % 4, 0:L], diagw[:, cb * 4 + kk, :], st_t[:, cb, kk:kk + L], kk == 0, kk == 3, [diagw, st_t], [b])
            for cb in range(12):
                b = cv_banks[cb // 4]
                actf(xcT[:, cb, 0:L], v3(b[:], 4)[:, cb % 4, 0:L], AF.Silu, [b, convb], [xcT] if cb in (0, 11) else [],
                     bias=convb[:, l, cb:cb + 1])
            if ch.last:
                for j in range(3):
                    b = bank()
                    for i in range(4):
                        tr(b[0:3, i * 128:(i + 1) * 128], tails[:, 4 * j + i, :], identf[:, :], [tails, identf], [b])
                    rt = relu_tmp[j]
                    rtv = rt[:].rearrange("p a b -> p (a b)")
                    actf(rtv[0:3, :], b[0:3, :], AF.Copy, [b], [rt])
                    k.dma(sp, (conv_s if samp else conv_p)[l][:, j * 512:(j + 1) * 512], rtv[0:3, :], reads=[rt])
            yield

        def A2B_gen(t, l, ch, d, is_last, par):
            L, col = ch.L, ch.col
            F1 = F1s[par]; expa = expas[par]; xcT = xcTs[par]; dtv = dtvs[par]; da = das[par]; bdec = bdecs[par]
            mx = mixT_c[ch.rrow]
            samp = ch.seq == "s"
            state = stS if samp else stP[l]
            tails = tailsS if samp else tailsP[l]
            has_state = samp or not ch.first
            stateb = statebs[par]
            if samp or (has_state and ch.rrow == 0):
                actf(stateb[:, :], state[:, :], AF.Copy, [state], [stateb])
            if has_state:
                tt(v3(state[:, :], 16), v3(state[:, :], 16), bdec[:, :].unsqueeze(2).to_broadcast([128, 16, 64]), ALU.mult,
                   [state, bdec], [state], eng=pool)
            pA = pbank(); pAv = v3(pA[:], 8)
            for hb in range(8):
                tr(pAv[0:L, hb, :], xcT[:, hb, 0:L], identb[:, :], [xcT, identb], [pA])
            actf(xt[0:L, :], pA[0:L, :], AF.Copy, [pA], [xt])
            tt(v3(xdt[0:L, :], 16), v3(xt[0:L, :], 16), dtv[0:L, :].unsqueeze(2).to_broadcast([L, 16, 64]), ALU.mult, [xt, dtv], [xdt])
            tt(v3(F3[0:L, :], 16), v3(xt[0:L, :], 16), Dall[0:L, l * 16:(l + 1) * 16].unsqueeze(2).to_broadcast([L, 16, 64]),
               ALU.mult, [xt, Dall], [F3], eng=pool)
            pB = pbank()
            for g in range(2):
                tr(pB[0:L, g * 128:(g + 1) * 128], xcT[:, 8 + g, 0:L], identb[:, :], [xcT, identb], [pB])
            actf(Bt[0:L, :], pB[0:L, 0:256], AF.Copy, [pB], [Bt])
            bc = bank()
            for g in range(2):
                mm(v3(bc[:], 4)[0:L, g, 0:L], xcT[:, 8 + g, 0:L], xcT[:, 10 + g, 0:L], True, True, [xcT], [bc])
            tt(cbm[0:L, :, 0:L], v3(bc[:], 4)[0:L, 0:2, 0:L], Um[0:L, 0:L].unsqueeze(1).to_broadcast([L, 2, L]), ALU.mult, [bc, Um], [cbm])
            yield
            for hq in range(4):
                ru = rhsU[hq % 2]
                tt(ru[0:L, :, 0:L], Um[0:L, 0:L].unsqueeze(1).to_broadcast([L, 4, L]),
                   da[0:L, hq * 4:hq * 4 + 4].unsqueeze(2).to_broadcast([L, 4, L]), ALU.mult, [Um, da], [ru])
                b = bank()
                if L == 128:
                    mm(v3(b[:], 4)[0:L, :, 0:L], Ls[0:L, 0:L], ru[0:L, :, 0:L], True, True, [Ls, ru], [b])
                else:
                    for hh in range(4):
                        mm(v3(b[:], 4)[0:L, hh, 0:L], Ls[0:L, 0:L], ru[0:L, hh, 0:L], True, True, [Ls, ru], [b])
                actf(dec[0:L, hq * 4:hq * 4 + 4, 0:L], v3(b[:], 4)[0:L, :, 0:L], AF.Exp, [b], [dec] if hq in (0, 3) else [])
            tt(wcol[0:L, :].unsqueeze(2), dtv[0:L, :].unsqueeze(2), dec[0:L, :, L - 1:L], ALU.mult, [dtv, dec], [wcol])
            tt(v3(xw[0:L, :], 16), v3(xt[0:L, :], 16), wcol[0:L, :].unsqueeze(2).to_broadcast([L, 16, 64]), ALU.mult, [xt, wcol], [xw])
            yield
            for g in range(2):
                tt(dec[0:L, g * 8:(g + 1) * 8, 0:L], dec[0:L, g * 8:(g + 1) * 8, 0:L],
                   cbm[0:L, g:g + 1, 0:L].to_broadcast([L, 8, L]), ALU.mult, [dec, cbm], [dec])
            bS = [bank(), bank()]
            for g in range(2):
                mm(bS[g][:, :], Bt[0:L, g * 128:(g + 1) * 128], xw[0:L, g * 512:(g + 1) * 512], True, True, [Bt, xw], [bS[g]])
            if has_state:
                for g in range(2):
                    tt(state[:, g * 512:(g + 1) * 512], state[:, g * 512:(g + 1) * 512], bS[g][:, :], ALU.add, [state, bS[g]], [state])
            else:
                for g in range(2):
                    actf(state[:, g * 512:(g + 1) * 512], bS[g][:, :], AF.Copy, [bS[g]], [state])
            if not samp and ch.rrow < 3:
                actf(statebs[1 - par][:, :], state[:, :], AF.Copy, [state], [statebs[1 - par]])
            yield
            bY = [bank(), bank()]
            for h in range(16):
                b = bY[h // 8]
                mm(b[0:L, (h % 8) * 64:(h % 8 + 1) * 64], dec[0:L, h, 0:L], xdt[0:L, h * 64:(h + 1) * 64], True, True, [dec, xdt], [b])
            if has_state:
                bO = [bank(), bank()]
                for g in range(2):
                    mm(bO[g][0:L, :], xcT[:, 10 + g, 0:L], stateb[:, g * 512:(g + 1) * 512], True, True, [xcT, stateb], [bO[g]])
                for g in range(2):
                    tt(v3(F2[0:L, g * 512:(g + 1) * 512], 8), v3(bO[g][0:L, :], 8),
                       expa[0:L, g * 8:(g + 1) * 8].unsqueeze(2).to_broadcast([L, 8, 64]), ALU.mult, [bO[g], expa], [F2])
                    tt(F2[0:L, g * 512:(g + 1) * 512], F2[0:L, g * 512:(g + 1) * 512], bY[g][0:L, :], ALU.add, [F2, bY[g]], [F2])
            else:
                for g in range(2):
                    actf(F2[0:L, g * 512:(g + 1) * 512], bY[g][0:L, :], AF.Copy, [bY[g]], [F2])
            tt(F2[0:L, :], F2[0:L, :], F3[0:L, :], ALU.add, [F2, F3], [F2])
            tt(F2[0:L, :], F2[0:L, :], F1[0:L, :], ALU.mult, [F2, F1], [F2])
            yield
            k.op(dve, lambda e: e.memset(ssq[:], 0.0), writes=[ssq])
            actf(rs[0:1, 0:1], ones[0:1, 0:1], AF.Exp, [ones], [rs])
            for g in range(2):
                actf(H1[0:L, g * 512:(g + 1) * 512], F2[0:L, g * 512:(g + 1) * 512], AF.Square, [F2, ssq], [H1, ssq],
                     accum_out=ssq[0:L, g:g + 1])
            actf(rs[0:L, :], ssq[0:L, :], AF.Ln, [ssq], [rs], scale=1.0 / 512.0, bias=1e-5)
            actf(rs[0:L, :], rs[0:L, :], AF.Exp, [rs], [rs], scale=-0.5)
            for g in range(2):
                k.op(dve, lambda e: e.tensor_scalar(out=H1[0:L, g * 512:(g + 1) * 512], in0=F2[0:L, g * 512:(g + 1) * 512],
                                                    scalar1=rs[0:L, g:g + 1], scalar2=None, op0=ALU.mult), reads=[F2, rs], writes=[H1])
            pC = pbank(); pCv = v3(pC[:], 8)
            for eb in range(8):
                tr(pCv[:, eb, 0:L], H1[0:L, eb * 128:(eb + 1) * 128], identb[0:L, 0:L], [H1, identb], [pC])
            k.op(dve, lambda e: e.tensor_copy(out=mixT[:, 8:16, col:col + L], in_=pCv[:, :, 0:L]), reads=[pC], writes=[mx])
            if ch.last:
                for half in range(2):
                    b = bank()
                    for j in range(4):
                        blk = half * 4 + j
                        tr(b[:, j * 128:(j + 1) * 128], state[:, blk * 128:(blk + 1) * 128], identf[:, :], [state, identf], [b])
                    actf(F3[:, half * 512:(half + 1) * 512], b[:, :], AF.Copy, [b], [F3])
                k.dma(sp, (ssm_s if samp else ssm_p)[l].rearrange("(blk p) n -> p blk n", p=128), v3(F3[:], 8), reads=[F3])
            yield

        def A3_gen(t, l, ch, d, is_first, is_last):
            L, col = ch.L, ch.col
            r = resid[ch.rrow]
            mx = mixT_c[ch.rrow]
            if is_first:
                load_rows(ln1_g[l:l + 1, :], ln1_b[l:l + 1, :])
                for pi in (2, 3):
                    s_, sv = W(d["o"][pi], 4)
                    for j in range(4):
                        eb = (pi - 2) * 4 + j
                        k.op(dve, lambda e: e.tensor_scalar(out=sv[:, j, :], in0=sv[:, j, :], scalar1=gn[:, l, eb:eb + 1], scalar2=None,
                                                            op0=ALU.mult), reads=[s_, gn], writes=[s_])
            for nb in range(2):
                b = bank()
                for kk in range(16):
                    s_, sv = W(d["o"][kk // 4], 4)
                    mm(b[0:L, :], mixT[:, kk, col:col + L], sv[:, kk % 4, nb * 512:(nb + 1) * 512], kk == 0, kk == 15, [s_, mx], [b])
                k.op(dve, lambda e: e.scalar_tensor_tensor(out=r[0:L, nb * 512:(nb + 1) * 512], in0=r[0:L, nb * 512:(nb + 1) * 512],
                                                           scalar=float(ALPHA), in1=b[0:L, :], op0=ALU.mult, op1=ALU.add),
                     reads=[r, b], writes=[r])
            if is_last:
                release(d["o"][3])
            yield
            layer_norm(ch, rowc[:, 0, :], rowc[:, 1, :])
            actf(xw[0:L, :], r[0:L, :], AF.Copy, [r], [xw])
            yield
            make_xT_b(ch)
            yield

        def B_group(t, l, q, d):
            if q == 3:
                load_rows(ln2_g[l:l + 1, :], ln2_b[l:l + 1, :])
            if t == 0:
                colr = [(0, 512, [0, 1, 2, 3]), (512, 16, [4])]
            elif q == 0:
                colr = [(0, 384, [0, 1, 2]), (384, 128, [3])]
            else:
                colr = [(0, 512, [0, 1, 2, 3])]
            for fb in range(8):
                s_, sv = W(d[f"f1_{q}"][fb // 4], 8)
                for (c0, n, cl) in colr:
                    b = bank()
                    for kk in range(8):
                        mm(b[:, 0:n], sv[:, kk, (fb % 4) * 128:(fb % 4 + 1) * 128], xT[:, kk, c0:c0 + n], kk == 0, kk == 7,
                           [s_] + [xT_c[c] for c in cl], [b])
                    rt = relu_tmp[relu_i[0] % 3]; relu_i[0] += 1
                    rtv = rt[:].rearrange("p a b -> p (a b)")
                    actf(rtv[:, 0:n], b[:, 0:n], AF.Relu, [b], [rt])
                    tt(hT[:, fb, c0:c0 + n], rtv[:, 0:n], rtv[:, 0:n], ALU.mult, [rt],
                       [hT] if (fb == 0 and c0 == colr[0][0]) or (fb == 7 and c0 == colr[-1][0]) else [])
            release(d[f"f1_{q}"][1])

        def B_gen(t, l, q, ch, d, is_last):
            L, col = ch.L, ch.col
            r = resid[ch.rrow]
            for nb in range(2):
                b = bank()
                for fc in range(8):
                    s_, sv = W(d[f"f2_{q}"][fc // 4], 4)
                    mm(b[0:L, :], hT[:, fc, col:col + L], sv[:, fc % 4, nb * 512:(nb + 1) * 512], fc == 0, fc == 7, [s_, hT], [b])
                rr = r[0:L, nb * 512:(nb + 1) * 512]
                if q == 0:
                    k.op(dve, lambda e: e.scalar_tensor_tensor(out=rr, in0=rr, scalar=float(ALPHA), in1=b[0:L, :],
                                                               op0=ALU.mult, op1=ALU.add), reads=[r, b], writes=[r])
                else:
                    tt(rr, rr, b[0:L, :], ALU.add, [r, b], [r])
            if is_last:
                release(d[f"f2_{q}"][1])
            yield
            if q == 3:
                layer_norm(ch, rowc[:, 0, :], rowc[:, 1, :])
                if l != depth - 1:
                    actf(xw[0:L, :], r[0:L, :], AF.Copy, [r], [xw])
                yield
                if l == depth - 1:
                    if ch.seq == "s":
                        k.dma(sp, ys[:, :], r[0:L, :], reads=[r])
                    else:
                        k.dma(sp, yp[ch.tok0:ch.tok0 + L, :], r[0:L, :], reads=[r])
                else:
                    make_xT_b(ch)
                yield

        def chk(tag):
            if stop == tag:
                raise _Stop()
        try:
          for t in range(ntiles):
              chunks = [Chunk(128, c * 128, c, "p", t == 0 and c == 0, t == ntiles - 1 and c == 3, t * 512 + c * 128) for c in range(4)]
              if t == 0:
                  chunks.append(Chunk(16, 512, 4, "s", True, True, 0))
              for ch in chunks:
                  if t > 0:
                      k.dma(sp, resid[ch.rrow][0:ch.L, :], xp[ch.tok0:ch.tok0 + ch.L, :], writes=[resid[ch.rrow]])
                  make_xT(ch)
              n = len(chunks)
              if t == 0:
                  actf(aall[:], aall[:], AF.Exp, [aall], [aall])
                  k.op(dve, lambda e: e.tensor_scalar(out=aall[:], in0=aall[:], scalar1=-1.0, scalar2=None, op0=ALU.mult),
                       reads=[aall], writes=[aall])
              for l in range(depth):
                  d = plan[t * depth + l]
                  layer_consts_a(l)
                  A1_group(t, l, d)
                  layer_consts_b(l)
                  gens = []
                  gens += [(A1_gen(t, l, ch, d, i == n - 1), 1, True) for i, ch in enumerate(chunks)]
                  for i, ch in enumerate(chunks):
                      gens.append((A2F_gen(t, l, ch, d, i == n - 1, i % 2), 4))
                      gens.append((A2B_gen(t, l, ch, d, i == n - 1, i % 2), 0))
                  gens += [(A3_gen(t, l, ch, d, i == 0, i == n - 1), 1, True) for i, ch in enumerate(chunks)]
                  run_pipeline(gens)
                  chk("A3")
                  for q in range(4):
                      B_group(t, l, q, d)
                      run_pipeline([(B_gen(t, l, q, ch, d, i == n - 1), 1, True) for i, ch in enumerate(chunks)])
        except _Stop:
            pass
        k.finish(extra=wq + [pdq])
    return nc


_NC_CACHE = {}


def kernel(x_prompt, x_sample, state_ssm, state_conv, w_in, gmlp_ln_g, gmlp_ln_b, gmlp_ws, gmlp_bs,
           conv_w, conv_b, dt_bias, a_log, d_skip, ssd_norm_g, w_out, ln1_g, ln1_b, w_ff1, w_ff2,
           ln2_g, ln2_b):
    f = lambda a: np.ascontiguousarray(np.asarray(a, dtype=np.float32))
    if "nc" not in _NC_CACHE:
        _NC_CACHE["nc"] = build_program()
    nc = _NC_CACHE["nc"]
    shared = {
        "w_in": f(w_in), "gmlp_ln_g": f(gmlp_ln_g), "gmlp_ln_b": f(gmlp_ln_b), "gmlp_ws": f(gmlp_ws),
        "gmlp_bs": f(gmlp_bs).reshape(4, 1024), "conv_w": f(conv_w), "conv_b": f(conv_b),
        "dt_bias": f(dt_bias).reshape(1, 64), "a_log": f(a_log).reshape(1, 64), "d_skip": f(d_skip).reshape(1, 64),
        "ssd_norm_g": f(ssd_norm_g), "w_out": f(w_out), "ln1_g": f(ln1_g), "ln1_b": f(ln1_b),
        "w_ff1": f(w_ff1), "w_ff2": f(w_ff2), "ln2_g": f(ln2_g), "ln2_b": f(ln2_b),
    }
    xp = f(x_prompt); xs = f(x_sample); ss = f(state_ssm); sc = f(state_conv)
    in_maps = []
    for b in range(8):
        m = dict(shared)
        m["xp"] = xp[b]; m["xs"] = xs[b]
        m["sssm"] = np.ascontiguousarray(ss[:, b].reshape(4, 1024, 128))
        m["sconv"] = np.ascontiguousarray(sc[:, b])
        in_maps.append(m)
    res = run_bass_kernel_spmd(nc, in_maps, core_ids=list(range(8)))
    R = res.results
    y_prompt = np.stack([R[b]["yp"] for b in range(8)], axis=0)
    y_sample = np.stack([R[b]["ys"] for b in range(8)], axis=0)
    ssm_p = np.stack([R[b]["ssm_p"].reshape(4, 16, 64, 128) for b in range(8)], axis=1)
    conv_p = np.stack([R[b]["conv_p"] for b in range(8)], axis=1)
    ssm_s = np.stack([R[b]["ssm_s"].reshape(4, 16, 64, 128) for b in range(8)], axis=1)
    conv_s = np.stack([R[b]["conv_s"] for b in range(8)], axis=1)
    v_s = np.stack([R[b]["v_s"] for b in range(8)], axis=1)
    return (y_prompt, y_sample, ssm_p, conv_p, ssm_s, conv_s, v_s)
```
